# Optimizing a Trainium2 kernel written in Bass

```python
import jax, jax.numpy as jnp
from jax import lax
import numpy as np

D_MODEL = 1024
BATCH = 16
SEQ = 2048
DEPTH = 1

CHUNK = 64
Q_BLOCK = 128
EPS = 1e-6
MLA_HEADS = 8
QK_NOPE = 64
QK_ROPE = 32
QK_HEAD = QK_NOPE + QK_ROPE
V_HEAD = 64
Q_LORA = 256
KV_LORA = 128
ROPE_THETA = 10000.0
LRU_WIDTH = 1024
LRU_BLOCKS = 8
LRU_BW = LRU_WIDTH // LRU_BLOCKS
CONV_WIDTH = 4
LRU_C = 8.0
PEER_HEADS = 8
N_KEYS = 128
N_EXPERTS = N_KEYS * N_KEYS
PEER_DKEY = 256
PEER_HALF = PEER_DKEY // 2
PEER_TOPK = 16
TOKEN_BLOCK = 128
IN_COLS = Q_LORA + KV_LORA + QK_ROPE + 2 * LRU_WIDTH + 2 * D_MODEL
IN_SPLITS = [Q_LORA, Q_LORA + KV_LORA, Q_LORA + KV_LORA + QK_ROPE,
             Q_LORA + KV_LORA + QK_ROPE + LRU_WIDTH,
             Q_LORA + KV_LORA + QK_ROPE + 2 * LRU_WIDTH]

kernel_name = "hybrid_mla_rglru_peer_chunk_causal"


def rms_norm(x, g):
    xf = x.astype(jnp.float32)
    y = xf * lax.rsqrt(jnp.mean(xf * xf, axis=-1, keepdims=True) + EPS)
    return (y * g.astype(jnp.float32)).astype(x.dtype)


def apply_rope(t, cos, sin):
    t1, t2 = jnp.split(t, 2, axis=-1)
    return jnp.concatenate([t1 * cos - t2 * sin, t1 * sin + t2 * cos], axis=-1)


def mla_attention(c_q, c_kv, k_rope, positions, q_a_norm_g, w_uq, kv_a_norm_g, w_ukv,
                  q_norm_g, k_norm_g):
    B, S, _ = c_q.shape
    H = MLA_HEADS
    q = (rms_norm(c_q, q_a_norm_g) @ w_uq).reshape(B, S, H, QK_HEAD)
    kv = (rms_norm(c_kv, kv_a_norm_g) @ w_ukv).reshape(B, S, H, QK_NOPE + V_HEAD)
    k_nope, v = kv[..., :QK_NOPE], kv[..., QK_NOPE:]
    k = jnp.concatenate([k_nope, jnp.broadcast_to(k_rope[:, :, None, :], (B, S, H, QK_ROPE))], axis=-1)
    q = rms_norm(q, q_norm_g)
    k = rms_norm(k, k_norm_g)
    inv_freq = 1.0 / (ROPE_THETA ** (jnp.arange(0, QK_ROPE, 2, dtype=jnp.float32) / QK_ROPE))
    ang = positions.astype(jnp.float32)[..., None] * inv_freq
    cos = jnp.cos(ang)[:, :, None, :].astype(q.dtype)
    sin = jnp.sin(ang)[:, :, None, :].astype(q.dtype)
    q = jnp.concatenate([q[..., :QK_NOPE], apply_rope(q[..., QK_NOPE:], cos, sin)], axis=-1)
    k = jnp.concatenate([k[..., :QK_NOPE], apply_rope(k[..., QK_NOPE:], cos, sin)], axis=-1)
    nb = S // Q_BLOCK
    qb = q.reshape(B, nb, Q_BLOCK, H, QK_HEAD).transpose(1, 0, 2, 3, 4)
    key_chunk = jnp.arange(S) // CHUNK
    scale = QK_HEAD ** -0.5
    neg = jnp.finfo(jnp.float32).min

    def block(args):
        qi, bi = args
        s = jnp.einsum('bqhd,bkhd->bhqk', qi, k, preferred_element_type=jnp.float32) * scale
        q_chunk = (bi * Q_BLOCK + jnp.arange(Q_BLOCK)) // CHUNK
        mask = key_chunk[None, :] <= q_chunk[:, None]
        p = jax.nn.softmax(jnp.where(mask, s, neg), axis=-1).astype(v.dtype)
        return jnp.einsum('bhqk,bkhd->bqhd', p, v)

    o = lax.map(block, (qb, jnp.arange(nb)))
    return o.transpose(1, 0, 2, 3, 4).reshape(B, S, H * V_HEAD)


def rg_lru_branch(xb, gb, conv_w, conv_b, w_rg, b_rg, w_ig, b_ig, lru_lambda):
    B, S, C = xb.shape
    xc = lax.conv_general_dilated(xb, conv_w[:, None, :], window_strides=(1,),
                                  padding=[(CONV_WIDTH - 1, 0)],
                                  dimension_numbers=('NWC', 'WIO', 'NWC'),
                                  feature_group_count=C) + conv_b
    xblk = xc.reshape(B, S, LRU_BLOCKS, LRU_BW)
    r = jax.nn.sigmoid(jnp.einsum('bsni,nio->bsno', xblk, w_rg).reshape(B, S, C) + b_rg)
    i = jax.nn.sigmoid(jnp.einsum('bsni,nio->bsno', xblk, w_ig).reshape(B, S, C) + b_ig)
    log_a = (-LRU_C * r.astype(jnp.float32)) * jax.nn.softplus(-lru_lambda.astype(jnp.float32))
    a = jnp.exp(log_a)
    b = jnp.sqrt(-jnp.expm1(2.0 * log_a)) * (i * xc).astype(jnp.float32)

    def combine(e1, e2):
        a1, b1 = e1
        a2, b2 = e2
        return a1 * a2, a2 * b1 + b2

    _, h = lax.associative_scan(combine, (a, b), axis=1)
    return h.astype(xb.dtype) * jax.nn.gelu(gb)


def peer_ffn(h, w_query, sub_keys, expert_u, expert_v):
    B, S, D = h.shape
    hb = h.reshape((B * S) // TOKEN_BLOCK, TOKEN_BLOCK, D)
    K = PEER_TOPK

    def block(xt):
        q = (xt @ w_query).reshape(TOKEN_BLOCK, PEER_HEADS, 2, PEER_HALF)
        s = jnp.einsum('thpd,hpnd->thpn', q, sub_keys, preferred_element_type=jnp.float32)
        s_top, i_top = lax.top_k(s, K)
        cand = (s_top[:, :, 0, :, None] + s_top[:, :, 1, None, :]).reshape(TOKEN_BLOCK, PEER_HEADS, K * K)
        cand_idx = (i_top[:, :, 0, :, None] * N_KEYS + i_top[:, :, 1, None, :]).reshape(TOKEN_BLOCK, PEER_HEADS, K * K)
        best, pos = lax.top_k(cand, K)
        idx = jnp.take_along_axis(cand_idx, pos, axis=-1)
        g = jax.nn.softmax(best, axis=-1)
        u = expert_u[idx]
        v = expert_v[idx]
        act = jax.nn.gelu(jnp.einsum('td,thkd->thk', xt, u))
        return jnp.einsum('thk,thkd->td', (g * act).astype(xt.dtype), v)

    return lax.map(block, hb).reshape(B, S, D)


def _normal(key, shape, scale):
    return jax.random.normal(key, shape, jnp.float32) * scale


def setup_inputs(seed: int = 0) -> dict:
    key = jax.random.key(seed)
    ks = jax.random.split(key, 32)
    L, D, H = DEPTH, D_MODEL, MLA_HEADS
    x = _normal(ks[0], (BATCH, SEQ, D), 1.0)
    c = _normal(ks[1], (BATCH, D), 1.0)
    positions = (jax.random.randint(ks[2], (BATCH, 1), 0, 4096, dtype=jnp.int32)
                 + jnp.arange(SEQ, dtype=jnp.int32)[None, :]).astype(jnp.int32)
    u = jax.random.uniform(ks[21], (L, LRU_WIDTH), jnp.float32, 0.9, 0.999)
    a_base = u ** (1.0 / LRU_C)
    lru_lambda = jnp.log(a_base) - jnp.log1p(-a_base)
    return {
        'x': x,
        'c': c,
        'positions': positions,
        'w_ada': _normal(ks[3], (L, D, 6 * D), 0.5 * D ** -0.5),
        'b_ada': _normal(ks[4], (L, 6 * D), 0.02),
        'norm1_g': 1.0 + _normal(ks[5], (L, D), 0.02),
        'w_in': _normal(ks[6], (L, D, IN_COLS), D ** -0.5),
        'q_a_norm_g': 1.0 + _normal(ks[7], (L, Q_LORA), 0.02),
        'w_uq': _normal(ks[8], (L, Q_LORA, H * QK_HEAD), Q_LORA ** -0.5),
        'kv_a_norm_g': 1.0 + _normal(ks[9], (L, KV_LORA), 0.02),
        'w_ukv': _normal(ks[10], (L, KV_LORA, H * (QK_NOPE + V_HEAD)), KV_LORA ** -0.5),
        'q_norm_g': 1.0 + _normal(ks[11], (L, QK_HEAD), 0.02),
        'k_norm_g': 1.0 + _normal(ks[12], (L, QK_HEAD), 0.02),
        'w_o_attn': _normal(ks[13], (L, H * V_HEAD, D), (H * V_HEAD) ** -0.5),
        'conv_w': _normal(ks[14], (L, CONV_WIDTH, LRU_WIDTH), CONV_WIDTH ** -0.5),
        'conv_b': _normal(ks[15], (L, LRU_WIDTH), 0.02),
        'w_rg': _normal(ks[16], (L, LRU_BLOCKS, LRU_BW, LRU_BW), LRU_BW ** -0.5),
        'b_rg': _normal(ks[17], (L, LRU_WIDTH), 0.02),
        'w_ig': _normal(ks[18], (L, LRU_BLOCKS, LRU_BW, LRU_BW), LRU_BW ** -0.5),
        'b_ig': _normal(ks[19], (L, LRU_WIDTH), 0.02),
        'lru_lambda': lru_lambda,
        'w_o_lru': _normal(ks[22], (L, LRU_WIDTH, D), LRU_WIDTH ** -0.5),
        'w_out': _normal(ks[23], (L, D, D), D ** -0.5),
        'norm2_g': 1.0 + _normal(ks[24], (L, D), 0.02),
        'w_query': _normal(ks[25], (L, D, PEER_HEADS * PEER_DKEY), D ** -0.5),
        'sub_keys': _normal(ks[26], (L, PEER_HEADS, 2, N_KEYS, PEER_HALF), PEER_HALF ** -0.5),
        'expert_u': _normal(ks[27], (L, N_EXPERTS, D), D ** -0.5),
        'expert_v': _normal(ks[28], (L, N_EXPERTS, D), PEER_HEADS ** -0.5),
    }


def reference(x, c, positions, w_ada, b_ada, norm1_g, w_in, q_a_norm_g, w_uq, kv_a_norm_g,
              w_ukv, q_norm_g, k_norm_g, w_o_attn, conv_w, conv_b, w_rg, b_rg, w_ig, b_ig,
              lru_lambda, w_o_lru, w_out, norm2_g, w_query, sub_keys, expert_u, expert_v):
    cs = jax.nn.silu(c)
    for l in range(DEPTH):
        mod = cs @ w_ada[l] + b_ada[l]
        shift1, scale1, gate1, shift2, scale2, gate2 = [m[:, None, :] for m in jnp.split(mod, 6, axis=-1)]
        h = rms_norm(x, norm1_g[l]) * (1 + scale1) + shift1
        proj = h @ w_in[l]
        c_q, c_kv, k_rope, lru_x, lru_gate, merge_logits = jnp.split(proj, IN_SPLITS, axis=-1)
        attn = mla_attention(c_q, c_kv, k_rope, positions, q_a_norm_g[l], w_uq[l],
                             kv_a_norm_g[l], w_ukv[l], q_norm_g[l], k_norm_g[l])
        rec = rg_lru_branch(lru_x, lru_gate, conv_w[l], conv_b[l], w_rg[l], b_rg[l],
                            w_ig[l], b_ig[l], lru_lambda[l])
        g_attn, g_rec = jnp.split(jax.nn.sigmoid(merge_logits), 2, axis=-1)
        y = g_attn * (attn @ w_o_attn[l]) + g_rec * (rec @ w_o_lru[l])
        x = x + gate1 * (y @ w_out[l])
        h2 = rms_norm(x, norm2_g[l]) * (1 + scale2) + shift2
        x = x + gate2 * peer_ffn(h2, w_query[l], sub_keys[l], expert_u[l], expert_v[l])
    return x
```

```python
import math
from contextlib import ExitStack

import numpy as np
import concourse.bass as bass
import concourse.mybir as mybir
from concourse.bass_utils import run_bass_kernel_spmd

F32 = mybir.dt.float32
F32R = mybir.dt.float32r
BF16 = mybir.dt.bfloat16
I32 = mybir.dt.int32
U32 = mybir.dt.uint32
AF = mybir.ActivationFunctionType
ALU = mybir.AluOpType

NCORES = 8
NSEQ = 2
S = 2048
D = 1024
G = 512
NG = S // G
EPS = 1e-6
NEXP = 16384
PI = math.pi

DEBUG = {}


class Buf:
    def __init__(self, name):
        self.name = name
        self.w = {}
        self.r = {}
        self.dsem = None
        self.dcnt = 0
        self.psum = False


class Tile:
    def __init__(self, t, name):
        self.t = t
        self.b = Buf(name)

    def __getitem__(self, k):
        return self.t[k]


class Sched:
    def __init__(self, nc, es):
        self.nc = nc
        self.es = es
        self.eng = dict(pe=nc.tensor, act=nc.scalar, dve=nc.vector, pool=nc.gpsimd, sp=nc.sync)
        self.sem = {k: es.enter_context(nc.semaphore("S_" + k)) for k in self.eng}
        self.cnt = {k: 0 for k in self.eng}
        self.seen = {k: {} for k in self.eng}
        self.dbufs = []
        self.free = []
        self.free_sw = []
        self.nd = 0

    def _wait(self, e, tok):
        key, sem, val, src = tok
        if self.seen[e].get(key, 0) >= val:
            return
        self.eng[e].wait_ge(sem, val)
        self.seen[e][key] = val

    def _deps(self, e, r, w, is_dma, dkey=None):
        for b in r:
            for tok in b.w.values():
                if (not is_dma) and tok[3] == e and e == 'pe':
                    continue
                self._wait(e, tok)
        for b in w:
            for tok in b.w.values():
                if (not is_dma) and tok[3] == e and e == 'pe':
                    continue
                if is_dma and tok[0] == dkey:
                    continue
                self._wait(e, tok)
            for tok in b.r.values():
                if (not is_dma) and tok[3] == e and e == 'pe':
                    continue
                self._wait(e, tok)

    def _post(self, tok, r, w):
        for b in w:
            if tok[3] == 'dma':
                b.w = {tok[0]: tok}
            else:
                b.w = {tok[0]: tok}
            b.r = {}
        for b in r:
            if b not in w:
                b.r[tok[0]] = tok

    def op(self, e, fn, r=(), w=()):
        pr_ = [b for b in r if b.psum and b not in w]
        if pr_:
            w = list(w) + pr_
            r = [b for b in r if not b.psum]
        self._deps(e, r, w, False)
        inst = fn()
        self.cnt[e] += 1
        inst.then_inc(self.sem[e], 1)
        tok = ("E_" + e, self.sem[e], self.cnt[e], e)
        self.seen[e][tok[0]] = max(self.seen[e].get(tok[0], 0), 0)
        self._post(tok, r, w)
        return inst

    def dma(self, e, fn, sb, r=(), w=()):
        if sb.dsem is None:
            fl = self.free_sw if e == 'pool' else self.free
            if fl:
                sb.dsem = fl.pop()
            else:
                sem = self.es.enter_context(self.nc.semaphore("D%d" % self.nd))
                sb.dsem = ["D%d" % self.nd, sem, 0, e == 'pool']
                self.nd += 1
                self.dbufs.append(sb.dsem)
        ds = sb.dsem
        self._deps(e, r, w, True, ds[0])
        inst = fn()
        ds[2] += 16
        inst.then_inc(ds[1], 16)
        tok = (ds[0], ds[1], ds[2], 'dma')
        self._post(tok, r, w)
        return inst

    def release(self, buf):
        if buf.dsem is not None:
            (self.free_sw if buf.dsem[3] else self.free).append(buf.dsem)
            buf.dsem = None

    def barrier(self):
        for e in self.eng:
            for e2 in self.eng:
                if e2 != e and self.cnt[e2] > 0:
                    self._wait(e, ("E_" + e2, self.sem[e2], self.cnt[e2], e2))
            for ds in self.dbufs:
                if ds[2] > 0:
                    self._wait(e, (ds[0], ds[1], ds[2], 'dma'))


class _Stop(Exception):
    pass


def build_program(dbg_stop=None):
    nc = bass.Bass("TRN2", target_bir_lowering=False)

    def din(name, shape, dt=F32):
        return nc.dram_tensor(name, list(shape), dt, kind="ExternalInput").ap()

    xT_d = din("xT", [NSEQ, 128, 8, S])
    cT_d = din("cT", [128, 8, NSEQ])
    pos_d = din("pos", [NSEQ, S], I32)
    wada_d = din("w_ada", [128, 8, 6144])
    bada_d = din("b_adaT", [128, 48])
    pvec_d = din("pvec", [128, 96])
    wlat_d = din("w_lat", [128, 8, 512])
    wuq_d = din("w_uq2", [128, 2, 1536])
    wkn_d = din("w_kn", [128, 8 * 96])
    wv_d = din("w_v", [128, 512])
    emat_d = din("emat", [128, 96])
    wlx_d = din("w_lx", [128, 8, 1024])
    wlg_d = din("w_lg", [128, 8, 1024])
    wga_d = din("w_ga", [128, 8, 1024])
    wgb_d = din("w_gb", [128, 8, 1024])
    wrg_d = din("w_rg", [128, 8, 128])
    wig_d = din("w_ig", [128, 8, 128])
    woa_d = din("w_oa", [64, 8, 1024])
    wol_d = din("w_ol", [128, 8, 1024])
    wout_d = din("w_out", [128, 8, 1024])
    wq_d = din("w_q", [128, 8, 2048])
    skT_d = din("skT", [128, 16, 128])
    uv_d = din("uv", [NEXP, 2048])
    out_d = nc.dram_tensor("outT", [NSEQ, 128, 8, S], F32, kind="ExternalOutput").ap()
    hT_d = nc.dram_tensor("hT_s", [NSEQ, 128, 8, S], BF16, kind="Internal").ap()
    ya_d = nc.dram_tensor("ya_s", [NSEQ, 128, 8, S], BF16, kind="Internal").ap()
    uvb_d = nc.dram_tensor("uvb_s", [NEXP, 2048], BF16, kind="Internal").ap()
    dbg_d = {k: nc.dram_tensor(k, list(v[0]), v[1], kind="ExternalOutput").ap() for k, v in DEBUG.items()}

    es = ExitStack()
    try:
      with es:
        sc = Sched(nc, es)

        uid = {'i': 0}

        def T(name, shape, dt, st=es):
            uid['i'] += 1
            nm = "t%d_%s" % (uid['i'], name)
            tl = Tile(st.enter_context(nc.sbuf_tensor(nm, list(shape), dt)), nm)
            if st is not es:
                st.callback(sc.release, tl.b)
            return tl

        class BankView:
            def __init__(self, t, off, name):
                self.t = t
                self.off = off
                self.b = Buf(name)

            def __getitem__(self, key):
                ps_, cs = key
                a = (cs.start or 0) + self.off
                b_ = (cs.stop if cs.stop is not None else 512) + self.off
                return self.t[ps_, slice(a, b_, cs.step)]

        P2 = [es.enter_context(nc.psum_tensor("pp%d" % i, [128, 1024], F32)) for i in range(4)]
        PSB = [BankView(P2[i // 2], (i % 2) * 512, "ps%d" % i) for i in range(8)]
        for p_ in PSB:
            p_.b.psum = True
        rot = {'i': 0}

        def PS(lo=0, hi=8):
            i = rot['i']
            n = hi - lo
            b = PSB[lo + (i % n)]
            rot['i'] = i + 1
            return b

        hT_b = [[Buf("hTd%d_%d" % (s, g)) for g in range(NG)] for s in range(NSEQ)]
        ya_b = [[Buf("yad%d_%d" % (s, g)) for g in range(NG)] for s in range(NSEQ)]
        out_b = [[Buf("outd%d_%d" % (s, g)) for g in range(NG)] for s in range(NSEQ)]
        dbg_b = {k: Buf("dbg_" + k) for k in DEBUG}

        def mm(out, lhsT, rhs, start, stop, r, w):
            return sc.op('pe', lambda: nc.tensor.matmul(out, lhsT, rhs, start=start, stop=stop), r=r, w=w)

        def act(out, in_, func, r, w, bias=None, scale=None):
            kw = {}
            if bias is not None:
                kw['bias'] = bias
            if scale is not None:
                kw['scale'] = scale
            return sc.op('act', lambda: nc.scalar.activation(out=out, in_=in_, func=func, **kw), r=r, w=w)

        def tt(e, out, in0, in1, op, r, w):
            eng = nc.vector if e == 'dve' else nc.gpsimd
            return sc.op(e, lambda: eng.tensor_tensor(out=out, in0=in0, in1=in1, op=op), r=r, w=w)

        def ts(e, out, in0, s1, s2, op0, op1, r, w):
            eng = nc.vector if e == 'dve' else nc.gpsimd
            if op1 is None:
                return sc.op(e, lambda: eng.tensor_scalar(out=out, in0=in0, scalar1=s1, scalar2=None, op0=op0), r=r, w=w)
            return sc.op(e, lambda: eng.tensor_scalar(out=out, in0=in0, scalar1=s1, scalar2=s2, op0=op0, op1=op1), r=r, w=w)

        def stt(out, in0, scalar, in1, op0, op1, r, w, accum_out=None):
            if accum_out is not None:
                return sc.op('dve', lambda: nc.vector.scalar_tensor_tensor(out=out, in0=in0, scalar=scalar, in1=in1, op0=op0, op1=op1, accum_out=accum_out), r=r, w=w)
            return sc.op('dve', lambda: nc.vector.scalar_tensor_tensor(out=out, in0=in0, scalar=scalar, in1=in1, op0=op0, op1=op1), r=r, w=w)

        def cp(e, out, in_, r, w):
            if e == 'act':
                return sc.op('act', lambda: nc.scalar.copy(out=out, in_=in_), r=r, w=w)
            eng = nc.vector if e == 'dve' else nc.gpsimd
            return sc.op(e, lambda: eng.tensor_copy(out=out, in_=in_), r=r, w=w)

        def ld(out, in_, tile, r=(), eng='sp'):
            e = {'sp': nc.sync, 'pool': nc.gpsimd, 'act': nc.scalar}[eng]
            return sc.dma(eng, lambda: e.dma_start(out=out, in_=in_), tile.b, r=r, w=[tile.b])

        def st_(out, in_, tile, wb, eng='sp'):
            e = {'sp': nc.sync, 'pool': nc.gpsimd, 'act': nc.scalar}[eng]
            return sc.dma(eng, lambda: e.dma_start(out=out, in_=in_), tile.b, r=[tile.b], w=wb)

        def dump(name, in_ap, tile, out_ap=None):
            if name in DEBUG:
                o = dbg_d[name] if out_ap is None else out_ap
                st_(o, in_ap, tile, [dbg_b[name]])

        pvec = T("pvec", [128, 96], F32)
        ld(pvec[:], pvec_d[:, :], pvec)
        C_N1, C_N2, C_CW, C_CB, C_BRG, C_BIG, C_LAM, C_QA, C_KVA = 0, 8, 16, 48, 56, 64, 72, 80, 82
        C_GQ, C_GQR, C_GK, C_GKR, C_INVF, C_SGN = 83, 84, 85, 86, 87, 88
        ones_f = T("ones_f", [128, 128], F32)
        sc.op('pool', lambda: nc.gpsimd.memset(ones_f[:], 1.0), w=[ones_f.b])
        onesD = T("onesD", [128, 128], F32)
        sc.op('pool', lambda: nc.gpsimd.memset(onesD[:], 1.0 / D), w=[onesD.b])
        ones256 = T("ones256", [128, 128], F32)
        sc.op('pool', lambda: nc.gpsimd.memset(ones256[:], 1.0 / 256), w=[ones256.b])
        ones128 = T("ones128", [128, 128], F32)
        sc.op('pool', lambda: nc.gpsimd.memset(ones128[:], 1.0 / 128), w=[ones128.b])
        ones96 = T("ones96", [128, 128], F32)
        sc.op('pool', lambda: nc.gpsimd.memset(ones96[:], 1.0 / 96), w=[ones96.b])
        epsb = T("epsb", [128, 1], F32)
        sc.op('pool', lambda: nc.gpsimd.memset(epsb[:], EPS), w=[epsb.b])
        oneb = T("oneb", [128, 1], F32)
        sc.op('pool', lambda: nc.gpsimd.memset(oneb[:], 1.0), w=[oneb.b])
        iot = T("iot", [128, 128], F32)
        sc.op('pool', lambda: nc.gpsimd.iota(iot[:], pattern=[[1, 128]], base=0, channel_multiplier=-1,
                                             allow_small_or_imprecise_dtypes=True), w=[iot.b])
        ident_f = T("ident_f", [128, 128], F32)
        ts('dve', ident_f[:], iot[:], 0.0, None, ALU.is_equal, None, [iot.b], [ident_f.b])
        ident_b = T("ident_b", [128, 128], BF16)
        cp('dve', ident_b[:], ident_f[:], [ident_f.b], [ident_b.b])
        crow = T("crow", [128, 255], F32)
        sc.op('pool', lambda: nc.gpsimd.iota(crow[:], pattern=[[1, 255]], base=-127, channel_multiplier=0,
                                             allow_small_or_imprecise_dtypes=True), w=[crow.b])
        ts('dve', crow[:], crow[:], 0.0, None, ALU.is_equal, None, [crow.b], [crow.b])
        iota256 = T("iota256", [128, 256], F32)
        sc.op('pool', lambda: nc.gpsimd.iota(iota256[:], pattern=[[1, 256]], base=0, channel_multiplier=0,
                                             allow_small_or_imprecise_dtypes=True), w=[iota256.b])

        sc.barrier()
        stg = [T("stg%d" % i, [128, 2048], F32) for i in range(2)]
        stg_i = {'i': 0}

        def load_w_bf16(dst_tile, dst_ap, src_ap, shape):
            p, a, b = shape
            cb = max(1, 2048 // a)
            for c0 in range(0, b, cb):
                c1 = min(b, c0 + cb)
                sg = stg[stg_i['i'] % 2]
                stg_i['i'] += 1
                sv = sg[0:p, 0:a * (c1 - c0)].rearrange("p (a b) -> p a b", a=a)
                ld(sv, src_ap[:, :, c0:c1], sg)
                cp('dve', dst_ap[:, :, c0:c1], sv, [sg.b], [dst_tile.b])

        modT = T("modT", [128, 48, NSEQ], F32)
        A1 = T("A1", [128, 8, NSEQ], F32)
        A2 = T("A2", [128, 8, NSEQ], F32)
        kap = T("kap", [128, 8], F32)
        kap2 = T("kap2", [128, 8], F32)
        gqs = T("gqs", [128, 1], F32)
        gks = T("gks", [128, 1], F32)
        with ExitStack() as st:
            cTt = T("cTt", [128, 8, NSEQ], F32, st)
            csT = T("csT", [128, 8, NSEQ], F32, st)
            badaT = T("badaT", [128, 48], F32, st)
            ld(cTt[:], cT_d[:, :, :], cTt)
            ld(badaT[:], bada_d[:, :], badaT)
            act(csT[:], cTt[:], AF.Silu, [cTt.b], [csT.b])
            pm = PSB[0]
            for cc in range(24):
                sg0 = stg[cc % 2]
                sg = sg0[:, :].rearrange("p (a b) -> p a b", a=8)
                ld(sg, wada_d[:, :, cc * 256:(cc + 1) * 256], sg0)
                for j in range(2):
                    oc = cc * 2 + j
                    for k in range(8):
                        mm(pm[:, oc * 2:oc * 2 + 2], sg[:, k, j * 128:(j + 1) * 128], csT[:, k, :],
                           k == 0, k == 7, [sg0.b, csT.b], [pm.b])
            for b in range(NSEQ):
                tt('dve', modT[:, :, b], pm[:, b:96:2], badaT[:, :], ALU.add, [pm.b, badaT.b], [modT.b])
            for b in range(NSEQ):
                stt(A1[:, :, b], modT[:, 8:16, b], 1.0, pvec[:, C_N1:C_N1 + 8], ALU.add, ALU.mult,
                    [modT.b, pvec.b], [A1.b])
                stt(A2[:, :, b], modT[:, 32:40, b], 1.0, pvec[:, C_N2:C_N2 + 8], ALU.add, ALU.mult,
                    [modT.b, pvec.b], [A2.b])
            tk = T("tk", [128, 8], F32, st)
            act(tk[:], pvec[:, C_LAM:C_LAM + 8], AF.Exp, [pvec.b], [tk.b], scale=-1.0)
            act(tk[:], tk[:], AF.Ln, [tk.b], [tk.b], bias=oneb[:, 0:1])
            ts('dve', kap[:], tk[:], -8.0, None, ALU.mult, None, [tk.b], [kap.b])
            ts('dve', kap2[:], tk[:], -16.0, None, ALU.mult, None, [tk.b], [kap2.b])
            tt('dve', gqs[:], pvec[:, C_GQR:C_GQR + 1], pvec[:, C_SGN:C_SGN + 1], ALU.mult, [pvec.b], [gqs.b])
            tt('dve', gks[:], pvec[:, C_GKR:C_GKR + 1], pvec[:, C_SGN:C_SGN + 1], ALU.mult, [pvec.b], [gks.b])
            dump("modT", modT[:], modT)
            sc.barrier()

        uvb_b = Buf("uvb")
        with ExitStack() as st:
            cin = [T("cin%d" % i, [128, 4, 2048], F32, st) for i in range(2)]
            co_d = [T("cod%d" % i, [128, 3, 2048], BF16, st) for i in range(2)]
            co_a = [T("coa%d" % i, [128, 1, 2048], BF16, st) for i in range(2)]
            uvv = uv_d.rearrange("(c p j) d -> c p j d", p=128, j=4)
            uvbv = uvb_d.rearrange("(c p j) d -> c p j d", p=128, j=4)
            for c in range(NEXP // 512):
                ci = cin[c % 2]
                ld(ci[:], uvv[c], ci)
                cp('dve', co_d[c % 2][:], ci[:, 0:3, :], [ci.b], [co_d[c % 2].b])
                cp('act', co_a[c % 2][:], ci[:, 3:4, :], [ci.b], [co_a[c % 2].b])
                st_(uvbv[c][:, 0:3, :], co_d[c % 2][:], co_d[c % 2], [uvb_b])
                st_(uvbv[c][:, 3:4, :], co_a[c % 2][:], co_a[c % 2], [uvb_b])
            sc.barrier()
        if dbg_stop == 'S0':
            sc.barrier()
            return nc
        SH1, GT1, SH2, GT2 = 0, 16, 24, 40

        def rms_modulate(xb, hTb, A, shoff, s, st, tag):
            sq = T("sq" + tag, [128, 8, G], F32, st)
            rstd = T("rstd" + tag, [128, G], F32, st)
            act(sq[:], xb[:], AF.Square, [xb.b], [sq.b])
            pb = PS()
            for k in range(8):
                mm(pb[:, :], onesD[:, :], sq[:, k, :], k == 0, k == 7, [onesD.b, sq.b], [pb.b])
            act(rstd[:], pb[:, :], AF.Ln, [pb.b], [rstd.b], bias=epsb[:, 0:1])
            act(rstd[:], rstd[:], AF.Exp, [rstd.b], [rstd.b], scale=-0.5)
            for k in range(8):
                stt(sq[:, k, :], xb[:, k, :], A[:, k, s:s + 1], rstd[:], ALU.mult, ALU.mult,
                    [xb.b, A.b, rstd.b], [sq.b])
                ts('pool', hTb[:, k, :], sq[:, k, :], modT[:, shoff + k, s:s + 1], None, ALU.add, None,
                   [sq.b, modT.b], [hTb.b])

        def chk(name):
            if dbg_stop == name:
                raise _Stop()

        try:
         for s in range(NSEQ):
            with ExitStack() as sa:
                qT = T("qT", [96, 8, S], BF16, sa)
                kT = T("kT", [96, 8, S], BF16, sa)
                Vb = T("Vb", [128, 16, 8, 65], BF16, sa)
                with ExitStack() as st:
                    cosT = T("cosT", [96, G], F32, st)
                    sinT = T("sinT", [96, G], F32, st)
                    posi = T("posi", [96, G], I32, st)
                    ang = T("ang", [96, G], F32, st)
                    nf = T("nf", [96, G], F32, st)
                    ni = T("ni", [96, G], I32, st)
                    wlat = T("wlat", [128, 8, 512], BF16, st)
                    wuq = T("wuq", [128, 2, 1536], BF16, st)
                    wkn = T("wkn", [128, 768], BF16, st)
                    wv = T("wv", [128, 512], BF16, st)
                    emat = T("emat", [128, 96], BF16, st)
                    load_w_bf16(wlat, wlat[:], wlat_d[:, :, :], (128, 8, 512))
                    load_w_bf16(wuq, wuq[:], wuq_d[:, :, :], (128, 2, 1536))
                    load_w_bf16(wkn, wkn[:].rearrange("p (a b) -> p a b", a=1), wkn_d[:, :].rearrange("p (a b) -> p a b", a=1), (128, 1, 768))
                    load_w_bf16(wv, wv[:].rearrange("p (a b) -> p a b", a=1), wv_d[:, :].rearrange("p (a b) -> p a b", a=1), (128, 1, 512))
                    load_w_bf16(emat, emat[:].rearrange("p (a b) -> p a b", a=1), emat_d[:, :].rearrange("p (a b) -> p a b", a=1), (128, 1, 96))
                    sc.op('pool', lambda: nc.gpsimd.memset(Vb[:], 1.0), w=[Vb.b])

                    def rope_tables(tok):
                        ld(posi[:], pos_d[s, tok].partition_broadcast(96), posi)
                        cp('dve', ang[:], posi[:], [posi.b], [ang.b])
                        ts('dve', ang[:], ang[:], pvec[0:96, C_INVF:C_INVF + 1], None, ALU.mult, None, [ang.b, pvec.b], [ang.b])
                        ts('dve', nf[:], ang[:], 1.0 / (2 * PI), None, ALU.mult, None, [ang.b], [nf.b])
                        cp('dve', ni[:], nf[:], [nf.b], [ni.b])
                        cp('dve', nf[:], ni[:], [ni.b], [nf.b])
                        C1 = 6.28125
                        C2 = 2 * PI - C1
                        stt(ang[:], nf[:], -C1, ang[:], ALU.mult, ALU.add, [nf.b, ang.b], [ang.b])
                        stt(ang[:], nf[:], -C2, ang[:], ALU.mult, ALU.add, [nf.b, ang.b], [ang.b])

                        def wrap_sin(dst, shift):
                            pf = posi[:].bitcast(F32)
                            ts('dve', nf[:], ang[:], shift, None, ALU.add, None, [ang.b], [nf.b])
                            for _ in range(2):
                                ts('dve', pf, nf[:], PI, -2 * PI, ALU.is_gt, ALU.mult, [nf.b], [posi.b])
                                tt('dve', nf[:], nf[:], pf, ALU.add, [nf.b, posi.b], [nf.b])
                                ts('dve', pf, nf[:], -PI, 2 * PI, ALU.is_lt, ALU.mult, [nf.b], [posi.b])
                                tt('dve', nf[:], nf[:], pf, ALU.add, [nf.b, posi.b], [nf.b])
                            ts('dve', nf[:], nf[:], 3.141592, -3.141592, ALU.min, ALU.max, [nf.b], [nf.b])
                            act(dst[:], nf[:], AF.Sin, [nf.b], [dst.b])
                        wrap_sin(sinT, 0.0)
                        wrap_sin(cosT, PI / 2)

                    xb = T("xb", [128, 8, G], F32, st)
                    hTb = T("hTb", [128, 8, G], BF16, st)
                    sqq = T("sqq", [128, 2, G], F32, st)
                    rq = T("rq", [128, G], F32, st)
                    cqn = T("cqn", [128, 2, G], BF16, st)
                    ckvn = T("ckvn", [128, G], BF16, st)
                    krb = T("krb", [128, G], BF16, st)
                    krrb = T("krrb", [128, G], BF16, st)
                    sc.op('pool', lambda: nc.gpsimd.memset(krb[:], 0.0), w=[krb.b])
                    sc.op('pool', lambda: nc.gpsimd.memset(krrb[:], 0.0), w=[krrb.b])
                    m2k = T("m2k", [96, G], F32, st)
                    sqh = T("sqh", [96, G], F32, st)
                    rh = T("rh", [96, G], F32, st)
                    m1 = T("m1", [96, G], F32, st)
                    m2 = T("m2", [96, G], F32, st)
                    for g in range(NG):
                        tok = slice(g * G, (g + 1) * G)
                        rope_tables(tok)
                        if g == 0 and s == 0:
                            dump("cosT", cosT[:], cosT)
                            dump("sinT", sinT[:], sinT)
                        chk('A1')
                        ld(xb[:], xT_d[s, :, :, tok], xb)
                        if g == 0:
                            sqA = T("sqA", [128, 8, G], F32, st)
                            rstdA = T("rstdA", [128, G], F32, st)
                        act(sqA[:], xb[:], AF.Square, [xb.b], [sqA.b])
                        pb = PS()
                        for k in range(8):
                            mm(pb[:, :], onesD[:, :], sqA[:, k, :], k == 0, k == 7, [onesD.b, sqA.b], [pb.b])
                        act(rstdA[:], pb[:, :], AF.Ln, [pb.b], [rstdA.b], bias=epsb[:, 0:1])
                        act(rstdA[:], rstdA[:], AF.Exp, [rstdA.b], [rstdA.b], scale=-0.5)
                        for k in range(8):
                            stt(sqA[:, k, :], xb[:, k, :], A1[:, k, s:s + 1], rstdA[:], ALU.mult, ALU.mult,
                                [xb.b, A1.b, rstdA.b], [sqA.b])
                            act(hTb[:, k, :], sqA[:, k, :], AF.Identity, [sqA.b, modT.b], [hTb.b], bias=modT[:, SH1 + k, s:s + 1])
                        st_(hT_d[s, :, :, tok], hTb[:], hTb, [hT_b[s][g]])
                        if g == 0 and s == 0:
                            dump("hT0", hTb[:], hTb)
                        chk('A2')
                        pcq = [PS(), PS()]
                        for j in range(2):
                            for k in range(8):
                                mm(pcq[j][:, :], wlat[:, k, j * 128:(j + 1) * 128], hTb[:, k, :], k == 0, k == 7,
                                   [wlat.b, hTb.b], [pcq[j].b])
                        pkv = PS()
                        for k in range(8):
                            mm(pkv[:, :], wlat[:, k, 256:384], hTb[:, k, :], k == 0, k == 7, [wlat.b, hTb.b], [pkv.b])
                        pkr = PS()
                        for k in range(8):
                            mm(pkr[0:32, :], wlat[:, k, 384:416], hTb[:, k, :], k == 0, k == 7, [wlat.b, hTb.b], [pkr.b])
                        pkrr = PS()
                        for k in range(8):
                            mm(pkrr[0:32, :], wlat[:, k, 416:448], hTb[:, k, :], k == 0, k == 7, [wlat.b, hTb.b], [pkrr.b])
                        for j in range(2):
                            act(sqq[:, j, :], pcq[j][:, :], AF.Square, [pcq[j].b], [sqq.b])
                        pn = PS()
                        for j in range(2):
                            mm(pn[:, :], ones256[:, :], sqq[:, j, :], j == 0, j == 1, [ones256.b, sqq.b], [pn.b])
                        act(rq[:], pn[:, :], AF.Ln, [pn.b], [rq.b], bias=epsb[:, 0:1])
                        act(rq[:], rq[:], AF.Exp, [rq.b], [rq.b], scale=-0.5)
                        for j in range(2):
                            stt(cqn[:, j, :], pcq[j][:, :], pvec[:, C_QA + j:C_QA + j + 1], rq[:], ALU.mult, ALU.mult,
                                [pcq[j].b, pvec.b, rq.b], [cqn.b])
                        act(sqq[:, 0, :], pkv[:, :], AF.Square, [pkv.b], [sqq.b])
                        pn = PS()
                        mm(pn[:, :], ones128[:, :], sqq[:, 0, :], True, True, [ones128.b, sqq.b], [pn.b])
                        act(rq[:], pn[:, :], AF.Ln, [pn.b], [rq.b], bias=epsb[:, 0:1])
                        act(rq[:], rq[:], AF.Exp, [rq.b], [rq.b], scale=-0.5)
                        stt(ckvn[:], pkv[:, :], pvec[:, C_KVA:C_KVA + 1], rq[:], ALU.mult, ALU.mult,
                            [pkv.b, pvec.b, rq.b], [ckvn.b])
                        cp('act', krb[0:32, :], pkr[0:32, :], [pkr.b], [krb.b])
                        cp('act', krrb[0:32, :], pkrr[0:32, :], [pkrr.b], [krrb.b])
                        chk('A3')
                        pr = PS()
                        mm(pr[0:96, :], emat[:, :], krrb[:, :], True, True, [emat.b, krrb.b], [pr.b])
                        stt(m2k[:], pr[0:96, :], gks[0:96, 0:1], sinT[:, :], ALU.mult, ALU.mult,
                            [pr.b, gks.b, sinT.b], [m2k.b])
                        chk('A3a')
                        for hh in range(16):
                            isq = hh < 8
                            h = hh % 8
                            pa = PS()
                            if isq:
                                for j in range(2):
                                    mm(pa[0:96, :], wuq[:, j, h * 96:(h + 1) * 96], cqn[:, j, :], j == 0, j == 1,
                                       [wuq.b, cqn.b], [pa.b])
                                pb2 = PS()
                                for j in range(2):
                                    mm(pb2[0:96, :], wuq[:, j, 768 + h * 96:768 + (h + 1) * 96], cqn[:, j, :], j == 0, j == 1,
                                       [wuq.b, cqn.b], [pb2.b])
                            else:
                                mm(pa[0:96, :], wkn[:, h * 96:(h + 1) * 96], ckvn[:, :], True, False, [wkn.b, ckvn.b], [pa.b])
                                mm(pa[0:96, :], emat[:, :], krb[:, :], False, True, [emat.b, krb.b], [pa.b])
                            chk('A3b')
                            act(sqh[:], pa[0:96, :], AF.Square, [pa.b], [sqh.b])
                            pc = PS()
                            mm(pc[0:96, :], ones96[0:96, 0:96], sqh[:, :], True, True, [ones96.b, sqh.b], [pc.b])
                            act(rh[:], pc[0:96, :], AF.Ln, [pc.b], [rh.b], bias=epsb[0:96, 0:1])
                            act(rh[:], rh[:], AF.Exp, [rh.b], [rh.b], scale=-0.5)
                            chk('A3c')
                            gcol = C_GQ if isq else C_GK
                            stt(m1[:], pa[0:96, :], pvec[0:96, gcol:gcol + 1], cosT[:, :], ALU.mult, ALU.mult,
                                [pa.b, pvec.b, cosT.b], [m1.b])
                            chk('A3d')
                            if isq:
                                stt(m2[:], pb2[0:96, :], gqs[0:96, 0:1], sinT[:, :], ALU.mult, ALU.mult,
                                    [pb2.b, gqs.b, sinT.b], [m2.b])
                                tt('dve', m1[:], m1[:], m2[:], ALU.add, [m1.b, m2.b], [m1.b])
                                tt('dve', qT[:, h, tok], m1[:], rh[:], ALU.mult, [m1.b, rh.b], [qT.b])
                            else:
                                tt('dve', m1[:], m1[:], m2k[:], ALU.add, [m1.b, m2k.b], [m1.b])
                                tt('dve', kT[:, h, tok], m1[:], rh[:], ALU.mult, [m1.b, rh.b], [kT.b])
                        chk('A4')
                        for t4 in range(4):
                            pv = PS()
                            mm(pv[:, :], ckvn[:, t4 * 128:(t4 + 1) * 128], wv[:, :], True, True, [ckvn.b, wv.b], [pv.b])
                            cp('act', Vb[:, g * 4 + t4, :, 0:64], pv[:, :].rearrange("p (h d) -> p h d", h=8),
                               [pv.b], [Vb.b])
                    if s == 0:
                        dump("qT0", qT[:], qT)
                        dump("kT0", kT[:], kT)
                        dump("Vb0", Vb[:], Vb)
                    sc.barrier()
                if dbg_stop == 'A':
                    break
                attnT = T("attnT", [64, 8, S], BF16, sa)
                with ExitStack() as st:
                    pTs = [T("pT%d" % i, [128, G], BF16, st) for i in range(3)]
                    accS = T("accS", [65, S], F32, st)
                    rrow = T("rrow", [65, S], F32, st)
                    scale = 96 ** -0.5
                    acc = PSB[0:4]
                    steps = []
                    for h in range(8):
                        for j in range(16):
                            for qg in range(j // 4, 4):
                                steps.append((h, j, qg))

                    def qk(step):
                        h, j, qg = step
                        q0 = max(qg * G, j * 128)
                        q1 = (qg + 1) * G
                        ps_ = PS(4, 8)
                        mm(ps_[:, 0:q1 - q0], kT[:, h, j * 128:(j + 1) * 128], qT[:, h, q0:q1], True, True,
                           [kT.b, qT.b], [ps_.b])
                        return ps_

                    def epilogue(h):
                        for qg in range(4):
                            cp('act', accS[:, qg * G:(qg + 1) * G], acc[qg][0:65, :], [acc[qg].b], [accS.b])
                        act(rrow[64:65, :], accS[64:65, :], AF.Ln, [accS.b], [rrow.b])
                        act(rrow[64:65, :], rrow[64:65, :], AF.Exp, [rrow.b], [rrow.b], scale=-1.0)
                        for qg in range(4):
                            pbc = acc[qg]
                            mm(pbc[0:64, :], ones_f[64:65, 0:64], rrow[64:65, qg * G:(qg + 1) * G], True, True,
                               [ones_f.b, rrow.b], [pbc.b])
                            tt('dve', attnT[:, h, qg * G:(qg + 1) * G], accS[0:64, qg * G:(qg + 1) * G], pbc[0:64, :],
                               ALU.mult, [accS.b, pbc.b], [attnT.b])

                    cur = qk(steps[0])
                    for n, (h, j, qg) in enumerate(steps):
                        nxt = qk(steps[n + 1]) if n + 1 < len(steps) else None
                        q0 = max(qg * G, j * 128)
                        q1 = (qg + 1) * G
                        N = q1 - q0
                        pT = pTs[n % 3]
                        act(pT[:, 0:N], cur[:, 0:N], AF.Exp, [cur.b], [pT.b], scale=scale)
                        if qg == j // 4:
                            sc.op('pool', lambda: nc.gpsimd.memset(pT[64:128, 0:64], 0.0), w=[pT.b])
                        c0 = q0 - qg * G
                        lastj = min(15, 4 * qg + 3)
                        mm(acc[qg][0:65, c0:c0 + N], Vb[:, j, h, :], pT[:, 0:N], j == 0, j == lastj,
                           [Vb.b, pT.b], [acc[qg].b])
                        cur = nxt
                        if n + 1 == len(steps) or steps[n + 1][0] != h:
                            epilogue(h)
                    if s == 0:
                        dump("attnT0", attnT[:], attnT)
                    sc.barrier()
                if dbg_stop == 'B':
                    break
                with ExitStack() as st:
                    wga = T("wga", [128, 8, 1024], BF16, st)
                    woa = T("woa", [64, 8, 1024], BF16, st)
                    for cc in range(2):
                        load_w_bf16(wga, wga[:, :, cc * 512:(cc + 1) * 512], wga_d[:, :, cc * 512:(cc + 1) * 512], (128, 8, 512))
                        load_w_bf16(woa, woa[:, :, cc * 512:(cc + 1) * 512], woa_d[:, :, cc * 512:(cc + 1) * 512], (64, 8, 512))
                    hTb = T("hTbC", [128, 8, G], BF16, st)
                    gaT = T("gaT", [128, G], F32, st)
                    yaT = T("yaT", [128, 8, G], BF16, st)
                    for g in range(NG):
                        tok = slice(g * G, (g + 1) * G)
                        ld(hTb[:], hT_d[s, :, :, tok], hTb, r=[hT_b[s][g]])
                        for oc in range(8):
                            p1 = PS()
                            for k in range(8):
                                mm(p1[:, :], wga[:, k, oc * 128:(oc + 1) * 128], hTb[:, k, :], k == 0, k == 7,
                                   [wga.b, hTb.b], [p1.b])
                            act(gaT[:], p1[:, :], AF.Sigmoid, [p1.b], [gaT.b])
                            p2 = PS()
                            for h in range(8):
                                mm(p2[:, :], woa[:, h, oc * 128:(oc + 1) * 128], attnT[:, h, tok], h == 0, h == 7,
                                   [woa.b, attnT.b], [p2.b])
                            tt('dve', yaT[:, oc, :], p2[:, :], gaT[:], ALU.mult, [p2.b, gaT.b], [yaT.b])
                        st_(ya_d[s, :, :, tok], yaT[:], yaT, [ya_b[s][g]])
                    sc.barrier()
            if dbg_stop in ('A', 'B', 'C'):
                break
            with ExitStack() as st:
                wlx = T("wlx", [128, 8, 1024], BF16, st)
                wlg = T("wlg", [128, 8, 1024], BF16, st)
                wgb = T("wgb", [128, 8, 1024], BF16, st)
                wol = T("wol", [128, 8, 1024], BF16, st)
                wout = T("wout", [128, 8, 1024], BF16, st)
                wrg = T("wrg", [128, 8, 128], BF16, st)
                wig = T("wig", [128, 8, 128], BF16, st)
                for cc in range(2):
                    cs_ = slice(cc * 512, (cc + 1) * 512)
                    load_w_bf16(wlx, wlx[:, :, cs_], wlx_d[:, :, cs_], (128, 8, 512))
                    load_w_bf16(wlg, wlg[:, :, cs_], wlg_d[:, :, cs_], (128, 8, 512))
                    load_w_bf16(wgb, wgb[:, :, cs_], wgb_d[:, :, cs_], (128, 8, 512))
                    load_w_bf16(wol, wol[:, :, cs_], wol_d[:, :, cs_], (128, 8, 512))
                    load_w_bf16(wout, wout[:, :, cs_], wout_d[:, :, cs_], (128, 8, 512))
                load_w_bf16(wrg, wrg[:], wrg_d[:, :, :], (128, 8, 128))
                load_w_bf16(wig, wig[:], wig_d[:, :, :], (128, 8, 128))
                hTb = T("hTbD", [128, 8, G], BF16, st)
                xb = T("xbD", [128, 8, G], F32, st)
                yaT = T("yaTD", [128, 8, G], BF16, st)
                xe = T("xe", [128, 8, G + 4], F32, st)
                hprev = T("hprev", [128, 8], F32, st)
                xcs = [T("xc%d" % i, [128, G], F32, st) for i in range(2)]
                xcbs = [T("xcb%d" % i, [128, G], BF16, st) for i in range(2)]
                rr = T("rr", [128, G], F32, st)
                ii = T("ii", [128, G], F32, st)
                aa = T("aa", [128, G], F32, st)
                a2 = T("a2", [128, G], F32, st)
                bb = T("bb", [128, G], F32, st)
                hs = T("hs", [128, G], F32, st)
                gls = [T("gl%d" % i, [128, G], F32, st) for i in range(2)]
                recT = T("recT", [128, 8, G], BF16, st)
                gb = T("gb", [128, G], F32, st)
                t1 = T("t1", [128, G], F32, st)
                sc.op('pool', lambda: nc.gpsimd.memset(xe[:], 0.0), w=[xe.b])
                sc.op('pool', lambda: nc.gpsimd.memset(hprev[:], 0.0), w=[hprev.b])
                for g in range(NG):
                    tok = slice(g * G, (g + 1) * G)
                    ld(hTb[:], hT_d[s, :, :, tok], hTb, r=[hT_b[s][g]])
                    ld(xb[:], xT_d[s, :, :, tok], xb)
                    ld(yaT[:], ya_d[s, :, :, tok], yaT, r=[ya_b[s][g]])
                    def lru_p1(n):
                        xc_, xcb_, gl_ = xcs[n % 2], xcbs[n % 2], gls[n % 2]
                        px = PS()
                        for k in range(8):
                            mm(px[:, :], wlx[:, k, n * 128:(n + 1) * 128], hTb[:, k, :], k == 0, k == 7,
                               [wlx.b, hTb.b], [px.b])
                        cp('act', xe[:, n, 4:G + 4], px[:, :], [px.b], [xe.b])
                        ts('dve', xc_[:], xe[:, n, 4:G + 4], pvec[:, C_CW + 3 * 8 + n:C_CW + 3 * 8 + n + 1],
                           pvec[:, C_CB + n:C_CB + n + 1], ALU.mult, ALU.add, [xe.b, pvec.b], [xc_.b])
                        for kk in range(3):
                            stt(xc_[:], xe[:, n, 1 + kk:1 + kk + G], pvec[:, C_CW + kk * 8 + n:C_CW + kk * 8 + n + 1], xc_[:],
                                ALU.mult, ALU.add, [xe.b, pvec.b, xc_.b], [xc_.b])
                        cp('pool', xe[:, n, 1:4], xe[:, n, G + 1:G + 4], [xe.b], [xe.b])
                        cp('act', xcb_[:], xc_[:], [xc_.b], [xcb_.b])
                        pg = PS()
                        for k in range(8):
                            mm(pg[:, :], wlg[:, k, n * 128:(n + 1) * 128], hTb[:, k, :], k == 0, k == 7,
                               [wlg.b, hTb.b], [pg.b])
                        act(gl_[:], pg[:, :], AF.Gelu_apprx_tanh, [pg.b], [gl_.b])

                    def lru_p2(n):
                        xc_, xcb_, gl_ = xcs[n % 2], xcbs[n % 2], gls[n % 2]
                        pr_ = PS()
                        mm(pr_[:, :], wrg[:, n, :], xcb_[:], True, True, [wrg.b, xcb_.b], [pr_.b])
                        pi_ = PS()
                        mm(pi_[:, :], wig[:, n, :], xcb_[:], True, True, [wig.b, xcb_.b], [pi_.b])
                        act(rr[:], pr_[:, :], AF.Sigmoid, [pr_.b, pvec.b], [rr.b], bias=pvec[:, C_BRG + n:C_BRG + n + 1])
                        act(ii[:], pi_[:, :], AF.Sigmoid, [pi_.b, pvec.b], [ii.b], bias=pvec[:, C_BIG + n:C_BIG + n + 1])
                        act(aa[:], rr[:], AF.Exp, [rr.b, kap.b], [aa.b], scale=kap[:, n:n + 1])
                        act(a2[:], rr[:], AF.Exp, [rr.b, kap2.b], [a2.b], scale=kap2[:, n:n + 1])
                        act(a2[:], a2[:], AF.Sqrt, [a2.b], [a2.b], scale=-1.0, bias=oneb[:, 0:1])
                        tt('dve', bb[:], a2[:], ii[:], ALU.mult, [a2.b, ii.b], [bb.b])
                        tt('dve', bb[:], bb[:], xc_[:], ALU.mult, [bb.b, xc_.b], [bb.b])
                        sc.op('dve', lambda: nc.vector.tensor_tensor_scan(out=hs[:], data0=aa[:], data1=bb[:],
                                                                          initial=hprev[:, n:n + 1], op0=ALU.mult, op1=ALU.add),
                              r=[aa.b, bb.b, hprev.b], w=[hs.b])
                        cp('pool', hprev[:, n:n + 1], hs[:, G - 1:G], [hs.b], [hprev.b])
                        tt('dve', recT[:, n, :], hs[:], gl_[:], ALU.mult, [hs.b, gl_.b], [recT.b])

                    lru_p1(0)
                    for n in range(8):
                        if n + 1 < 8:
                            lru_p1(n + 1)
                        lru_p2(n)
                    if s == 0 and g == 0:
                        dump("recT0", recT[:], recT)
                    for oc in range(8):
                        p1 = PS()
                        for k in range(8):
                            mm(p1[:, :], wgb[:, k, oc * 128:(oc + 1) * 128], hTb[:, k, :], k == 0, k == 7,
                               [wgb.b, hTb.b], [p1.b])
                        act(gb[:], p1[:, :], AF.Sigmoid, [p1.b], [gb.b])
                        p2 = PS()
                        for k in range(8):
                            mm(p2[:, :], wol[:, k, oc * 128:(oc + 1) * 128], recT[:, k, :], k == 0, k == 7,
                               [wol.b, recT.b], [p2.b])
                        tt('dve', t1[:], p2[:, :], gb[:], ALU.mult, [p2.b, gb.b], [t1.b])
                        tt('dve', yaT[:, oc, :], t1[:], yaT[:, oc, :], ALU.add, [t1.b, yaT.b], [yaT.b])
                    for oc in range(8):
                        p3 = PS()
                        for k in range(8):
                            mm(p3[:, :], wout[:, k, oc * 128:(oc + 1) * 128], yaT[:, k, :], k == 0, k == 7,
                               [wout.b, yaT.b], [p3.b])
                        stt(xb[:, oc, :], p3[:, :], modT[:, GT1 + oc, s:s + 1], xb[:, oc, :], ALU.mult, ALU.add,
                            [p3.b, modT.b, xb.b], [xb.b])
                    st_(out_d[s, :, :, tok], xb[:], xb, [out_b[s][g]])
                sc.barrier()
            if dbg_stop == 'D':
                continue
            with ExitStack() as st:
                wq = T("wq", [128, 8, 2048], BF16, st)
                skT = T("skT", [128, 16, 128], BF16, st)
                load_w_bf16(wq, wq[:], wq_d[:, :, :], (128, 8, 2048))
                load_w_bf16(skT, skT[:], skT_d[:, :, :], (128, 16, 128))
                x1g = T("x1g", [128, 8, G], F32, st)
                h2b = T("h2b", [128, 8, G], BF16, st)
                qTb = T("qTb", [128, 16, G], BF16, st)
                gi = 0
                for g in range(NG):
                    tok = slice(g * G, (g + 1) * G)
                    with ExitStack() as s3:
                        sqE = T("sqE", [128, 8, G], F32, s3)
                        rstdE = T("rstdE", [128, G], F32, s3)
                        ld(x1g[:], out_d[s, :, :, tok], x1g, r=[out_b[s][g]])
                        act(sqE[:], x1g[:], AF.Square, [x1g.b], [sqE.b])
                        pb = PS(2, 8)
                        for k in range(8):
                            mm(pb[:, :], onesD[:, :], sqE[:, k, :], k == 0, k == 7, [onesD.b, sqE.b], [pb.b])
                        act(rstdE[:], pb[:, :], AF.Ln, [pb.b], [rstdE.b], bias=epsb[:, 0:1])
                        act(rstdE[:], rstdE[:], AF.Exp, [rstdE.b], [rstdE.b], scale=-0.5)
                        for k in range(8):
                            stt(sqE[:, k, :], x1g[:, k, :], A2[:, k, s:s + 1], rstdE[:], ALU.mult, ALU.mult,
                                [x1g.b, A2.b, rstdE.b], [sqE.b])
                            act(h2b[:, k, :], sqE[:, k, :], AF.Identity, [sqE.b, modT.b], [h2b.b], bias=modT[:, SH2 + k, s:s + 1])
                        if s == 0 and g == 0:
                            dump("h2T0", h2b[:], h2b)
                        for m in range(16):
                            pq = PS(2, 8)
                            for k in range(8):
                                mm(pq[:, :], wq[:, k, m * 128:(m + 1) * 128], h2b[:, k, :], k == 0, k == 7,
                                   [wq.b, h2b.b], [pq.b])
                            cp('act', qTb[:, m, :], pq[:, :], [pq.b], [qTb.b])
                        sc.barrier()
                    with ExitStack() as s3:
                        scs = T("scs", [128, 16, 128], F32, s3)
                        top = T("top", [128, 16, 16], F32, s3)
                        tix = T("tix", [128, 16, 16], U32, s3)
                        tif = T("tif", [128, 16, 16], F32, s3)
                        cand = T("cand", [128, 8, 256], F32, s3)
                        eq = T("eq", [128, 128, 16], BF16, s3)
                        ai = T("ai", [128, 128], U32, s3)
                        bi = T("bi", [128, 128], U32, s3)
                        af = T("af", [128, 128], F32, s3)
                        bf = T("bf", [128, 128], F32, s3)
                        Isel = T("Isel", [128, 128], F32, s3)
                        Jsel = T("Jsel", [128, 128], F32, s3)
                        best = T("best", [128, 8, 16], F32, s3)
                        bpos = T("bpos", [128, 8, 16], U32, s3)
                        gsum = T("gsum", [128, 8], F32, s3)
                        idxf = T("idxf", [128, 128], F32, s3)
                        gw = T("gw", [128, 8, 16], F32, s3)
                        idxTs = [T("idxT%d" % i, [128, 128], U32, s3) for i in range(4)]
                        gTs = [T("gT%d" % i, [128, 128], F32, s3) for i in range(4)]
                        NGB = 10
                        DG = 6
                        assert NGB - DG >= 4
                        Gs = [T("Gg%d" % i, [128, 2048], BF16, s3) for i in range(NGB)]
                        wsel = [T("wsel%d" % i, [128, 128], BF16, s3) for i in range(3)]
                        actw = [T("actw%d" % i, [128, 1], F32, s3) for i in range(4)]
                        pacc = T("pacc", [128, 1024], F32, s3)
                        junk2 = T("junk2", [128, 1024], BF16, s3)
                        actr = [T("actr%d" % i, [128, 1], F32, s3) for i in range(4)]
                        actg = [T("actg%d" % i, [128, 1], F32, s3) for i in range(4)]
                        scs2v = stg[0][:, :].rearrange("p (a b) -> p a b", a=16)
                        cand2v = stg[1][:, :].rearrange("p (a b) -> p a b", a=8)
                        scs2b, cand2b = stg[0].b, stg[1].b
                        pS = PSB[6]
                        pT7 = PSB[7]

                        def topk_gen(t4):
                            tsl = slice(t4 * 128, (t4 + 1) * 128)
                            idxT = idxTs[t4]
                            gT = gTs[t4]
                            for q4 in range(4):
                                for mi in range(4):
                                    m = q4 * 4 + mi
                                    mm(pS[:, mi * 128:(mi + 1) * 128], qTb[:, m, tsl], skT[:, m, :], True, True,
                                       [qTb.b, skT.b], [pS.b])
                                yield
                                cp('act', scs[:, q4 * 4:(q4 + 1) * 4, :], pS[:, :].rearrange("p (a b) -> p a b", a=4),
                                   [pS.b], [scs.b])
                                yield
                            for m in range(16):
                                sc.op('dve', lambda: nc.vector.max(out=top[:, m, 0:8], in_=scs[:, m, :]), r=[scs.b], w=[top.b])
                                yield
                                sc.op('dve', lambda: nc.vector.max_index(out=tix[:, m, 0:8], in_max=top[:, m, 0:8], in_values=scs[:, m, :]),
                                      r=[scs.b, top.b], w=[tix.b])
                                yield
                                sc.op('dve', lambda: nc.vector.match_replace(out=scs2v[:, m, :], in_to_replace=top[:, m, 0:8],
                                                                             in_values=scs[:, m, :], imm_value=-1e30),
                                      r=[scs.b, top.b], w=[scs2b])
                                yield
                                sc.op('dve', lambda: nc.vector.max(out=top[:, m, 8:16], in_=scs2v[:, m, :]), r=[scs2b], w=[top.b])
                                yield
                                sc.op('dve', lambda: nc.vector.max_index(out=tix[:, m, 8:16], in_max=top[:, m, 8:16], in_values=scs2v[:, m, :]),
                                      r=[scs2b, top.b], w=[tix.b])
                                yield
                            cp('dve', tif[:], tix[:], [tix.b], [tif.b])
                            yield
                            for h in range(8):
                                cv = cand[:, h, :].rearrange("p (a b) -> p a b", a=16)
                                tt('dve', cv, top[:, 2 * h, :].unsqueeze(2).to_broadcast([128, 16, 16]),
                                   top[:, 2 * h + 1, :].unsqueeze(1).to_broadcast([128, 16, 16]), ALU.add, [top.b], [cand.b])
                                yield
                                sc.op('dve', lambda: nc.vector.max(out=best[:, h, 0:8], in_=cand[:, h, :]), r=[cand.b], w=[best.b])
                                yield
                                sc.op('dve', lambda: nc.vector.max_index(out=bpos[:, h, 0:8], in_max=best[:, h, 0:8], in_values=cand[:, h, :]),
                                      r=[cand.b, best.b], w=[bpos.b])
                                yield
                                sc.op('dve', lambda: nc.vector.match_replace(out=cand2v[:, h, :], in_to_replace=best[:, h, 0:8],
                                                                             in_values=cand[:, h, :], imm_value=-1e30),
                                      r=[cand.b, best.b], w=[cand2b])
                                yield
                                sc.op('dve', lambda: nc.vector.max(out=best[:, h, 8:16], in_=cand2v[:, h, :]), r=[cand2b], w=[best.b])
                                yield
                                sc.op('dve', lambda: nc.vector.max_index(out=bpos[:, h, 8:16], in_max=best[:, h, 8:16], in_values=cand2v[:, h, :]),
                                      r=[cand2b, best.b], w=[bpos.b])
                                yield
                            bflat = bpos[:].rearrange("p a b -> p (a b)")
                            ts('dve', ai[:], bflat, 4, None, ALU.logical_shift_right, None, [bpos.b], [ai.b])
                            yield
                            ts('dve', bi[:], bflat, 15, None, ALU.bitwise_and, None, [bpos.b], [bi.b])
                            yield
                            cp('dve', af[:], ai[:], [ai.b], [af.b])
                            yield
                            cp('dve', bf[:], bi[:], [bi.b], [bf.b])
                            yield
                            for (xf, par, sel) in ((af, 0, Isel), (bf, 1, Jsel)):
                                tt('dve', eq[:], iota256[:, 0:16].unsqueeze(1).to_broadcast([128, 128, 16]),
                                   xf[:].unsqueeze(2).to_broadcast([128, 128, 16]), ALU.is_equal, [iota256.b, xf.b], [eq.b])
                                yield
                                e4 = eq[:].rearrange("p (h k) a -> p h k a", h=8)
                                tt('dve', e4, e4, tif[:, par::2, :].unsqueeze(2).to_broadcast([128, 8, 16, 16]), ALU.mult,
                                   [eq.b, tif.b], [eq.b])
                                yield
                                sc.op('dve', lambda: nc.vector.tensor_reduce(out=sel[:], in_=eq[:], axis=mybir.AxisListType.X, op=ALU.add),
                                      r=[eq.b], w=[sel.b])
                                yield
                            stt(idxf[:], Isel[:], 128.0, Jsel[:], ALU.mult, ALU.add, [Isel.b, Jsel.b], [idxf.b])
                            yield
                            tt('dve', gw[:], best[:], best[:, :, 0:1].to_broadcast([128, 8, 16]), ALU.subtract, [best.b], [gw.b])
                            yield
                            act(gw[:], gw[:], AF.Exp, [gw.b], [gw.b])
                            yield
                            sc.op('dve', lambda: nc.vector.tensor_reduce(out=gsum[:], in_=gw[:], axis=mybir.AxisListType.X, op=ALU.add),
                                  r=[gw.b], w=[gsum.b])
                            yield
                            sc.op('dve', lambda: nc.vector.reciprocal(out=gsum[:], in_=gsum[:]), r=[gsum.b], w=[gsum.b])
                            yield
                            tt('dve', gw[:], gw[:], gsum[:].unsqueeze(2).to_broadcast([128, 8, 16]), ALU.mult, [gw.b, gsum.b], [gw.b])
                            yield
                            if s == 0 and g == 0 and t4 == 0:
                                dump("idxf0", idxf[:], idxf)
                                dump("gw0", gw[:], gw)
                            sc.op('pe', lambda: nc.tensor.transpose(pS[:, 0:128], idxf[:], ident_f[:]), r=[idxf.b, ident_f.b], w=[pS.b])
                            yield
                            cp('dve', idxT[:], pS[:, 0:128], [pS.b], [idxT.b])
                            yield
                            sc.op('pe', lambda: nc.tensor.transpose(pS[:, 128:256], gw[:].rearrange("p a b -> p (a b)"), ident_f[:]),
                                  r=[gw.b, ident_f.b], w=[pS.b])
                            yield
                            cp('act', gT[:], pS[:, 128:256], [pS.b], [gT.b])
                            yield

                        def pull(gen, k):
                            if gen is None:
                                return
                            for _ in range(k):
                                try:
                                    next(gen)
                                except StopIteration:
                                    return

                        gi = 0
                        pull(topk_gen(0), 100000)
                        for t4 in range(4):
                            tsl = slice(t4 * 128, (t4 + 1) * 128)
                            idxT = idxTs[t4]
                            gT = gTs[t4]
                            gen = topk_gen(t4 + 1) if t4 < 3 else None

                            def gather(t):
                                Gt = Gs[(gi + t) % NGB]
                                sc.dma('pool', lambda: nc.gpsimd.indirect_dma_start(
                                    out=Gt[:], out_offset=None, in_=uvb_d[:, :],
                                    in_offset=bass.IndirectOffsetOnAxis(ap=idxT[:, t:t + 1], axis=0),
                                    bounds_check=None), Gt.b, r=[idxT.b, uvb_b], w=[Gt.b])

                            for t in range(DG):
                                gather(t)
                            for i in range(128 + 3):
                                if i + DG < 128:
                                    gather(i + DG)
                                if i < 128:
                                    t = i
                                    pj = 1 + (t % 2)
                                    tg = t4 * 128 + t
                                    for k in range(8):
                                        pk = PSB[2 * pj + k // 4]
                                        mm(pk[:, (k % 4) * 128:(k % 4 + 1) * 128], h2b[:, k, tg:tg + 1].to_broadcast([128, 128]),
                                           ident_b[:, :], True, True, [h2b.b, ident_b.b], [pk.b])
                                if 0 <= i - 1 < 128:
                                    t = i - 1
                                    pj = 1 + (t % 2)
                                    Gt = Gs[(gi + t) % NGB]
                                    stt(junk2[:, :], Gt[:, 0:1024], 1.0, P2[pj][:, :], ALU.mult, ALU.mult,
                                        [Gt.b, PSB[2 * pj].b, PSB[2 * pj + 1].b], [junk2.b, actr[t % 4].b],
                                        accum_out=actr[t % 4][:, 0:1])
                                if 0 <= i - 2 < 128:
                                    t = i - 2
                                    ws = wsel[t % 3]
                                    act(actg[t % 4][:], actr[t % 4][:], AF.Gelu_apprx_tanh, [actr[t % 4].b], [actg[t % 4].b])
                                    act(actw[t % 4][:], actg[t % 4][:], AF.Identity, [actg[t % 4].b, gT.b], [actw[t % 4].b],
                                        scale=gT[:, t:t + 1])
                                    act(ws[:], crow[:, 127 - t:255 - t], AF.Identity, [crow.b, actw[t % 4].b], [ws.b],
                                        scale=actw[t % 4][:, 0:1])
                                if 0 <= i - 3 < 128:
                                    t = i - 3
                                    ws = wsel[t % 3]
                                    Gt = Gs[(gi + t) % NGB]
                                    for hf in range(2):
                                        pa_ = PSB[hf]
                                        mm(pa_[:, :], ws[:], Gt[:, 1024 + hf * 512:1024 + (hf + 1) * 512],
                                           t == 0, t == 127, [ws.b, Gt.b], [pa_.b])
                                pull(gen, 3)
                            pull(gen, 100000)
                            gi += 128
                            for hf in range(2):
                                cp('act', pacc[:, hf * 512:(hf + 1) * 512], PSB[hf][:, :], [PSB[hf].b], [pacc.b])
                            if s == 0 and g == 0 and t4 == 0:
                                dump("pacc0", pacc[:], pacc)
                            for k in range(8):
                                sc.op('pe', lambda: nc.tensor.transpose(pT7[:, 0:128], pacc[:, k * 128:(k + 1) * 128], ident_f[:]),
                                      r=[pacc.b, ident_f.b], w=[pT7.b])
                                stt(x1g[:, k, tsl], pT7[:, 0:128], modT[:, GT2 + k, s:s + 1], x1g[:, k, tsl], ALU.mult, ALU.add,
                                    [pT7.b, modT.b, x1g.b], [x1g.b])
                        sc.barrier()
                    st_(out_d[s, :, :, tok], x1g[:], x1g, [out_b[s][g]])
                sc.barrier()
        except _Stop:
            pass
        sc.barrier()
    except AssertionError:
        if dbg_stop is None:
            raise
    return nc


def _chunkT(v, n):
    return np.ascontiguousarray(np.asarray(v, np.float32).reshape(n, 128).T)


def _kmaj(w):
    K, N = w.shape
    return np.ascontiguousarray(w.reshape(K // 128, 128, N).transpose(1, 0, 2))


def prep_shared(inp):
    f = lambda k: np.asarray(inp[k][0], np.float32)
    sh = {}
    sh["w_ada"] = _kmaj(f("w_ada"))
    sh["b_adaT"] = _chunkT(f("b_ada"), 48)
    pv = np.zeros((128, 96), np.float32)
    pv[:, 0:8] = _chunkT(f("norm1_g"), 8)
    pv[:, 8:16] = _chunkT(f("norm2_g"), 8)
    cw = f("conv_w")
    for k in range(4):
        pv[:, 16 + k * 8:16 + (k + 1) * 8] = _chunkT(cw[k], 8)
    pv[:, 48:56] = _chunkT(f("conv_b"), 8)
    pv[:, 56:64] = _chunkT(f("b_rg"), 8)
    pv[:, 64:72] = _chunkT(f("b_ig"), 8)
    pv[:, 72:80] = _chunkT(f("lru_lambda"), 8)
    pv[:, 80:82] = _chunkT(f("q_a_norm_g"), 2)
    pv[:, 82:83] = _chunkT(f("kv_a_norm_g"), 1)
    perm = np.concatenate([np.arange(64), 80 + np.arange(16), 64 + np.arange(16)])
    gq = f("q_norm_g")
    gk = f("k_norm_g")
    pv[:96, 83] = gq
    pv[:96, 84] = gq[perm]
    pv[:96, 85] = gk
    pv[:96, 86] = gk[perm]
    inv_freq = (1.0 / (10000.0 ** (np.arange(0, 32, 2, dtype=np.float32) / np.float32(32)))).astype(np.float32)
    pv[64:80, 87] = inv_freq
    pv[80:96, 87] = inv_freq
    pv[64:80, 88] = -1.0
    pv[80:96, 88] = 1.0
    sh["pvec"] = pv
    w_in = f("w_in")
    wl = np.zeros((1024, 512), np.float32)
    wl[:, 0:416] = w_in[:, 0:416]
    wl[:, 416:448] = w_in[:, 384:416][:, perm[64:] - 64]
    sh["w_lat"] = _kmaj(wl)
    w_uq = f("w_uq")
    wr = np.zeros((256, 1536), np.float32)
    wr[:, :768] = w_uq
    for h in range(8):
        wr[:, 768 + h * 96:768 + (h + 1) * 96] = w_uq[:, h * 96:(h + 1) * 96][:, perm]
    sh["w_uq2"] = _kmaj(wr)
    w_ukv = f("w_ukv")
    wkn = np.zeros((128, 768), np.float32)
    wv = np.zeros((128, 512), np.float32)
    for h in range(8):
        wkn[:, h * 96:h * 96 + 64] = w_ukv[:, h * 128:h * 128 + 64]
        wv[:, h * 64:(h + 1) * 64] = w_ukv[:, h * 128 + 64:h * 128 + 128]
    sh["w_kn"] = wkn
    sh["w_v"] = wv
    em = np.zeros((128, 96), np.float32)
    em[np.arange(32), 64 + np.arange(32)] = 1.0
    sh["emat"] = em
    sh["w_lx"] = _kmaj(w_in[:, 416:1440])
    sh["w_lg"] = _kmaj(w_in[:, 1440:2464])
    sh["w_ga"] = _kmaj(w_in[:, 2464:3488])
    sh["w_gb"] = _kmaj(w_in[:, 3488:4512])
    sh["w_rg"] = np.ascontiguousarray(f("w_rg").transpose(1, 0, 2))
    sh["w_ig"] = np.ascontiguousarray(f("w_ig").transpose(1, 0, 2))
    sh["w_oa"] = np.ascontiguousarray(f("w_o_attn").reshape(8, 64, 1024).transpose(1, 0, 2))
    sh["w_ol"] = _kmaj(f("w_o_lru"))
    sh["w_out"] = _kmaj(f("w_out"))
    sh["w_q"] = _kmaj(f("w_query"))
    sk = f("sub_keys").reshape(16, 128, 128)
    sh["skT"] = np.ascontiguousarray(sk.transpose(2, 0, 1))
    sh["uv"] = np.ascontiguousarray(np.concatenate([f("expert_u"), f("expert_v")], axis=1))
    return sh


def prep_core(inp, c):
    x = np.asarray(inp["x"][c * NSEQ:(c + 1) * NSEQ], np.float32)
    xT = np.ascontiguousarray(x.reshape(NSEQ, S, 8, 128).transpose(0, 3, 2, 1))
    cc = np.asarray(inp["c"][c * NSEQ:(c + 1) * NSEQ], np.float32)
    cT = np.ascontiguousarray(cc.reshape(NSEQ, 8, 128).transpose(2, 1, 0))
    pos = np.ascontiguousarray(np.asarray(inp["positions"][c * NSEQ:(c + 1) * NSEQ], np.int32))
    return {"xT": xT, "cT": cT, "pos": pos}


_NC_CACHE = {}


def kernel(**inputs):
    if "nc" not in _NC_CACHE:
        _NC_CACHE["nc"] = build_program()
    nc = _NC_CACHE["nc"]
    sh = prep_shared(inputs)
    in_maps = []
    for c in range(NCORES):
        m = dict(sh)
        m.update(prep_core(inputs, c))
        in_maps.append(m)
    res = run_bass_kernel_spmd(nc, in_maps, core_ids=list(range(NCORES)))
    outs = []
    for c in range(NCORES):
        o = np.asarray(res.results[c]["outT"])
        outs.append(o.transpose(0, 3, 2, 1).reshape(NSEQ, S, D))
    return np.ascontiguousarray(np.concatenate(outs, axis=0).astype(np.float32))
```

```python
import math
from contextlib import ExitStack

import numpy as np
import concourse.bass as bass
import concourse.mybir as mybir
from concourse.bass_utils import run_bass_kernel_spmd

F32 = mybir.dt.float32
F32R = mybir.dt.float32r
BF16 = mybir.dt.bfloat16
I32 = mybir.dt.int32
U32 = mybir.dt.uint32
AF = mybir.ActivationFunctionType
ALU = mybir.AluOpType

NCORES = 8
NSEQ = 2
S = 2048
D = 1024
G = 512
NG = S // G
EPS = 1e-6
NEXP = 16384
PI = math.pi

DEBUG = {}


class Buf:
    def __init__(self, name):
        self.name = name
        self.w = {}
        self.r = {}
        self.dsem = None
        self.dcnt = 0
        self.psum = False


class Tile:
    def __init__(self, t, name):
        self.t = t
        self.b = Buf(name)

    def __getitem__(self, k):
        return self.t[k]


class Sched:
    def __init__(self, nc, es):
        self.nc = nc
        self.es = es
        self.eng = dict(pe=nc.tensor, act=nc.scalar, dve=nc.vector, pool=nc.gpsimd, sp=nc.sync)
        self.sem = {k: es.enter_context(nc.semaphore("S_" + k)) for k in self.eng}
        self.cnt = {k: 0 for k in self.eng}
        self.seen = {k: {} for k in self.eng}
        self.dbufs = []
        self.free = []
        self.free_sw = []
        self.nd = 0

    def _wait(self, e, tok):
        key, sem, val, src = tok
        if self.seen[e].get(key, 0) >= val:
            return
        self.eng[e].wait_ge(sem, val)
        self.seen[e][key] = val

    def _deps(self, e, r, w, is_dma, dkey=None):
        for b in r:
            for tok in b.w.values():
                if (not is_dma) and tok[3] == e and e == 'pe':
                    continue
                self._wait(e, tok)
        for b in w:
            for tok in b.w.values():
                if (not is_dma) and tok[3] == e and e == 'pe':
                    continue
                if is_dma and tok[0] == dkey:
                    continue
                self._wait(e, tok)
            for tok in b.r.values():
                if (not is_dma) and tok[3] == e and e == 'pe':
                    continue
                self._wait(e, tok)

    def _post(self, tok, r, w):
        for b in w:
            if tok[3] == 'dma':
                b.w = {tok[0]: tok}
            else:
                b.w = {tok[0]: tok}
            b.r = {}
        for b in r:
            if b not in w:
                b.r[tok[0]] = tok

    def op(self, e, fn, r=(), w=()):
        pr_ = [b for b in r if b.psum and b not in w]
        if pr_:
            w = list(w) + pr_
            r = [b for b in r if not b.psum]
        self._deps(e, r, w, False)
        inst = fn()
        self.cnt[e] += 1
        inst.then_inc(self.sem[e], 1)
        tok = ("E_" + e, self.sem[e], self.cnt[e], e)
        self.seen[e][tok[0]] = max(self.seen[e].get(tok[0], 0), 0)
        self._post(tok, r, w)
        return inst

    def dma(self, e, fn, sb, r=(), w=()):
        if sb.dsem is None:
            fl = self.free_sw if e == 'pool' else self.free
            if fl:
                sb.dsem = fl.pop()
            else:
                sem = self.es.enter_context(self.nc.semaphore("D%d" % self.nd))
                sb.dsem = ["D%d" % self.nd, sem, 0, e == 'pool']
                self.nd += 1
                self.dbufs.append(sb.dsem)
        ds = sb.dsem
        self._deps(e, r, w, True, ds[0])
        inst = fn()
        ds[2] += 16
        inst.then_inc(ds[1], 16)
        tok = (ds[0], ds[1], ds[2], 'dma')
        self._post(tok, r, w)
        return inst

    def release(self, buf):
        if buf.dsem is not None:
            (self.free_sw if buf.dsem[3] else self.free).append(buf.dsem)
            buf.dsem = None

    def barrier(self):
        for e in self.eng:
            for e2 in self.eng:
                if e2 != e and self.cnt[e2] > 0:
                    self._wait(e, ("E_" + e2, self.sem[e2], self.cnt[e2], e2))
            for ds in self.dbufs:
                if ds[2] > 0:
                    self._wait(e, (ds[0], ds[1], ds[2], 'dma'))


class _Stop(Exception):
    pass


def build_program(dbg_stop=None):
    nc = bass.Bass("TRN2", target_bir_lowering=False)

    def din(name, shape, dt=F32):
        return nc.dram_tensor(name, list(shape), dt, kind="ExternalInput").ap()

    xT_d = din("xT", [NSEQ, 128, 8, S])
    cT_d = din("cT", [128, 8, NSEQ])
    pos_d = din("pos", [NSEQ, S], I32)
    wada_d = din("w_ada", [128, 8, 6144])
    bada_d = din("b_adaT", [128, 48])
    pvec_d = din("pvec", [128, 96])
    wlat_d = din("w_lat", [128, 8, 512])
    wuq_d = din("w_uq2", [128, 2, 1536])
    wkn_d = din("w_kn", [128, 8 * 96])
    wv_d = din("w_v", [128, 512])
    emat_d = din("emat", [128, 96])
    wlx_d = din("w_lx", [128, 8, 1024])
    wlg_d = din("w_lg", [128, 8, 1024])
    wga_d = din("w_ga", [128, 8, 1024])
    wgb_d = din("w_gb", [128, 8, 1024])
    wrg_d = din("w_rg", [128, 8, 128])
    wig_d = din("w_ig", [128, 8, 128])
    woa_d = din("w_oa", [64, 8, 1024])
    wol_d = din("w_ol", [128, 8, 1024])
    wout_d = din("w_out", [128, 8, 1024])
    wq_d = din("w_q", [128, 8, 2048])
    skT_d = din("skT", [128, 16, 128])
    uv_d = din("uv", [NEXP, 2048])
    out_d = nc.dram_tensor("outT", [NSEQ, 128, 8, S], F32, kind="ExternalOutput").ap()
    hT_d = nc.dram_tensor("hT_s", [NSEQ, 128, 8, S], BF16, kind="Internal").ap()
    ya_d = nc.dram_tensor("ya_s", [NSEQ, 128, 8, S], BF16, kind="Internal").ap()
    uvb_d = nc.dram_tensor("uvb_s", [NEXP, 2048], BF16, kind="Internal").ap()
    dbg_d = {k: nc.dram_tensor(k, list(v[0]), v[1], kind="ExternalOutput").ap() for k, v in DEBUG.items()}

    es = ExitStack()
    try:
      with es:
        sc = Sched(nc, es)

        uid = {'i': 0}

        def T(name, shape, dt, st=es):
            uid['i'] += 1
            nm = "t%d_%s" % (uid['i'], name)
            tl = Tile(st.enter_context(nc.sbuf_tensor(nm, list(shape), dt)), nm)
            if st is not es:
                st.callback(sc.release, tl.b)
            return tl

        class BankView:
            def __init__(self, t, off, name):
                self.t = t
                self.off = off
                self.b = Buf(name)

            def __getitem__(self, key):
                ps_, cs = key
                a = (cs.start or 0) + self.off
                b_ = (cs.stop if cs.stop is not None else 512) + self.off
                return self.t[ps_, slice(a, b_, cs.step)]

        P2 = [es.enter_context(nc.psum_tensor("pp%d" % i, [128, 1024], F32)) for i in range(4)]
        PSB = [BankView(P2[i // 2], (i % 2) * 512, "ps%d" % i) for i in range(8)]
        for p_ in PSB:
            p_.b.psum = True
        rot = {'i': 0}

        def PS(lo=0, hi=8):
            i = rot['i']
            n = hi - lo
            b = PSB[lo + (i % n)]
            rot['i'] = i + 1
            return b

        hT_b = [[Buf("hTd%d_%d" % (s, g)) for g in range(NG)] for s in range(NSEQ)]
        ya_b = [[Buf("yad%d_%d" % (s, g)) for g in range(NG)] for s in range(NSEQ)]
        out_b = [[Buf("outd%d_%d" % (s, g)) for g in range(NG)] for s in range(NSEQ)]
        dbg_b = {k: Buf("dbg_" + k) for k in DEBUG}

        def mm(out, lhsT, rhs, start, stop, r, w):
            return sc.op('pe', lambda: nc.tensor.matmul(out, lhsT, rhs, start=start, stop=stop), r=r, w=w)

        def act(out, in_, func, r, w, bias=None, scale=None):
            kw = {}
            if bias is not None:
                kw['bias'] = bias
            if scale is not None:
                kw['scale'] = scale
            return sc.op('act', lambda: nc.scalar.activation(out=out, in_=in_, func=func, **kw), r=r, w=w)

        def tt(e, out, in0, in1, op, r, w):
            eng = nc.vector if e == 'dve' else nc.gpsimd
            return sc.op(e, lambda: eng.tensor_tensor(out=out, in0=in0, in1=in1, op=op), r=r, w=w)

        def ts(e, out, in0, s1, s2, op0, op1, r, w):
            eng = nc.vector if e == 'dve' else nc.gpsimd
            if op1 is None:
                return sc.op(e, lambda: eng.tensor_scalar(out=out, in0=in0, scalar1=s1, scalar2=None, op0=op0), r=r, w=w)
            return sc.op(e, lambda: eng.tensor_scalar(out=out, in0=in0, scalar1=s1, scalar2=s2, op0=op0, op1=op1), r=r, w=w)

        def stt(out, in0, scalar, in1, op0, op1, r, w, accum_out=None):
            if accum_out is not None:
                return sc.op('dve', lambda: nc.vector.scalar_tensor_tensor(out=out, in0=in0, scalar=scalar, in1=in1, op0=op0, op1=op1, accum_out=accum_out), r=r, w=w)
            return sc.op('dve', lambda: nc.vector.scalar_tensor_tensor(out=out, in0=in0, scalar=scalar, in1=in1, op0=op0, op1=op1), r=r, w=w)

        def cp(e, out, in_, r, w):
            if e == 'act':
                return sc.op('act', lambda: nc.scalar.copy(out=out, in_=in_), r=r, w=w)
            eng = nc.vector if e == 'dve' else nc.gpsimd
            return sc.op(e, lambda: eng.tensor_copy(out=out, in_=in_), r=r, w=w)

        def ld(out, in_, tile, r=(), eng='sp'):
            e = {'sp': nc.sync, 'pool': nc.gpsimd, 'act': nc.scalar}[eng]
            return sc.dma(eng, lambda: e.dma_start(out=out, in_=in_), tile.b, r=r, w=[tile.b])

        def st_(out, in_, tile, wb, eng='sp'):
            e = {'sp': nc.sync, 'pool': nc.gpsimd, 'act': nc.scalar}[eng]
            return sc.dma(eng, lambda: e.dma_start(out=out, in_=in_), tile.b, r=[tile.b], w=wb)

        def dump(name, in_ap, tile, out_ap=None):
            if name in DEBUG:
                o = dbg_d[name] if out_ap is None else out_ap
                st_(o, in_ap, tile, [dbg_b[name]])

        pvec = T("pvec", [128, 96], F32)
        ld(pvec[:], pvec_d[:, :], pvec)
        C_N1, C_N2, C_CW, C_CB, C_BRG, C_BIG, C_LAM, C_QA, C_KVA = 0, 8, 16, 48, 56, 64, 72, 80, 82
        C_GQ, C_GQR, C_GK, C_GKR, C_INVF, C_SGN = 83, 84, 85, 86, 87, 88
        ones_f = T("ones_f", [128, 128], F32)
        sc.op('pool', lambda: nc.gpsimd.memset(ones_f[:], 1.0), w=[ones_f.b])
        onesD = T("onesD", [128, 128], F32)
        sc.op('pool', lambda: nc.gpsimd.memset(onesD[:], 1.0 / D), w=[onesD.b])
        ones256 = T("ones256", [128, 128], F32)
        sc.op('pool', lambda: nc.gpsimd.memset(ones256[:], 1.0 / 256), w=[ones256.b])
        ones128 = T("ones128", [128, 128], F32)
        sc.op('pool', lambda: nc.gpsimd.memset(ones128[:], 1.0 / 128), w=[ones128.b])
        ones96 = T("ones96", [128, 128], F32)
        sc.op('pool', lambda: nc.gpsimd.memset(ones96[:], 1.0 / 96), w=[ones96.b])
        epsb = T("epsb", [128, 1], F32)
        sc.op('pool', lambda: nc.gpsimd.memset(epsb[:], EPS), w=[epsb.b])
        oneb = T("oneb", [128, 1], F32)
        sc.op('pool', lambda: nc.gpsimd.memset(oneb[:], 1.0), w=[oneb.b])
        iot = T("iot", [128, 128], F32)
        sc.op('pool', lambda: nc.gpsimd.iota(iot[:], pattern=[[1, 128]], base=0, channel_multiplier=-1,
                                             allow_small_or_imprecise_dtypes=True), w=[iot.b])
        ident_f = T("ident_f", [128, 128], F32)
        ts('dve', ident_f[:], iot[:], 0.0, None, ALU.is_equal, None, [iot.b], [ident_f.b])
        ident_b = T("ident_b", [128, 128], BF16)
        cp('dve', ident_b[:], ident_f[:], [ident_f.b], [ident_b.b])
        crow = T("crow", [128, 255], F32)
        sc.op('pool', lambda: nc.gpsimd.iota(crow[:], pattern=[[1, 255]], base=-127, channel_multiplier=0,
                                             allow_small_or_imprecise_dtypes=True), w=[crow.b])
        ts('dve', crow[:], crow[:], 0.0, None, ALU.is_equal, None, [crow.b], [crow.b])
        iota256 = T("iota256", [128, 256], F32)
        sc.op('pool', lambda: nc.gpsimd.iota(iota256[:], pattern=[[1, 256]], base=0, channel_multiplier=0,
                                             allow_small_or_imprecise_dtypes=True), w=[iota256.b])

        sc.barrier()
        stg = [T("stg%d" % i, [128, 2048], F32) for i in range(2)]
        stg_i = {'i': 0}

        def load_w_bf16(dst_tile, dst_ap, src_ap, shape):
            p, a, b = shape
            cb = max(1, 2048 // a)
            for c0 in range(0, b, cb):
                c1 = min(b, c0 + cb)
                sg = stg[stg_i['i'] % 2]
                stg_i['i'] += 1
                sv = sg[0:p, 0:a * (c1 - c0)].rearrange("p (a b) -> p a b", a=a)
                ld(sv, src_ap[:, :, c0:c1], sg)
                cp('dve', dst_ap[:, :, c0:c1], sv, [sg.b], [dst_tile.b])

        modT = T("modT", [128, 48, NSEQ], F32)
        A1 = T("A1", [128, 8, NSEQ], F32)
        A2 = T("A2", [128, 8, NSEQ], F32)
        kap = T("kap", [128, 8], F32)
        kap2 = T("kap2", [128, 8], F32)
        gqs = T("gqs", [128, 1], F32)
        gks = T("gks", [128, 1], F32)
        with ExitStack() as st:
            cTt = T("cTt", [128, 8, NSEQ], F32, st)
            csT = T("csT", [128, 8, NSEQ], F32, st)
            badaT = T("badaT", [128, 48], F32, st)
            ld(cTt[:], cT_d[:, :, :], cTt)
            ld(badaT[:], bada_d[:, :], badaT)
            act(csT[:], cTt[:], AF.Silu, [cTt.b], [csT.b])
            pm = PSB[0]
            for cc in range(24):
                sg0 = stg[cc % 2]
                sg = sg0[:, :].rearrange("p (a b) -> p a b", a=8)
                ld(sg, wada_d[:, :, cc * 256:(cc + 1) * 256], sg0)
                for j in range(2):
                    oc = cc * 2 + j
                    for k in range(8):
                        mm(pm[:, oc * 2:oc * 2 + 2], sg[:, k, j * 128:(j + 1) * 128], csT[:, k, :],
                           k == 0, k == 7, [sg0.b, csT.b], [pm.b])
            for b in range(NSEQ):
                tt('dve', modT[:, :, b], pm[:, b:96:2], badaT[:, :], ALU.add, [pm.b, badaT.b], [modT.b])
            for b in range(NSEQ):
                stt(A1[:, :, b], modT[:, 8:16, b], 1.0, pvec[:, C_N1:C_N1 + 8], ALU.add, ALU.mult,
                    [modT.b, pvec.b], [A1.b])
                stt(A2[:, :, b], modT[:, 32:40, b], 1.0, pvec[:, C_N2:C_N2 + 8], ALU.add, ALU.mult,
                    [modT.b, pvec.b], [A2.b])
            tk = T("tk", [128, 8], F32, st)
            act(tk[:], pvec[:, C_LAM:C_LAM + 8], AF.Exp, [pvec.b], [tk.b], scale=-1.0)
            act(tk[:], tk[:], AF.Ln, [tk.b], [tk.b], bias=oneb[:, 0:1])
            ts('dve', kap[:], tk[:], -8.0, None, ALU.mult, None, [tk.b], [kap.b])
            ts('dve', kap2[:], tk[:], -16.0, None, ALU.mult, None, [tk.b], [kap2.b])
            tt('dve', gqs[:], pvec[:, C_GQR:C_GQR + 1], pvec[:, C_SGN:C_SGN + 1], ALU.mult, [pvec.b], [gqs.b])
            tt('dve', gks[:], pvec[:, C_GKR:C_GKR + 1], pvec[:, C_SGN:C_SGN + 1], ALU.mult, [pvec.b], [gks.b])
            dump("modT", modT[:], modT)
            sc.barrier()

        uvb_b = Buf("uvb")
        with ExitStack() as st:
            cin = [T("cin%d" % i, [128, 4, 2048], F32, st) for i in range(2)]
            co_d = [T("cod%d" % i, [128, 3, 2048], BF16, st) for i in range(2)]
            co_a = [T("coa%d" % i, [128, 1, 2048], BF16, st) for i in range(2)]
            uvv = uv_d.rearrange("(c p j) d -> c p j d", p=128, j=4)
            uvbv = uvb_d.rearrange("(c p j) d -> c p j d", p=128, j=4)
            for c in range(NEXP // 512):
                ci = cin[c % 2]
                ld(ci[:], uvv[c], ci)
                cp('dve', co_d[c % 2][:], ci[:, 0:3, :], [ci.b], [co_d[c % 2].b])
                cp('act', co_a[c % 2][:], ci[:, 3:4, :], [ci.b], [co_a[c % 2].b])
                st_(uvbv[c][:, 0:3, :], co_d[c % 2][:], co_d[c % 2], [uvb_b])
                st_(uvbv[c][:, 3:4, :], co_a[c % 2][:], co_a[c % 2], [uvb_b])
            sc.barrier()
        if dbg_stop == 'S0':
            sc.barrier()
            return nc
        SH1, GT1, SH2, GT2 = 0, 16, 24, 40

        def rms_modulate(xb, hTb, A, shoff, s, st, tag):
            sq = T("sq" + tag, [128, 8, G], F32, st)
            rstd = T("rstd" + tag, [128, G], F32, st)
            act(sq[:], xb[:], AF.Square, [xb.b], [sq.b])
            pb = PS()
            for k in range(8):
                mm(pb[:, :], onesD[:, :], sq[:, k, :], k == 0, k == 7, [onesD.b, sq.b], [pb.b])
            act(rstd[:], pb[:, :], AF.Ln, [pb.b], [rstd.b], bias=epsb[:, 0:1])
            act(rstd[:], rstd[:], AF.Exp, [rstd.b], [rstd.b], scale=-0.5)
            for k in range(8):
                stt(sq[:, k, :], xb[:, k, :], A[:, k, s:s + 1], rstd[:], ALU.mult, ALU.mult,
                    [xb.b, A.b, rstd.b], [sq.b])
                ts('pool', hTb[:, k, :], sq[:, k, :], modT[:, shoff + k, s:s + 1], None, ALU.add, None,
                   [sq.b, modT.b], [hTb.b])

        def chk(name):
            if dbg_stop == name:
                raise _Stop()

        try:
         for s in range(NSEQ):
            with ExitStack() as sa:
                qT = T("qT", [96, 8, S], BF16, sa)
                kT = T("kT", [96, 8, S], BF16, sa)
                Vb = T("Vb", [128, 16, 8, 65], BF16, sa)
                with ExitStack() as st:
                    cosT = T("cosT", [96, G], F32, st)
                    sinT = T("sinT", [96, G], F32, st)
                    posi = T("posi", [96, G], I32, st)
                    ang = T("ang", [96, G], F32, st)
                    nf = T("nf", [96, G], F32, st)
                    ni = T("ni", [96, G], I32, st)
                    wlat = T("wlat", [128, 8, 512], BF16, st)
                    wuq = T("wuq", [128, 2, 1536], BF16, st)
                    wkn = T("wkn", [128, 768], BF16, st)
                    wv = T("wv", [128, 512], BF16, st)
                    emat = T("emat", [128, 96], BF16, st)
                    load_w_bf16(wlat, wlat[:], wlat_d[:, :, :], (128, 8, 512))
                    load_w_bf16(wuq, wuq[:], wuq_d[:, :, :], (128, 2, 1536))
                    load_w_bf16(wkn, wkn[:].rearrange("p (a b) -> p a b", a=1), wkn_d[:, :].rearrange("p (a b) -> p a b", a=1), (128, 1, 768))
                    load_w_bf16(wv, wv[:].rearrange("p (a b) -> p a b", a=1), wv_d[:, :].rearrange("p (a b) -> p a b", a=1), (128, 1, 512))
                    load_w_bf16(emat, emat[:].rearrange("p (a b) -> p a b", a=1), emat_d[:, :].rearrange("p (a b) -> p a b", a=1), (128, 1, 96))
                    sc.op('pool', lambda: nc.gpsimd.memset(Vb[:], 1.0), w=[Vb.b])

                    def rope_tables(tok):
                        ld(posi[:], pos_d[s, tok].partition_broadcast(96), posi)
                        cp('dve', ang[:], posi[:], [posi.b], [ang.b])
                        ts('dve', ang[:], ang[:], pvec[0:96, C_INVF:C_INVF + 1], None, ALU.mult, None, [ang.b, pvec.b], [ang.b])
                        ts('dve', nf[:], ang[:], 1.0 / (2 * PI), None, ALU.mult, None, [ang.b], [nf.b])
                        cp('dve', ni[:], nf[:], [nf.b], [ni.b])
                        cp('dve', nf[:], ni[:], [ni.b], [nf.b])
                        C1 = 6.28125
                        C2 = 2 * PI - C1
                        stt(ang[:], nf[:], -C1, ang[:], ALU.mult, ALU.add, [nf.b, ang.b], [ang.b])
                        stt(ang[:], nf[:], -C2, ang[:], ALU.mult, ALU.add, [nf.b, ang.b], [ang.b])

                        def wrap_sin(dst, shift):
                            pf = posi[:].bitcast(F32)
                            ts('dve', nf[:], ang[:], shift, None, ALU.add, None, [ang.b], [nf.b])
                            for _ in range(2):
                                ts('dve', pf, nf[:], PI, -2 * PI, ALU.is_gt, ALU.mult, [nf.b], [posi.b])
                                tt('dve', nf[:], nf[:], pf, ALU.add, [nf.b, posi.b], [nf.b])
                                ts('dve', pf, nf[:], -PI, 2 * PI, ALU.is_lt, ALU.mult, [nf.b], [posi.b])
                                tt('dve', nf[:], nf[:], pf, ALU.add, [nf.b, posi.b], [nf.b])
                            ts('dve', nf[:], nf[:], 3.141592, -3.141592, ALU.min, ALU.max, [nf.b], [nf.b])
                            act(dst[:], nf[:], AF.Sin, [nf.b], [dst.b])
                        wrap_sin(sinT, 0.0)
                        wrap_sin(cosT, PI / 2)

                    xb = T("xb", [128, 8, G], F32, st)
                    hTb = T("hTb", [128, 8, G], BF16, st)
                    sqq = T("sqq", [128, 2, G], F32, st)
                    rq = T("rq", [128, G], F32, st)
                    cqn = T("cqn", [128, 2, G], BF16, st)
                    ckvn = T("ckvn", [128, G], BF16, st)
                    krb = T("krb", [128, G], BF16, st)
                    krrb = T("krrb", [128, G], BF16, st)
                    sc.op('pool', lambda: nc.gpsimd.memset(krb[:], 0.0), w=[krb.b])
                    sc.op('pool', lambda: nc.gpsimd.memset(krrb[:], 0.0), w=[krrb.b])
                    m2k = T("m2k", [96, G], F32, st)
                    sqh = T("sqh", [96, G], F32, st)
                    rh = T("rh", [96, G], F32, st)
                    m1 = T("m1", [96, G], F32, st)
                    m2 = T("m2", [96, G], F32, st)
                    for g in range(NG):
                        tok = slice(g * G, (g + 1) * G)
                        rope_tables(tok)
                        if g == 0 and s == 0:
                            dump("cosT", cosT[:], cosT)
                            dump("sinT", sinT[:], sinT)
                        chk('A1')
                        ld(xb[:], xT_d[s, :, :, tok], xb)
                        if g == 0:
                            sqA = T("sqA", [128, 8, G], F32, st)
                            rstdA = T("rstdA", [128, G], F32, st)
                        act(sqA[:], xb[:], AF.Square, [xb.b], [sqA.b])
                        pb = PS()
                        for k in range(8):
                            mm(pb[:, :], onesD[:, :], sqA[:, k, :], k == 0, k == 7, [onesD.b, sqA.b], [pb.b])
                        act(rstdA[:], pb[:, :], AF.Ln, [pb.b], [rstdA.b], bias=epsb[:, 0:1])
                        act(rstdA[:], rstdA[:], AF.Exp, [rstdA.b], [rstdA.b], scale=-0.5)
                        for k in range(8):
                            stt(sqA[:, k, :], xb[:, k, :], A1[:, k, s:s + 1], rstdA[:], ALU.mult, ALU.mult,
                                [xb.b, A1.b, rstdA.b], [sqA.b])
                            act(hTb[:, k, :], sqA[:, k, :], AF.Identity, [sqA.b, modT.b], [hTb.b], bias=modT[:, SH1 + k, s:s + 1])
                        st_(hT_d[s, :, :, tok], hTb[:], hTb, [hT_b[s][g]])
                        if g == 0 and s == 0:
                            dump("hT0", hTb[:], hTb)
                        chk('A2')
                        pcq = [PS(), PS()]
                        for j in range(2):
                            for k in range(8):
                                mm(pcq[j][:, :], wlat[:, k, j * 128:(j + 1) * 128], hTb[:, k, :], k == 0, k == 7,
                                   [wlat.b, hTb.b], [pcq[j].b])
                        pkv = PS()
                        for k in range(8):
                            mm(pkv[:, :], wlat[:, k, 256:384], hTb[:, k, :], k == 0, k == 7, [wlat.b, hTb.b], [pkv.b])
                        pkr = PS()
                        for k in range(8):
                            mm(pkr[0:32, :], wlat[:, k, 384:416], hTb[:, k, :], k == 0, k == 7, [wlat.b, hTb.b], [pkr.b])
                        pkrr = PS()
                        for k in range(8):
                            mm(pkrr[0:32, :], wlat[:, k, 416:448], hTb[:, k, :], k == 0, k == 7, [wlat.b, hTb.b], [pkrr.b])
                        for j in range(2):
                            act(sqq[:, j, :], pcq[j][:, :], AF.Square, [pcq[j].b], [sqq.b])
                        pn = PS()
                        for j in range(2):
                            mm(pn[:, :], ones256[:, :], sqq[:, j, :], j == 0, j == 1, [ones256.b, sqq.b], [pn.b])
                        act(rq[:], pn[:, :], AF.Ln, [pn.b], [rq.b], bias=epsb[:, 0:1])
                        act(rq[:], rq[:], AF.Exp, [rq.b], [rq.b], scale=-0.5)
                        for j in range(2):
                            stt(cqn[:, j, :], pcq[j][:, :], pvec[:, C_QA + j:C_QA + j + 1], rq[:], ALU.mult, ALU.mult,
                                [pcq[j].b, pvec.b, rq.b], [cqn.b])
                        act(sqq[:, 0, :], pkv[:, :], AF.Square, [pkv.b], [sqq.b])
                        pn = PS()
                        mm(pn[:, :], ones128[:, :], sqq[:, 0, :], True, True, [ones128.b, sqq.b], [pn.b])
                        act(rq[:], pn[:, :], AF.Ln, [pn.b], [rq.b], bias=epsb[:, 0:1])
                        act(rq[:], rq[:], AF.Exp, [rq.b], [rq.b], scale=-0.5)
                        stt(ckvn[:], pkv[:, :], pvec[:, C_KVA:C_KVA + 1], rq[:], ALU.mult, ALU.mult,
                            [pkv.b, pvec.b, rq.b], [ckvn.b])
                        cp('act', krb[0:32, :], pkr[0:32, :], [pkr.b], [krb.b])
                        cp('act', krrb[0:32, :], pkrr[0:32, :], [pkrr.b], [krrb.b])
                        chk('A3')
                        pr = PS()
                        mm(pr[0:96, :], emat[:, :], krrb[:, :], True, True, [emat.b, krrb.b], [pr.b])
                        stt(m2k[:], pr[0:96, :], gks[0:96, 0:1], sinT[:, :], ALU.mult, ALU.mult,
                            [pr.b, gks.b, sinT.b], [m2k.b])
                        chk('A3a')
                        for hh in range(16):
                            isq = hh < 8
                            h = hh % 8
                            pa = PS()
                            if isq:
                                for j in range(2):
                                    mm(pa[0:96, :], wuq[:, j, h * 96:(h + 1) * 96], cqn[:, j, :], j == 0, j == 1,
                                       [wuq.b, cqn.b], [pa.b])
                                pb2 = PS()
                                for j in range(2):
                                    mm(pb2[0:96, :], wuq[:, j, 768 + h * 96:768 + (h + 1) * 96], cqn[:, j, :], j == 0, j == 1,
                                       [wuq.b, cqn.b], [pb2.b])
                            else:
                                mm(pa[0:96, :], wkn[:, h * 96:(h + 1) * 96], ckvn[:, :], True, False, [wkn.b, ckvn.b], [pa.b])
                                mm(pa[0:96, :], emat[:, :], krb[:, :], False, True, [emat.b, krb.b], [pa.b])
                            chk('A3b')
                            act(sqh[:], pa[0:96, :], AF.Square, [pa.b], [sqh.b])
                            pc = PS()
                            mm(pc[0:96, :], ones96[0:96, 0:96], sqh[:, :], True, True, [ones96.b, sqh.b], [pc.b])
                            act(rh[:], pc[0:96, :], AF.Ln, [pc.b], [rh.b], bias=epsb[0:96, 0:1])
                            act(rh[:], rh[:], AF.Exp, [rh.b], [rh.b], scale=-0.5)
                            chk('A3c')
                            gcol = C_GQ if isq else C_GK
                            stt(m1[:], pa[0:96, :], pvec[0:96, gcol:gcol + 1], cosT[:, :], ALU.mult, ALU.mult,
                                [pa.b, pvec.b, cosT.b], [m1.b])
                            chk('A3d')
                            if isq:
                                stt(m2[:], pb2[0:96, :], gqs[0:96, 0:1], sinT[:, :], ALU.mult, ALU.mult,
                                    [pb2.b, gqs.b, sinT.b], [m2.b])
                                tt('dve', m1[:], m1[:], m2[:], ALU.add, [m1.b, m2.b], [m1.b])
                                tt('dve', qT[:, h, tok], m1[:], rh[:], ALU.mult, [m1.b, rh.b], [qT.b])
                            else:
                                tt('dve', m1[:], m1[:], m2k[:], ALU.add, [m1.b, m2k.b], [m1.b])
                                tt('dve', kT[:, h, tok], m1[:], rh[:], ALU.mult, [m1.b, rh.b], [kT.b])
                        chk('A4')
                        for t4 in range(4):
                            pv = PS()
                            mm(pv[:, :], ckvn[:, t4 * 128:(t4 + 1) * 128], wv[:, :], True, True, [ckvn.b, wv.b], [pv.b])
                            cp('act', Vb[:, g * 4 + t4, :, 0:64], pv[:, :].rearrange("p (h d) -> p h d", h=8),
                               [pv.b], [Vb.b])
                    if s == 0:
                        dump("qT0", qT[:], qT)
                        dump("kT0", kT[:], kT)
                        dump("Vb0", Vb[:], Vb)
                    sc.barrier()
                if dbg_stop == 'A':
                    break
                attnT = T("attnT", [64, 8, S], BF16, sa)
                with ExitStack() as st:
                    pTs = [T("pT%d" % i, [128, G], BF16, st) for i in range(3)]
                    accS = T("accS", [65, S], F32, st)
                    rrow = T("rrow", [65, S], F32, st)
                    scale = 96 ** -0.5
                    acc = PSB[0:4]
                    steps = []
                    for h in range(8):
                        for j in range(16):
                            for qg in range(j // 4, 4):
                                steps.append((h, j, qg))

                    def qk(step):
                        h, j, qg = step
                        q0 = max(qg * G, j * 128)
                        q1 = (qg + 1) * G
                        ps_ = PS(4, 8)
                        mm(ps_[:, 0:q1 - q0], kT[:, h, j * 128:(j + 1) * 128], qT[:, h, q0:q1], True, True,
                           [kT.b, qT.b], [ps_.b])
                        return ps_

                    def epilogue(h):
                        for qg in range(4):
                            cp('act', accS[:, qg * G:(qg + 1) * G], acc[qg][0:65, :], [acc[qg].b], [accS.b])
                        act(rrow[64:65, :], accS[64:65, :], AF.Ln, [accS.b], [rrow.b])
                        act(rrow[64:65, :], rrow[64:65, :], AF.Exp, [rrow.b], [rrow.b], scale=-1.0)
                        for qg in range(4):
                            pbc = acc[qg]
                            mm(pbc[0:64, :], ones_f[64:65, 0:64], rrow[64:65, qg * G:(qg + 1) * G], True, True,
                               [ones_f.b, rrow.b], [pbc.b])
                            tt('dve', attnT[:, h, qg * G:(qg + 1) * G], accS[0:64, qg * G:(qg + 1) * G], pbc[0:64, :],
                               ALU.mult, [accS.b, pbc.b], [attnT.b])

                    cur = qk(steps[0])
                    for n, (h, j, qg) in enumerate(steps):
                        nxt = qk(steps[n + 1]) if n + 1 < len(steps) else None
                        q0 = max(qg * G, j * 128)
                        q1 = (qg + 1) * G
                        N = q1 - q0
                        pT = pTs[n % 3]
                        act(pT[:, 0:N], cur[:, 0:N], AF.Exp, [cur.b], [pT.b], scale=scale)
                        if qg == j // 4:
                            sc.op('pool', lambda: nc.gpsimd.memset(pT[64:128, 0:64], 0.0), w=[pT.b])
                        c0 = q0 - qg * G
                        lastj = min(15, 4 * qg + 3)
                        mm(acc[qg][0:65, c0:c0 + N], Vb[:, j, h, :], pT[:, 0:N], j == 0, j == lastj,
                           [Vb.b, pT.b], [acc[qg].b])
                        cur = nxt
                        if n + 1 == len(steps) or steps[n + 1][0] != h:
                            epilogue(h)
                    if s == 0:
                        dump("attnT0", attnT[:], attnT)
                    sc.barrier()
                if dbg_stop == 'B':
                    break
                with ExitStack() as st:
                    wga = T("wga", [128, 8, 1024], BF16, st)
                    woa = T("woa", [64, 8, 1024], BF16, st)
                    for cc in range(2):
                        load_w_bf16(wga, wga[:, :, cc * 512:(cc + 1) * 512], wga_d[:, :, cc * 512:(cc + 1) * 512], (128, 8, 512))
                        load_w_bf16(woa, woa[:, :, cc * 512:(cc + 1) * 512], woa_d[:, :, cc * 512:(cc + 1) * 512], (64, 8, 512))
                    hTb = T("hTbC", [128, 8, G], BF16, st)
                    gaT = T("gaT", [128, G], F32, st)
                    yaT = T("yaT", [128, 8, G], BF16, st)
                    for g in range(NG):
                        tok = slice(g * G, (g + 1) * G)
                        ld(hTb[:], hT_d[s, :, :, tok], hTb, r=[hT_b[s][g]])
                        for oc in range(8):
                            p1 = PS()
                            for k in range(8):
                                mm(p1[:, :], wga[:, k, oc * 128:(oc + 1) * 128], hTb[:, k, :], k == 0, k == 7,
                                   [wga.b, hTb.b], [p1.b])
                            act(gaT[:], p1[:, :], AF.Sigmoid, [p1.b], [gaT.b])
                            p2 = PS()
                            for h in range(8):
                                mm(p2[:, :], woa[:, h, oc * 128:(oc + 1) * 128], attnT[:, h, tok], h == 0, h == 7,
                                   [woa.b, attnT.b], [p2.b])
                            tt('dve', yaT[:, oc, :], p2[:, :], gaT[:], ALU.mult, [p2.b, gaT.b], [yaT.b])
                        st_(ya_d[s, :, :, tok], yaT[:], yaT, [ya_b[s][g]])
                    sc.barrier()
            if dbg_stop in ('A', 'B', 'C'):
                break
            with ExitStack() as st:
                wlx = T("wlx", [128, 8, 1024], BF16, st)
                wlg = T("wlg", [128, 8, 1024], BF16, st)
                wgb = T("wgb", [128, 8, 1024], BF16, st)
                wol = T("wol", [128, 8, 1024], BF16, st)
                wout = T("wout", [128, 8, 1024], BF16, st)
                wrg = T("wrg", [128, 8, 128], BF16, st)
                wig = T("wig", [128, 8, 128], BF16, st)
                for cc in range(2):
                    cs_ = slice(cc * 512, (cc + 1) * 512)
                    load_w_bf16(wlx, wlx[:, :, cs_], wlx_d[:, :, cs_], (128, 8, 512))
                    load_w_bf16(wlg, wlg[:, :, cs_], wlg_d[:, :, cs_], (128, 8, 512))
                    load_w_bf16(wgb, wgb[:, :, cs_], wgb_d[:, :, cs_], (128, 8, 512))
                    load_w_bf16(wol, wol[:, :, cs_], wol_d[:, :, cs_], (128, 8, 512))
                    load_w_bf16(wout, wout[:, :, cs_], wout_d[:, :, cs_], (128, 8, 512))
                load_w_bf16(wrg, wrg[:], wrg_d[:, :, :], (128, 8, 128))
                load_w_bf16(wig, wig[:], wig_d[:, :, :], (128, 8, 128))
                hTb = T("hTbD", [128, 8, G], BF16, st)
                xb = T("xbD", [128, 8, G], F32, st)
                yaT = T("yaTD", [128, 8, G], BF16, st)
                xe = T("xe", [128, 8, G + 4], F32, st)
                hprev = T("hprev", [128, 8], F32, st)
                xcs = [T("xc%d" % i, [128, G], F32, st) for i in range(2)]
                xcbs = [T("xcb%d" % i, [128, G], BF16, st) for i in range(2)]
                rr = T("rr", [128, G], F32, st)
                ii = T("ii", [128, G], F32, st)
                aa = T("aa", [128, G], F32, st)
                a2 = T("a2", [128, G], F32, st)
                bb = T("bb", [128, G], F32, st)
                hs = T("hs", [128, G], F32, st)
                gls = [T("gl%d" % i, [128, G], F32, st) for i in range(2)]
                recT = T("recT", [128, 8, G], BF16, st)
                gb = T("gb", [128, G], F32, st)
                t1 = T("t1", [128, G], F32, st)
                sc.op('pool', lambda: nc.gpsimd.memset(xe[:], 0.0), w=[xe.b])
                sc.op('pool', lambda: nc.gpsimd.memset(hprev[:], 0.0), w=[hprev.b])
                for g in range(NG):
                    tok = slice(g * G, (g + 1) * G)
                    ld(hTb[:], hT_d[s, :, :, tok], hTb, r=[hT_b[s][g]])
                    ld(xb[:], xT_d[s, :, :, tok], xb)
                    ld(yaT[:], ya_d[s, :, :, tok], yaT, r=[ya_b[s][g]])
                    def lru_p1(n):
                        xc_, xcb_, gl_ = xcs[n % 2], xcbs[n % 2], gls[n % 2]
                        px = PS()
                        for k in range(8):
                            mm(px[:, :], wlx[:, k, n * 128:(n + 1) * 128], hTb[:, k, :], k == 0, k == 7,
                               [wlx.b, hTb.b], [px.b])
                        cp('act', xe[:, n, 4:G + 4], px[:, :], [px.b], [xe.b])
                        ts('dve', xc_[:], xe[:, n, 4:G + 4], pvec[:, C_CW + 3 * 8 + n:C_CW + 3 * 8 + n + 1],
                           pvec[:, C_CB + n:C_CB + n + 1], ALU.mult, ALU.add, [xe.b, pvec.b], [xc_.b])
                        for kk in range(3):
                            stt(xc_[:], xe[:, n, 1 + kk:1 + kk + G], pvec[:, C_CW + kk * 8 + n:C_CW + kk * 8 + n + 1], xc_[:],
                                ALU.mult, ALU.add, [xe.b, pvec.b, xc_.b], [xc_.b])
                        cp('pool', xe[:, n, 1:4], xe[:, n, G + 1:G + 4], [xe.b], [xe.b])
                        cp('act', xcb_[:], xc_[:], [xc_.b], [xcb_.b])
                        pg = PS()
                        for k in range(8):
                            mm(pg[:, :], wlg[:, k, n * 128:(n + 1) * 128], hTb[:, k, :], k == 0, k == 7,
                               [wlg.b, hTb.b], [pg.b])
                        act(gl_[:], pg[:, :], AF.Gelu_apprx_tanh, [pg.b], [gl_.b])

                    def lru_p2(n):
                        xc_, xcb_, gl_ = xcs[n % 2], xcbs[n % 2], gls[n % 2]
                        pr_ = PS()
                        mm(pr_[:, :], wrg[:, n, :], xcb_[:], True, True, [wrg.b, xcb_.b], [pr_.b])
                        pi_ = PS()
                        mm(pi_[:, :], wig[:, n, :], xcb_[:], True, True, [wig.b, xcb_.b], [pi_.b])
                        act(rr[:], pr_[:, :], AF.Sigmoid, [pr_.b, pvec.b], [rr.b], bias=pvec[:, C_BRG + n:C_BRG + n + 1])
                        act(ii[:], pi_[:, :], AF.Sigmoid, [pi_.b, pvec.b], [ii.b], bias=pvec[:, C_BIG + n:C_BIG + n + 1])
                        act(aa[:], rr[:], AF.Exp, [rr.b, kap.b], [aa.b], scale=kap[:, n:n + 1])
                        act(a2[:], rr[:], AF.Exp, [rr.b, kap2.b], [a2.b], scale=kap2[:, n:n + 1])
                        act(a2[:], a2[:], AF.Sqrt, [a2.b], [a2.b], scale=-1.0, bias=oneb[:, 0:1])
                        tt('dve', bb[:], a2[:], ii[:], ALU.mult, [a2.b, ii.b], [bb.b])
                        tt('dve', bb[:], bb[:], xc_[:], ALU.mult, [bb.b, xc_.b], [bb.b])
                        sc.op('dve', lambda: nc.vector.tensor_tensor_scan(out=hs[:], data0=aa[:], data1=bb[:],
                                                                          initial=hprev[:, n:n + 1], op0=ALU.mult, op1=ALU.add),
                              r=[aa.b, bb.b, hprev.b], w=[hs.b])
                        cp('pool', hprev[:, n:n + 1], hs[:, G - 1:G], [hs.b], [hprev.b])
                        tt('dve', recT[:, n, :], hs[:], gl_[:], ALU.mult, [hs.b, gl_.b], [recT.b])

                    lru_p1(0)
                    for n in range(8):
                        if n + 1 < 8:
                            lru_p1(n + 1)
                        lru_p2(n)
                    if s == 0 and g == 0:
                        dump("recT0", recT[:], recT)
                    for oc in range(8):
                        p1 = PS()
                        for k in range(8):
                            mm(p1[:, :], wgb[:, k, oc * 128:(oc + 1) * 128], hTb[:, k, :], k == 0, k == 7,
                               [wgb.b, hTb.b], [p1.b])
                        act(gb[:], p1[:, :], AF.Sigmoid, [p1.b], [gb.b])
                        p2 = PS()
                        for k in range(8):
                            mm(p2[:, :], wol[:, k, oc * 128:(oc + 1) * 128], recT[:, k, :], k == 0, k == 7,
                               [wol.b, recT.b], [p2.b])
                        tt('dve', t1[:], p2[:, :], gb[:], ALU.mult, [p2.b, gb.b], [t1.b])
                        tt('dve', yaT[:, oc, :], t1[:], yaT[:, oc, :], ALU.add, [t1.b, yaT.b], [yaT.b])
                    for oc in range(8):
                        p3 = PS()
                        for k in range(8):
                            mm(p3[:, :], wout[:, k, oc * 128:(oc + 1) * 128], yaT[:, k, :], k == 0, k == 7,
                               [wout.b, yaT.b], [p3.b])
                        stt(xb[:, oc, :], p3[:, :], modT[:, GT1 + oc, s:s + 1], xb[:, oc, :], ALU.mult, ALU.add,
                            [p3.b, modT.b, xb.b], [xb.b])
                    st_(out_d[s, :, :, tok], xb[:], xb, [out_b[s][g]])
                sc.barrier()
            if dbg_stop == 'D':
                continue
            with ExitStack() as st:
                wq = T("wq", [128, 8, 2048], BF16, st)
                skT = T("skT", [128, 16, 128], BF16, st)
                load_w_bf16(wq, wq[:], wq_d[:, :, :], (128, 8, 2048))
                load_w_bf16(skT, skT[:], skT_d[:, :, :], (128, 16, 128))
                x1g = T("x1g", [128, 8, G], F32, st)
                h2b = T("h2b", [128, 8, G], BF16, st)
                qTb = T("qTb", [128, 16, G], BF16, st)
                gi = 0
                for g in range(NG):
                    tok = slice(g * G, (g + 1) * G)
                    with ExitStack() as s3:
                        sqE = T("sqE", [128, 8, G], F32, s3)
                        rstdE = T("rstdE", [128, G], F32, s3)
                        ld(x1g[:], out_d[s, :, :, tok], x1g, r=[out_b[s][g]])
                        act(sqE[:], x1g[:], AF.Square, [x1g.b], [sqE.b])
                        pb = PS(2, 8)
                        for k in range(8):
                            mm(pb[:, :], onesD[:, :], sqE[:, k, :], k == 0, k == 7, [onesD.b, sqE.b], [pb.b])
                        act(rstdE[:], pb[:, :], AF.Ln, [pb.b], [rstdE.b], bias=epsb[:, 0:1])
                        act(rstdE[:], rstdE[:], AF.Exp, [rstdE.b], [rstdE.b], scale=-0.5)
                        for k in range(8):
                            stt(sqE[:, k, :], x1g[:, k, :], A2[:, k, s:s + 1], rstdE[:], ALU.mult, ALU.mult,
                                [x1g.b, A2.b, rstdE.b], [sqE.b])
                            act(h2b[:, k, :], sqE[:, k, :], AF.Identity, [sqE.b, modT.b], [h2b.b], bias=modT[:, SH2 + k, s:s + 1])
                        if s == 0 and g == 0:
                            dump("h2T0", h2b[:], h2b)
                        for m in range(16):
                            pq = PS(2, 8)
                            for k in range(8):
                                mm(pq[:, :], wq[:, k, m * 128:(m + 1) * 128], h2b[:, k, :], k == 0, k == 7,
                                   [wq.b, h2b.b], [pq.b])
                            cp('act', qTb[:, m, :], pq[:, :], [pq.b], [qTb.b])
                        sc.barrier()
                    with ExitStack() as s3:
                        scs = T("scs", [128, 16, 128], F32, s3)
                        top = T("top", [128, 16, 16], F32, s3)
                        tix = T("tix", [128, 16, 16], U32, s3)
                        tif = T("tif", [128, 16, 16], F32, s3)
                        cand = T("cand", [128, 8, 256], F32, s3)
                        eq = T("eq", [128, 128, 16], BF16, s3)
                        ai = T("ai", [128, 128], U32, s3)
                        bi = T("bi", [128, 128], U32, s3)
                        af = T("af", [128, 128], F32, s3)
                        bf = T("bf", [128, 128], F32, s3)
                        Isel = T("Isel", [128, 128], F32, s3)
                        Jsel = T("Jsel", [128, 128], F32, s3)
                        best = T("best", [128, 8, 16], F32, s3)
                        bpos = T("bpos", [128, 8, 16], U32, s3)
                        gsum = T("gsum", [128, 8], F32, s3)
                        idxf = T("idxf", [128, 128], F32, s3)
                        gw = T("gw", [128, 8, 16], F32, s3)
                        idxTs = [T("idxT%d" % i, [128, 128], U32, s3) for i in range(4)]
                        gTs = [T("gT%d" % i, [128, 128], F32, s3) for i in range(4)]
                        NGB = 10
                        DG = 6
                        assert NGB - DG >= 4
                        Gs = [T("Gg%d" % i, [128, 2048], BF16, s3) for i in range(NGB)]
                        wsel = [T("wsel%d" % i, [128, 128], BF16, s3) for i in range(3)]
                        actw = [T("actw%d" % i, [128, 1], F32, s3) for i in range(4)]
                        pacc = T("pacc", [128, 1024], F32, s3)
                        junk2s = [T("junk2_%d" % i, [128, 1024], BF16, s3) for i in range(3)]
                        actr = [T("actr%d" % i, [128, 1], F32, s3) for i in range(4)]
                        actg = [T("actg%d" % i, [128, 1], F32, s3) for i in range(4)]
                        scs2v = stg[0][:, :].rearrange("p (a b) -> p a b", a=16)
                        cand2v = stg[1][:, :].rearrange("p (a b) -> p a b", a=8)
                        scs2b, cand2b = stg[0].b, stg[1].b
                        pS = PSB[6]
                        pT7 = PSB[7]

                        def topk_gen(t4):
                            tsl = slice(t4 * 128, (t4 + 1) * 128)
                            idxT = idxTs[t4]
                            gT = gTs[t4]
                            for q4 in range(4):
                                for mi in range(4):
                                    m = q4 * 4 + mi
                                    mm(pS[:, mi * 128:(mi + 1) * 128], qTb[:, m, tsl], skT[:, m, :], True, True,
                                       [qTb.b, skT.b], [pS.b])
                                yield
                                cp('act', scs[:, q4 * 4:(q4 + 1) * 4, :], pS[:, :].rearrange("p (a b) -> p a b", a=4),
                                   [pS.b], [scs.b])
                                yield
                            for m in range(16):
                                sc.op('dve', lambda: nc.vector.max(out=top[:, m, 0:8], in_=scs[:, m, :]), r=[scs.b], w=[top.b])
                                yield
                                sc.op('dve', lambda: nc.vector.max_index(out=tix[:, m, 0:8], in_max=top[:, m, 0:8], in_values=scs[:, m, :]),
                                      r=[scs.b, top.b], w=[tix.b])
                                yield
                                sc.op('dve', lambda: nc.vector.match_replace(out=scs2v[:, m, :], in_to_replace=top[:, m, 0:8],
                                                                             in_values=scs[:, m, :], imm_value=-1e30),
                                      r=[scs.b, top.b], w=[scs2b])
                                yield
                                sc.op('dve', lambda: nc.vector.max(out=top[:, m, 8:16], in_=scs2v[:, m, :]), r=[scs2b], w=[top.b])
                                yield
                                sc.op('dve', lambda: nc.vector.max_index(out=tix[:, m, 8:16], in_max=top[:, m, 8:16], in_values=scs2v[:, m, :]),
                                      r=[scs2b, top.b], w=[tix.b])
                                yield
                            cp('dve', tif[:], tix[:], [tix.b], [tif.b])
                            yield
                            for h in range(8):
                                cv = cand[:, h, :].rearrange("p (a b) -> p a b", a=16)
                                tt('dve', cv, top[:, 2 * h, :].unsqueeze(2).to_broadcast([128, 16, 16]),
                                   top[:, 2 * h + 1, :].unsqueeze(1).to_broadcast([128, 16, 16]), ALU.add, [top.b], [cand.b])
                                yield
                                sc.op('dve', lambda: nc.vector.max(out=best[:, h, 0:8], in_=cand[:, h, :]), r=[cand.b], w=[best.b])
                                yield
                                sc.op('dve', lambda: nc.vector.max_index(out=bpos[:, h, 0:8], in_max=best[:, h, 0:8], in_values=cand[:, h, :]),
                                      r=[cand.b, best.b], w=[bpos.b])
                                yield
                                sc.op('dve', lambda: nc.vector.match_replace(out=cand2v[:, h, :], in_to_replace=best[:, h, 0:8],
                                                                             in_values=cand[:, h, :], imm_value=-1e30),
                                      r=[cand.b, best.b], w=[cand2b])
                                yield
                                sc.op('dve', lambda: nc.vector.max(out=best[:, h, 8:16], in_=cand2v[:, h, :]), r=[cand2b], w=[best.b])
                                yield
                                sc.op('dve', lambda: nc.vector.max_index(out=bpos[:, h, 8:16], in_max=best[:, h, 8:16], in_values=cand2v[:, h, :]),
                                      r=[cand2b, best.b], w=[bpos.b])
                                yield
                            bflat = bpos[:].rearrange("p a b -> p (a b)")
                            ts('dve', ai[:], bflat, 4, None, ALU.logical_shift_right, None, [bpos.b], [ai.b])
                            yield
                            ts('dve', bi[:], bflat, 15, None, ALU.bitwise_and, None, [bpos.b], [bi.b])
                            yield
                            cp('dve', af[:], ai[:], [ai.b], [af.b])
                            yield
                            cp('dve', bf[:], bi[:], [bi.b], [bf.b])
                            yield
                            for (xf, par, sel) in ((af, 0, Isel), (bf, 1, Jsel)):
                                tt('dve', eq[:], iota256[:, 0:16].unsqueeze(1).to_broadcast([128, 128, 16]),
                                   xf[:].unsqueeze(2).to_broadcast([128, 128, 16]), ALU.is_equal, [iota256.b, xf.b], [eq.b])
                                yield
                                e4 = eq[:].rearrange("p (h k) a -> p h k a", h=8)
                                tt('dve', e4, e4, tif[:, par::2, :].unsqueeze(2).to_broadcast([128, 8, 16, 16]), ALU.mult,
                                   [eq.b, tif.b], [eq.b])
                                yield
                                sc.op('dve', lambda: nc.vector.tensor_reduce(out=sel[:], in_=eq[:], axis=mybir.AxisListType.X, op=ALU.add),
                                      r=[eq.b], w=[sel.b])
                                yield
                            stt(idxf[:], Isel[:], 128.0, Jsel[:], ALU.mult, ALU.add, [Isel.b, Jsel.b], [idxf.b])
                            yield
                            tt('dve', gw[:], best[:], best[:, :, 0:1].to_broadcast([128, 8, 16]), ALU.subtract, [best.b], [gw.b])
                            yield
                            act(gw[:], gw[:], AF.Exp, [gw.b], [gw.b])
                            yield
                            sc.op('dve', lambda: nc.vector.tensor_reduce(out=gsum[:], in_=gw[:], axis=mybir.AxisListType.X, op=ALU.add),
                                  r=[gw.b], w=[gsum.b])
                            yield
                            sc.op('dve', lambda: nc.vector.reciprocal(out=gsum[:], in_=gsum[:]), r=[gsum.b], w=[gsum.b])
                            yield
                            tt('dve', gw[:], gw[:], gsum[:].unsqueeze(2).to_broadcast([128, 8, 16]), ALU.mult, [gw.b, gsum.b], [gw.b])
                            yield
                            if s == 0 and g == 0 and t4 == 0:
                                dump("idxf0", idxf[:], idxf)
                                dump("gw0", gw[:], gw)
                            sc.op('pe', lambda: nc.tensor.transpose(pS[:, 0:128], idxf[:], ident_f[:]), r=[idxf.b, ident_f.b], w=[pS.b])
                            yield
                            cp('dve', idxT[:], pS[:, 0:128], [pS.b], [idxT.b])
                            yield
                            sc.op('pe', lambda: nc.tensor.transpose(pS[:, 128:256], gw[:].rearrange("p a b -> p (a b)"), ident_f[:]),
                                  r=[gw.b, ident_f.b], w=[pS.b])
                            yield
                            cp('act', gT[:], pS[:, 128:256], [pS.b], [gT.b])
                            yield

                        def pull(gen, k):
                            if gen is None:
                                return
                            for _ in range(k):
                                try:
                                    next(gen)
                                except StopIteration:
                                    return

                        gi = 0
                        pull(topk_gen(0), 100000)
                        for t4 in range(4):
                            tsl = slice(t4 * 128, (t4 + 1) * 128)
                            idxT = idxTs[t4]
                            gT = gTs[t4]
                            gen = topk_gen(t4 + 1) if t4 < 3 else None

                            def gather(t):
                                Gt = Gs[(gi + t) % NGB]
                                sc.dma('pool', lambda: nc.gpsimd.indirect_dma_start(
                                    out=Gt[:], out_offset=None, in_=uvb_d[:, :],
                                    in_offset=bass.IndirectOffsetOnAxis(ap=idxT[:, t:t + 1], axis=0),
                                    bounds_check=None), Gt.b, r=[idxT.b, uvb_b], w=[Gt.b])

                            for t in range(DG):
                                gather(t)
                            for i in range(128 + 3):
                                if i + DG < 128:
                                    gather(i + DG)
                                if i < 128:
                                    t = i
                                    pj = 1 + (t % 2)
                                    tg = t4 * 128 + t
                                    for k in range(8):
                                        pk = PSB[2 * pj + k // 4]
                                        mm(pk[:, (k % 4) * 128:(k % 4 + 1) * 128], h2b[:, k, tg:tg + 1].to_broadcast([128, 128]),
                                           ident_b[:, :], True, True, [h2b.b, ident_b.b], [pk.b])
                                if 0 <= i - 1 < 128:
                                    t = i - 1
                                    pj = 1 + (t % 2)
                                    Gt = Gs[(gi + t) % NGB]
                                    junk2 = junk2s[t % 3]
                                    stt(junk2[:, :], Gt[:, 0:1024], 1.0, P2[pj][:, :], ALU.mult, ALU.mult,
                                        [Gt.b, PSB[2 * pj].b, PSB[2 * pj + 1].b], [junk2.b, actr[t % 4].b],
                                        accum_out=actr[t % 4][:, 0:1])
                                if 0 <= i - 2 < 128:
                                    t = i - 2
                                    ws = wsel[t % 3]
                                    act(actg[t % 4][:], actr[t % 4][:], AF.Gelu_apprx_tanh, [actr[t % 4].b], [actg[t % 4].b])
                                    act(actw[t % 4][:], actg[t % 4][:], AF.Identity, [actg[t % 4].b, gT.b], [actw[t % 4].b],
                                        scale=gT[:, t:t + 1])
                                    act(ws[:], crow[:, 127 - t:255 - t], AF.Identity, [crow.b, actw[t % 4].b], [ws.b],
                                        scale=actw[t % 4][:, 0:1])
                                if 0 <= i - 3 < 128:
                                    t = i - 3
                                    ws = wsel[t % 3]
                                    Gt = Gs[(gi + t) % NGB]
                                    for hf in range(2):
                                        pa_ = PSB[hf]
                                        mm(pa_[:, :], ws[:], Gt[:, 1024 + hf * 512:1024 + (hf + 1) * 512],
                                           t == 0, t == 127, [ws.b, Gt.b], [pa_.b])
                                pull(gen, 3)
                            pull(gen, 100000)
                            gi += 128
                            for hf in range(2):
                                cp('act', pacc[:, hf * 512:(hf + 1) * 512], PSB[hf][:, :], [PSB[hf].b], [pacc.b])
                            if s == 0 and g == 0 and t4 == 0:
                                dump("pacc0", pacc[:], pacc)
                            for k in range(8):
                                sc.op('pe', lambda: nc.tensor.transpose(pT7[:, 0:128], pacc[:, k * 128:(k + 1) * 128], ident_f[:]),
                                      r=[pacc.b, ident_f.b], w=[pT7.b])
                                stt(x1g[:, k, tsl], pT7[:, 0:128], modT[:, GT2 + k, s:s + 1], x1g[:, k, tsl], ALU.mult, ALU.add,
                                    [pT7.b, modT.b, x1g.b], [x1g.b])
                        sc.barrier()
                    st_(out_d[s, :, :, tok], x1g[:], x1g, [out_b[s][g]])
                sc.barrier()
        except _Stop:
            pass
        sc.barrier()
    except AssertionError:
        if dbg_stop is None:
            raise
    return nc


def _chunkT(v, n):
    return np.ascontiguousarray(np.asarray(v, np.float32).reshape(n, 128).T)


def _kmaj(w):
    K, N = w.shape
    return np.ascontiguousarray(w.reshape(K // 128, 128, N).transpose(1, 0, 2))


def prep_shared(inp):
    f = lambda k: np.asarray(inp[k][0], np.float32)
    sh = {}
    sh["w_ada"] = _kmaj(f("w_ada"))
    sh["b_adaT"] = _chunkT(f("b_ada"), 48)
    pv = np.zeros((128, 96), np.float32)
    pv[:, 0:8] = _chunkT(f("norm1_g"), 8)
    pv[:, 8:16] = _chunkT(f("norm2_g"), 8)
    cw = f("conv_w")
    for k in range(4):
        pv[:, 16 + k * 8:16 + (k + 1) * 8] = _chunkT(cw[k], 8)
    pv[:, 48:56] = _chunkT(f("conv_b"), 8)
    pv[:, 56:64] = _chunkT(f("b_rg"), 8)
    pv[:, 64:72] = _chunkT(f("b_ig"), 8)
    pv[:, 72:80] = _chunkT(f("lru_lambda"), 8)
    pv[:, 80:82] = _chunkT(f("q_a_norm_g"), 2)
    pv[:, 82:83] = _chunkT(f("kv_a_norm_g"), 1)
    perm = np.concatenate([np.arange(64), 80 + np.arange(16), 64 + np.arange(16)])
    gq = f("q_norm_g")
    gk = f("k_norm_g")
    pv[:96, 83] = gq
    pv[:96, 84] = gq[perm]
    pv[:96, 85] = gk
    pv[:96, 86] = gk[perm]
    inv_freq = (1.0 / (10000.0 ** (np.arange(0, 32, 2, dtype=np.float32) / np.float32(32)))).astype(np.float32)
    pv[64:80, 87] = inv_freq
    pv[80:96, 87] = inv_freq
    pv[64:80, 88] = -1.0
    pv[80:96, 88] = 1.0
    sh["pvec"] = pv
    w_in = f("w_in")
    wl = np.zeros((1024, 512), np.float32)
    wl[:, 0:416] = w_in[:, 0:416]
    wl[:, 416:448] = w_in[:, 384:416][:, perm[64:] - 64]
    sh["w_lat"] = _kmaj(wl)
    w_uq = f("w_uq")
    wr = np.zeros((256, 1536), np.float32)
    wr[:, :768] = w_uq
    for h in range(8):
        wr[:, 768 + h * 96:768 + (h + 1) * 96] = w_uq[:, h * 96:(h + 1) * 96][:, perm]
    sh["w_uq2"] = _kmaj(wr)
    w_ukv = f("w_ukv")
    wkn = np.zeros((128, 768), np.float32)
    wv = np.zeros((128, 512), np.float32)
    for h in range(8):
        wkn[:, h * 96:h * 96 + 64] = w_ukv[:, h * 128:h * 128 + 64]
        wv[:, h * 64:(h + 1) * 64] = w_ukv[:, h * 128 + 64:h * 128 + 128]
    sh["w_kn"] = wkn
    sh["w_v"] = wv
    em = np.zeros((128, 96), np.float32)
    em[np.arange(32), 64 + np.arange(32)] = 1.0
    sh["emat"] = em
    sh["w_lx"] = _kmaj(w_in[:, 416:1440])
    sh["w_lg"] = _kmaj(w_in[:, 1440:2464])
    sh["w_ga"] = _kmaj(w_in[:, 2464:3488])
    sh["w_gb"] = _kmaj(w_in[:, 3488:4512])
    sh["w_rg"] = np.ascontiguousarray(f("w_rg").transpose(1, 0, 2))
    sh["w_ig"] = np.ascontiguousarray(f("w_ig").transpose(1, 0, 2))
    sh["w_oa"] = np.ascontiguousarray(f("w_o_attn").reshape(8, 64, 1024).transpose(1, 0, 2))
    sh["w_ol"] = _kmaj(f("w_o_lru"))
    sh["w_out"] = _kmaj(f("w_out"))
    sh["w_q"] = _kmaj(f("w_query"))
    sk = f("sub_keys").reshape(16, 128, 128)
    sh["skT"] = np.ascontiguousarray(sk.transpose(2, 0, 1))
    sh["uv"] = np.ascontiguousarray(np.concatenate([f("expert_u"), f("expert_v")], axis=1))
    return sh


def prep_core(inp, c):
    x = np.asarray(inp["x"][c * NSEQ:(c + 1) * NSEQ], np.float32)
    xT = np.ascontiguousarray(x.reshape(NSEQ, S, 8, 128).transpose(0, 3, 2, 1))
    cc = np.asarray(inp["c"][c * NSEQ:(c + 1) * NSEQ], np.float32)
    cT = np.ascontiguousarray(cc.reshape(NSEQ, 8, 128).transpose(2, 1, 0))
    pos = np.ascontiguousarray(np.asarray(inp["positions"][c * NSEQ:(c + 1) * NSEQ], np.int32))
    return {"xT": xT, "cT": cT, "pos": pos}


_NC_CACHE = {}


def kernel(**inputs):
    if "nc" not in _NC_CACHE:
        _NC_CACHE["nc"] = build_program()
    nc = _NC_CACHE["nc"]
    sh = prep_shared(inputs)
    in_maps = []
    for c in range(NCORES):
        m = dict(sh)
        m.update(prep_core(inputs, c))
        in_maps.append(m)
    res = run_bass_kernel_spmd(nc, in_maps, core_ids=list(range(NCORES)))
    outs = []
    for c in range(NCORES):
        o = np.asarray(res.results[c]["outT"])
        outs.append(o.transpose(0, 3, 2, 1).reshape(NSEQ, S, D))
    return np.ascontiguousarray(np.concatenate(outs, axis=0).astype(np.float32))
```

```python
import math
from contextlib import ExitStack

import numpy as np
import concourse.bass as bass
import concourse.mybir as mybir
from concourse.bass_utils import run_bass_kernel_spmd

F32 = mybir.dt.float32
F32R = mybir.dt.float32r
BF16 = mybir.dt.bfloat16
I32 = mybir.dt.int32
U32 = mybir.dt.uint32
AF = mybir.ActivationFunctionType
ALU = mybir.AluOpType

NCORES = 8
NSEQ = 2
S = 2048
D = 1024
G = 512
NG = S // G
EPS = 1e-6
NEXP = 16384
PI = math.pi

DEBUG = {}


class Buf:
    def __init__(self, name):
        self.name = name
        self.w = {}
        self.r = {}
        self.dsem = None
        self.dcnt = 0
        self.psum = False


class Tile:
    def __init__(self, t, name):
        self.t = t
        self.b = Buf(name)

    def __getitem__(self, k):
        return self.t[k]


class Sched:
    def __init__(self, nc, es):
        self.nc = nc
        self.es = es
        self.eng = dict(pe=nc.tensor, act=nc.scalar, dve=nc.vector, pool=nc.gpsimd, sp=nc.sync)
        self.sem = {k: es.enter_context(nc.semaphore("S_" + k)) for k in self.eng}
        self.cnt = {k: 0 for k in self.eng}
        self.seen = {k: {} for k in self.eng}
        self.dbufs = []
        self.free = []
        self.free_sw = []
        self.nd = 0

    def _wait(self, e, tok):
        key, sem, val, src = tok
        if self.seen[e].get(key, 0) >= val:
            return
        self.eng[e].wait_ge(sem, val)
        self.seen[e][key] = val

    def _deps(self, e, r, w, is_dma, dkey=None):
        for b in r:
            for tok in b.w.values():
                if (not is_dma) and tok[3] == e and e == 'pe':
                    continue
                self._wait(e, tok)
        for b in w:
            for tok in b.w.values():
                if (not is_dma) and tok[3] == e and e == 'pe':
                    continue
                if is_dma and tok[0] == dkey:
                    continue
                self._wait(e, tok)
            for tok in b.r.values():
                if (not is_dma) and tok[3] == e and e == 'pe':
                    continue
                self._wait(e, tok)

    def _post(self, tok, r, w):
        for b in w:
            if tok[3] == 'dma':
                b.w = {tok[0]: tok}
            else:
                b.w = {tok[0]: tok}
            b.r = {}
        for b in r:
            if b not in w:
                b.r[tok[0]] = tok

    def op(self, e, fn, r=(), w=()):
        pr_ = [b for b in r if b.psum and b not in w]
        if pr_:
            w = list(w) + pr_
            r = [b for b in r if not b.psum]
        self._deps(e, r, w, False)
        inst = fn()
        self.cnt[e] += 1
        inst.then_inc(self.sem[e], 1)
        tok = ("E_" + e, self.sem[e], self.cnt[e], e)
        self.seen[e][tok[0]] = max(self.seen[e].get(tok[0], 0), 0)
        self._post(tok, r, w)
        return inst

    def dma(self, e, fn, sb, r=(), w=()):
        if sb.dsem is None:
            fl = self.free_sw if e == 'pool' else self.free
            if fl:
                sb.dsem = fl.pop()
            else:
                sem = self.es.enter_context(self.nc.semaphore("D%d" % self.nd))
                sb.dsem = ["D%d" % self.nd, sem, 0, e == 'pool']
                self.nd += 1
                self.dbufs.append(sb.dsem)
        ds = sb.dsem
        self._deps(e, r, w, True, ds[0])
        inst = fn()
        ds[2] += 16
        inst.then_inc(ds[1], 16)
        tok = (ds[0], ds[1], ds[2], 'dma')
        self._post(tok, r, w)
        return inst

    def release(self, buf):
        if buf.dsem is not None:
            (self.free_sw if buf.dsem[3] else self.free).append(buf.dsem)
            buf.dsem = None

    def barrier(self):
        for e in self.eng:
            for e2 in self.eng:
                if e2 != e and self.cnt[e2] > 0:
                    self._wait(e, ("E_" + e2, self.sem[e2], self.cnt[e2], e2))
            for ds in self.dbufs:
                if ds[2] > 0:
                    self._wait(e, (ds[0], ds[1], ds[2], 'dma'))


class _Stop(Exception):
    pass


def build_program(dbg_stop=None):
    nc = bass.Bass("TRN2", target_bir_lowering=False)

    def din(name, shape, dt=F32):
        return nc.dram_tensor(name, list(shape), dt, kind="ExternalInput").ap()

    xT_d = din("xT", [NSEQ, 128, 8, S])
    cT_d = din("cT", [128, 8, NSEQ])
    pos_d = din("pos", [NSEQ, S], I32)
    wada_d = din("w_ada", [128, 8, 6144])
    bada_d = din("b_adaT", [128, 48])
    pvec_d = din("pvec", [128, 96])
    wlat_d = din("w_lat", [128, 8, 512])
    wuq_d = din("w_uq2", [128, 2, 1536])
    wkn_d = din("w_kn", [128, 8 * 96])
    wv_d = din("w_v", [128, 512])
    emat_d = din("emat", [128, 96])
    wlx_d = din("w_lx", [128, 8, 1024])
    wlg_d = din("w_lg", [128, 8, 1024])
    wga_d = din("w_ga", [128, 8, 1024])
    wgb_d = din("w_gb", [128, 8, 1024])
    wrg_d = din("w_rg", [128, 8, 128])
    wig_d = din("w_ig", [128, 8, 128])
    woa_d = din("w_oa", [64, 8, 1024])
    wol_d = din("w_ol", [128, 8, 1024])
    wout_d = din("w_out", [128, 8, 1024])
    wq_d = din("w_q", [128, 8, 2048])
    skT_d = din("skT", [128, 16, 128])
    uv_d = din("uv", [NEXP, 2048])
    out_d = nc.dram_tensor("outT", [NSEQ, 128, 8, S], F32, kind="ExternalOutput").ap()
    hT_d = nc.dram_tensor("hT_s", [NSEQ, 128, 8, S], BF16, kind="Internal").ap()
    ya_d = nc.dram_tensor("ya_s", [NSEQ, 128, 8, S], BF16, kind="Internal").ap()
    uvb_d = nc.dram_tensor("uvb_s", [NEXP, 2048], BF16, kind="Internal").ap()
    dbg_d = {k: nc.dram_tensor(k, list(v[0]), v[1], kind="ExternalOutput").ap() for k, v in DEBUG.items()}

    es = ExitStack()
    try:
      with es:
        sc = Sched(nc, es)

        uid = {'i': 0}

        def T(name, shape, dt, st=es):
            uid['i'] += 1
            nm = "t%d_%s" % (uid['i'], name)
            tl = Tile(st.enter_context(nc.sbuf_tensor(nm, list(shape), dt)), nm)
            if st is not es:
                st.callback(sc.release, tl.b)
            return tl

        class BankView:
            def __init__(self, t, off, name):
                self.t = t
                self.off = off
                self.b = Buf(name)

            def __getitem__(self, key):
                ps_, cs = key
                a = (cs.start or 0) + self.off
                b_ = (cs.stop if cs.stop is not None else 512) + self.off
                return self.t[ps_, slice(a, b_, cs.step)]

        P2 = [es.enter_context(nc.psum_tensor("pp%d" % i, [128, 1024], F32)) for i in range(4)]
        PSB = [BankView(P2[i // 2], (i % 2) * 512, "ps%d" % i) for i in range(8)]
        for p_ in PSB:
            p_.b.psum = True
        rot = {'i': 0}

        def PS(lo=0, hi=8):
            i = rot['i']
            n = hi - lo
            b = PSB[lo + (i % n)]
            rot['i'] = i + 1
            return b

        hT_b = [[Buf("hTd%d_%d" % (s, g)) for g in range(NG)] for s in range(NSEQ)]
        ya_b = [[Buf("yad%d_%d" % (s, g)) for g in range(NG)] for s in range(NSEQ)]
        out_b = [[Buf("outd%d_%d" % (s, g)) for g in range(NG)] for s in range(NSEQ)]
        dbg_b = {k: Buf("dbg_" + k) for k in DEBUG}

        def mm(out, lhsT, rhs, start, stop, r, w):
            return sc.op('pe', lambda: nc.tensor.matmul(out, lhsT, rhs, start=start, stop=stop), r=r, w=w)

        def act(out, in_, func, r, w, bias=None, scale=None):
            kw = {}
            if bias is not None:
                kw['bias'] = bias
            if scale is not None:
                kw['scale'] = scale
            return sc.op('act', lambda: nc.scalar.activation(out=out, in_=in_, func=func, **kw), r=r, w=w)

        def tt(e, out, in0, in1, op, r, w):
            eng = nc.vector if e == 'dve' else nc.gpsimd
            return sc.op(e, lambda: eng.tensor_tensor(out=out, in0=in0, in1=in1, op=op), r=r, w=w)

        def ts(e, out, in0, s1, s2, op0, op1, r, w):
            eng = nc.vector if e == 'dve' else nc.gpsimd
            if op1 is None:
                return sc.op(e, lambda: eng.tensor_scalar(out=out, in0=in0, scalar1=s1, scalar2=None, op0=op0), r=r, w=w)
            return sc.op(e, lambda: eng.tensor_scalar(out=out, in0=in0, scalar1=s1, scalar2=s2, op0=op0, op1=op1), r=r, w=w)

        def stt(out, in0, scalar, in1, op0, op1, r, w, accum_out=None):
            if accum_out is not None:
                return sc.op('dve', lambda: nc.vector.scalar_tensor_tensor(out=out, in0=in0, scalar=scalar, in1=in1, op0=op0, op1=op1, accum_out=accum_out), r=r, w=w)
            return sc.op('dve', lambda: nc.vector.scalar_tensor_tensor(out=out, in0=in0, scalar=scalar, in1=in1, op0=op0, op1=op1), r=r, w=w)

        def cp(e, out, in_, r, w):
            if e == 'act':
                return sc.op('act', lambda: nc.scalar.copy(out=out, in_=in_), r=r, w=w)
            eng = nc.vector if e == 'dve' else nc.gpsimd
            return sc.op(e, lambda: eng.tensor_copy(out=out, in_=in_), r=r, w=w)

        def ld(out, in_, tile, r=(), eng='sp'):
            e = {'sp': nc.sync, 'pool': nc.gpsimd, 'act': nc.scalar}[eng]
            return sc.dma(eng, lambda: e.dma_start(out=out, in_=in_), tile.b, r=r, w=[tile.b])

        def st_(out, in_, tile, wb, eng='sp'):
            e = {'sp': nc.sync, 'pool': nc.gpsimd, 'act': nc.scalar}[eng]
            return sc.dma(eng, lambda: e.dma_start(out=out, in_=in_), tile.b, r=[tile.b], w=wb)

        def dump(name, in_ap, tile, out_ap=None):
            if name in DEBUG:
                o = dbg_d[name] if out_ap is None else out_ap
                st_(o, in_ap, tile, [dbg_b[name]])

        pvec = T("pvec", [128, 96], F32)
        ld(pvec[:], pvec_d[:, :], pvec)
        C_N1, C_N2, C_CW, C_CB, C_BRG, C_BIG, C_LAM, C_QA, C_KVA = 0, 8, 16, 48, 56, 64, 72, 80, 82
        C_GQ, C_GQR, C_GK, C_GKR, C_INVF, C_SGN = 83, 84, 85, 86, 87, 88
        ones_f = T("ones_f", [128, 128], F32)
        sc.op('pool', lambda: nc.gpsimd.memset(ones_f[:], 1.0), w=[ones_f.b])
        onesD = T("onesD", [128, 128], F32)
        sc.op('pool', lambda: nc.gpsimd.memset(onesD[:], 1.0 / D), w=[onesD.b])
        ones256 = T("ones256", [128, 128], F32)
        sc.op('pool', lambda: nc.gpsimd.memset(ones256[:], 1.0 / 256), w=[ones256.b])
        ones128 = T("ones128", [128, 128], F32)
        sc.op('pool', lambda: nc.gpsimd.memset(ones128[:], 1.0 / 128), w=[ones128.b])
        ones96 = T("ones96", [128, 128], F32)
        sc.op('pool', lambda: nc.gpsimd.memset(ones96[:], 1.0 / 96), w=[ones96.b])
        epsb = T("epsb", [128, 1], F32)
        sc.op('pool', lambda: nc.gpsimd.memset(epsb[:], EPS), w=[epsb.b])
        oneb = T("oneb", [128, 1], F32)
        sc.op('pool', lambda: nc.gpsimd.memset(oneb[:], 1.0), w=[oneb.b])
        iot = T("iot", [128, 128], F32)
        sc.op('pool', lambda: nc.gpsimd.iota(iot[:], pattern=[[1, 128]], base=0, channel_multiplier=-1,
                                             allow_small_or_imprecise_dtypes=True), w=[iot.b])
        ident_f = T("ident_f", [128, 128], F32)
        ts('dve', ident_f[:], iot[:], 0.0, None, ALU.is_equal, None, [iot.b], [ident_f.b])
        ident_b = T("ident_b", [128, 128], BF16)
        cp('dve', ident_b[:], ident_f[:], [ident_f.b], [ident_b.b])
        crow = T("crow", [128, 255], F32)
        sc.op('pool', lambda: nc.gpsimd.iota(crow[:], pattern=[[1, 255]], base=-127, channel_multiplier=0,
                                             allow_small_or_imprecise_dtypes=True), w=[crow.b])
        ts('dve', crow[:], crow[:], 0.0, None, ALU.is_equal, None, [crow.b], [crow.b])
        iota256 = T("iota256", [128, 256], F32)
        sc.op('pool', lambda: nc.gpsimd.iota(iota256[:], pattern=[[1, 256]], base=0, channel_multiplier=0,
                                             allow_small_or_imprecise_dtypes=True), w=[iota256.b])

        sc.barrier()
        stg = [T("stg%d" % i, [128, 2048], F32) for i in range(2)]
        stg_i = {'i': 0}

        def load_w_bf16(dst_tile, dst_ap, src_ap, shape):
            p, a, b = shape
            cb = max(1, 2048 // a)
            for c0 in range(0, b, cb):
                c1 = min(b, c0 + cb)
                sg = stg[stg_i['i'] % 2]
                stg_i['i'] += 1
                sv = sg[0:p, 0:a * (c1 - c0)].rearrange("p (a b) -> p a b", a=a)
                ld(sv, src_ap[:, :, c0:c1], sg)
                cp('dve', dst_ap[:, :, c0:c1], sv, [sg.b], [dst_tile.b])

        modT = T("modT", [128, 48, NSEQ], F32)
        A1 = T("A1", [128, 8, NSEQ], F32)
        A2 = T("A2", [128, 8, NSEQ], F32)
        kap = T("kap", [128, 8], F32)
        kap2 = T("kap2", [128, 8], F32)
        gqs = T("gqs", [128, 1], F32)
        gks = T("gks", [128, 1], F32)
        with ExitStack() as st:
            cTt = T("cTt", [128, 8, NSEQ], F32, st)
            csT = T("csT", [128, 8, NSEQ], F32, st)
            badaT = T("badaT", [128, 48], F32, st)
            ld(cTt[:], cT_d[:, :, :], cTt)
            ld(badaT[:], bada_d[:, :], badaT)
            act(csT[:], cTt[:], AF.Silu, [cTt.b], [csT.b])
            pm = PSB[0]
            for cc in range(24):
                sg0 = stg[cc % 2]
                sg = sg0[:, :].rearrange("p (a b) -> p a b", a=8)
                ld(sg, wada_d[:, :, cc * 256:(cc + 1) * 256], sg0)
                for j in range(2):
                    oc = cc * 2 + j
                    for k in range(8):
                        mm(pm[:, oc * 2:oc * 2 + 2], sg[:, k, j * 128:(j + 1) * 128], csT[:, k, :],
                           k == 0, k == 7, [sg0.b, csT.b], [pm.b])
            for b in range(NSEQ):
                tt('dve', modT[:, :, b], pm[:, b:96:2], badaT[:, :], ALU.add, [pm.b, badaT.b], [modT.b])
            for b in range(NSEQ):
                stt(A1[:, :, b], modT[:, 8:16, b], 1.0, pvec[:, C_N1:C_N1 + 8], ALU.add, ALU.mult,
                    [modT.b, pvec.b], [A1.b])
                stt(A2[:, :, b], modT[:, 32:40, b], 1.0, pvec[:, C_N2:C_N2 + 8], ALU.add, ALU.mult,
                    [modT.b, pvec.b], [A2.b])
            tk = T("tk", [128, 8], F32, st)
            act(tk[:], pvec[:, C_LAM:C_LAM + 8], AF.Exp, [pvec.b], [tk.b], scale=-1.0)
            act(tk[:], tk[:], AF.Ln, [tk.b], [tk.b], bias=oneb[:, 0:1])
            ts('dve', kap[:], tk[:], -8.0, None, ALU.mult, None, [tk.b], [kap.b])
            ts('dve', kap2[:], tk[:], -16.0, None, ALU.mult, None, [tk.b], [kap2.b])
            tt('dve', gqs[:], pvec[:, C_GQR:C_GQR + 1], pvec[:, C_SGN:C_SGN + 1], ALU.mult, [pvec.b], [gqs.b])
            tt('dve', gks[:], pvec[:, C_GKR:C_GKR + 1], pvec[:, C_SGN:C_SGN + 1], ALU.mult, [pvec.b], [gks.b])
            dump("modT", modT[:], modT)
            sc.barrier()

        uvb_b = Buf("uvb")
        with ExitStack() as st:
            cin = [T("cin%d" % i, [128, 4, 2048], F32, st) for i in range(2)]
            co_d = [T("cod%d" % i, [128, 3, 2048], BF16, st) for i in range(2)]
            co_a = [T("coa%d" % i, [128, 1, 2048], BF16, st) for i in range(2)]
            uvv = uv_d.rearrange("(c p j) d -> c p j d", p=128, j=4)
            uvbv = uvb_d.rearrange("(c p j) d -> c p j d", p=128, j=4)
            for c in range(NEXP // 512):
                ci = cin[c % 2]
                ld(ci[:], uvv[c], ci)
                cp('dve', co_d[c % 2][:], ci[:, 0:3, :], [ci.b], [co_d[c % 2].b])
                cp('act', co_a[c % 2][:], ci[:, 3:4, :], [ci.b], [co_a[c % 2].b])
                st_(uvbv[c][:, 0:3, :], co_d[c % 2][:], co_d[c % 2], [uvb_b])
                st_(uvbv[c][:, 3:4, :], co_a[c % 2][:], co_a[c % 2], [uvb_b])
            sc.barrier()
        if dbg_stop == 'S0':
            sc.barrier()
            return nc
        SH1, GT1, SH2, GT2 = 0, 16, 24, 40

        def rms_modulate(xb, hTb, A, shoff, s, st, tag):
            sq = T("sq" + tag, [128, 8, G], F32, st)
            rstd = T("rstd" + tag, [128, G], F32, st)
            act(sq[:], xb[:], AF.Square, [xb.b], [sq.b])
            pb = PS()
            for k in range(8):
                mm(pb[:, :], onesD[:, :], sq[:, k, :], k == 0, k == 7, [onesD.b, sq.b], [pb.b])
            act(rstd[:], pb[:, :], AF.Ln, [pb.b], [rstd.b], bias=epsb[:, 0:1])
            act(rstd[:], rstd[:], AF.Exp, [rstd.b], [rstd.b], scale=-0.5)
            for k in range(8):
                stt(sq[:, k, :], xb[:, k, :], A[:, k, s:s + 1], rstd[:], ALU.mult, ALU.mult,
                    [xb.b, A.b, rstd.b], [sq.b])
                ts('pool', hTb[:, k, :], sq[:, k, :], modT[:, shoff + k, s:s + 1], None, ALU.add, None,
                   [sq.b, modT.b], [hTb.b])

        def chk(name):
            if dbg_stop == name:
                raise _Stop()

        try:
         for s in range(NSEQ):
            with ExitStack() as sa:
                qT = T("qT", [96, 8, S], BF16, sa)
                kT = T("kT", [96, 8, S], BF16, sa)
                Vb = T("Vb", [128, 16, 8, 65], BF16, sa)
                with ExitStack() as st:
                    cosT = T("cosT", [96, G], F32, st)
                    sinT = T("sinT", [96, G], F32, st)
                    posi = T("posi", [96, G], I32, st)
                    ang = T("ang", [96, G], F32, st)
                    nf = T("nf", [96, G], F32, st)
                    ni = T("ni", [96, G], I32, st)
                    wlat = T("wlat", [128, 8, 512], BF16, st)
                    wuq = T("wuq", [128, 2, 1536], BF16, st)
                    wkn = T("wkn", [128, 768], BF16, st)
                    wv = T("wv", [128, 512], BF16, st)
                    emat = T("emat", [128, 96], BF16, st)
                    load_w_bf16(wlat, wlat[:], wlat_d[:, :, :], (128, 8, 512))
                    load_w_bf16(wuq, wuq[:], wuq_d[:, :, :], (128, 2, 1536))
                    load_w_bf16(wkn, wkn[:].rearrange("p (a b) -> p a b", a=1), wkn_d[:, :].rearrange("p (a b) -> p a b", a=1), (128, 1, 768))
                    load_w_bf16(wv, wv[:].rearrange("p (a b) -> p a b", a=1), wv_d[:, :].rearrange("p (a b) -> p a b", a=1), (128, 1, 512))
                    load_w_bf16(emat, emat[:].rearrange("p (a b) -> p a b", a=1), emat_d[:, :].rearrange("p (a b) -> p a b", a=1), (128, 1, 96))
                    sc.op('pool', lambda: nc.gpsimd.memset(Vb[:], 1.0), w=[Vb.b])

                    def rope_tables(tok):
                        ld(posi[:], pos_d[s, tok].partition_broadcast(96), posi)
                        cp('dve', ang[:], posi[:], [posi.b], [ang.b])
                        ts('dve', ang[:], ang[:], pvec[0:96, C_INVF:C_INVF + 1], None, ALU.mult, None, [ang.b, pvec.b], [ang.b])
                        ts('dve', nf[:], ang[:], 1.0 / (2 * PI), None, ALU.mult, None, [ang.b], [nf.b])
                        cp('dve', ni[:], nf[:], [nf.b], [ni.b])
                        cp('dve', nf[:], ni[:], [ni.b], [nf.b])
                        C1 = 6.28125
                        C2 = 2 * PI - C1
                        stt(ang[:], nf[:], -C1, ang[:], ALU.mult, ALU.add, [nf.b, ang.b], [ang.b])
                        stt(ang[:], nf[:], -C2, ang[:], ALU.mult, ALU.add, [nf.b, ang.b], [ang.b])

                        def wrap_sin(dst, shift):
                            pf = posi[:].bitcast(F32)
                            ts('dve', nf[:], ang[:], shift, None, ALU.add, None, [ang.b], [nf.b])
                            for _ in range(2):
                                ts('dve', pf, nf[:], PI, -2 * PI, ALU.is_gt, ALU.mult, [nf.b], [posi.b])
                                tt('dve', nf[:], nf[:], pf, ALU.add, [nf.b, posi.b], [nf.b])
                                ts('dve', pf, nf[:], -PI, 2 * PI, ALU.is_lt, ALU.mult, [nf.b], [posi.b])
                                tt('dve', nf[:], nf[:], pf, ALU.add, [nf.b, posi.b], [nf.b])
                            ts('dve', nf[:], nf[:], 3.141592, -3.141592, ALU.min, ALU.max, [nf.b], [nf.b])
                            act(dst[:], nf[:], AF.Sin, [nf.b], [dst.b])
                        wrap_sin(sinT, 0.0)
                        wrap_sin(cosT, PI / 2)

                    xb = T("xb", [128, 8, G], F32, st)
                    hTb = T("hTb", [128, 8, G], BF16, st)
                    sqq = T("sqq", [128, 2, G], F32, st)
                    rq = T("rq", [128, G], F32, st)
                    cqn = T("cqn", [128, 2, G], BF16, st)
                    ckvn = T("ckvn", [128, G], BF16, st)
                    krb = T("krb", [128, G], BF16, st)
                    krrb = T("krrb", [128, G], BF16, st)
                    sc.op('pool', lambda: nc.gpsimd.memset(krb[:], 0.0), w=[krb.b])
                    sc.op('pool', lambda: nc.gpsimd.memset(krrb[:], 0.0), w=[krrb.b])
                    m2k = T("m2k", [96, G], F32, st)
                    sqh = T("sqh", [96, G], F32, st)
                    rh = T("rh", [96, G], F32, st)
                    m1 = T("m1", [96, G], F32, st)
                    m2 = T("m2", [96, G], F32, st)
                    for g in range(NG):
                        tok = slice(g * G, (g + 1) * G)
                        rope_tables(tok)
                        if g == 0 and s == 0:
                            dump("cosT", cosT[:], cosT)
                            dump("sinT", sinT[:], sinT)
                        chk('A1')
                        ld(xb[:], xT_d[s, :, :, tok], xb)
                        if g == 0:
                            sqA = T("sqA", [128, 8, G], F32, st)
                            rstdA = T("rstdA", [128, G], F32, st)
                        act(sqA[:], xb[:], AF.Square, [xb.b], [sqA.b])
                        pb = PS()
                        for k in range(8):
                            mm(pb[:, :], onesD[:, :], sqA[:, k, :], k == 0, k == 7, [onesD.b, sqA.b], [pb.b])
                        act(rstdA[:], pb[:, :], AF.Ln, [pb.b], [rstdA.b], bias=epsb[:, 0:1])
                        act(rstdA[:], rstdA[:], AF.Exp, [rstdA.b], [rstdA.b], scale=-0.5)
                        for k in range(8):
                            stt(sqA[:, k, :], xb[:, k, :], A1[:, k, s:s + 1], rstdA[:], ALU.mult, ALU.mult,
                                [xb.b, A1.b, rstdA.b], [sqA.b])
                            act(hTb[:, k, :], sqA[:, k, :], AF.Identity, [sqA.b, modT.b], [hTb.b], bias=modT[:, SH1 + k, s:s + 1])
                        st_(hT_d[s, :, :, tok], hTb[:], hTb, [hT_b[s][g]])
                        if g == 0 and s == 0:
                            dump("hT0", hTb[:], hTb)
                        chk('A2')
                        pcq = [PS(), PS()]
                        for j in range(2):
                            for k in range(8):
                                mm(pcq[j][:, :], wlat[:, k, j * 128:(j + 1) * 128], hTb[:, k, :], k == 0, k == 7,
                                   [wlat.b, hTb.b], [pcq[j].b])
                        pkv = PS()
                        for k in range(8):
                            mm(pkv[:, :], wlat[:, k, 256:384], hTb[:, k, :], k == 0, k == 7, [wlat.b, hTb.b], [pkv.b])
                        pkr = PS()
                        for k in range(8):
                            mm(pkr[0:32, :], wlat[:, k, 384:416], hTb[:, k, :], k == 0, k == 7, [wlat.b, hTb.b], [pkr.b])
                        pkrr = PS()
                        for k in range(8):
                            mm(pkrr[0:32, :], wlat[:, k, 416:448], hTb[:, k, :], k == 0, k == 7, [wlat.b, hTb.b], [pkrr.b])
                        for j in range(2):
                            act(sqq[:, j, :], pcq[j][:, :], AF.Square, [pcq[j].b], [sqq.b])
                        pn = PS()
                        for j in range(2):
                            mm(pn[:, :], ones256[:, :], sqq[:, j, :], j == 0, j == 1, [ones256.b, sqq.b], [pn.b])
                        act(rq[:], pn[:, :], AF.Ln, [pn.b], [rq.b], bias=epsb[:, 0:1])
                        act(rq[:], rq[:], AF.Exp, [rq.b], [rq.b], scale=-0.5)
                        for j in range(2):
                            stt(cqn[:, j, :], pcq[j][:, :], pvec[:, C_QA + j:C_QA + j + 1], rq[:], ALU.mult, ALU.mult,
                                [pcq[j].b, pvec.b, rq.b], [cqn.b])
                        act(sqq[:, 0, :], pkv[:, :], AF.Square, [pkv.b], [sqq.b])
                        pn = PS()
                        mm(pn[:, :], ones128[:, :], sqq[:, 0, :], True, True, [ones128.b, sqq.b], [pn.b])
                        act(rq[:], pn[:, :], AF.Ln, [pn.b], [rq.b], bias=epsb[:, 0:1])
                        act(rq[:], rq[:], AF.Exp, [rq.b], [rq.b], scale=-0.5)
                        stt(ckvn[:], pkv[:, :], pvec[:, C_KVA:C_KVA + 1], rq[:], ALU.mult, ALU.mult,
                            [pkv.b, pvec.b, rq.b], [ckvn.b])
                        cp('act', krb[0:32, :], pkr[0:32, :], [pkr.b], [krb.b])
                        cp('act', krrb[0:32, :], pkrr[0:32, :], [pkrr.b], [krrb.b])
                        chk('A3')
                        pr = PS()
                        mm(pr[0:96, :], emat[:, :], krrb[:, :], True, True, [emat.b, krrb.b], [pr.b])
                        stt(m2k[:], pr[0:96, :], gks[0:96, 0:1], sinT[:, :], ALU.mult, ALU.mult,
                            [pr.b, gks.b, sinT.b], [m2k.b])
                        chk('A3a')
                        for hh in range(16):
                            isq = hh < 8
                            h = hh % 8
                            pa = PS()
                            if isq:
                                for j in range(2):
                                    mm(pa[0:96, :], wuq[:, j, h * 96:(h + 1) * 96], cqn[:, j, :], j == 0, j == 1,
                                       [wuq.b, cqn.b], [pa.b])
                                pb2 = PS()
                                for j in range(2):
                                    mm(pb2[0:96, :], wuq[:, j, 768 + h * 96:768 + (h + 1) * 96], cqn[:, j, :], j == 0, j == 1,
                                       [wuq.b, cqn.b], [pb2.b])
                            else:
                                mm(pa[0:96, :], wkn[:, h * 96:(h + 1) * 96], ckvn[:, :], True, False, [wkn.b, ckvn.b], [pa.b])
                                mm(pa[0:96, :], emat[:, :], krb[:, :], False, True, [emat.b, krb.b], [pa.b])
                            chk('A3b')
                            act(sqh[:], pa[0:96, :], AF.Square, [pa.b], [sqh.b])
                            pc = PS()
                            mm(pc[0:96, :], ones96[0:96, 0:96], sqh[:, :], True, True, [ones96.b, sqh.b], [pc.b])
                            act(rh[:], pc[0:96, :], AF.Ln, [pc.b], [rh.b], bias=epsb[0:96, 0:1])
                            act(rh[:], rh[:], AF.Exp, [rh.b], [rh.b], scale=-0.5)
                            chk('A3c')
                            gcol = C_GQ if isq else C_GK
                            stt(m1[:], pa[0:96, :], pvec[0:96, gcol:gcol + 1], cosT[:, :], ALU.mult, ALU.mult,
                                [pa.b, pvec.b, cosT.b], [m1.b])
                            chk('A3d')
                            if isq:
                                stt(m2[:], pb2[0:96, :], gqs[0:96, 0:1], sinT[:, :], ALU.mult, ALU.mult,
                                    [pb2.b, gqs.b, sinT.b], [m2.b])
                                tt('dve', m1[:], m1[:], m2[:], ALU.add, [m1.b, m2.b], [m1.b])
                                tt('dve', qT[:, h, tok], m1[:], rh[:], ALU.mult, [m1.b, rh.b], [qT.b])
                            else:
                                tt('dve', m1[:], m1[:], m2k[:], ALU.add, [m1.b, m2k.b], [m1.b])
                                tt('dve', kT[:, h, tok], m1[:], rh[:], ALU.mult, [m1.b, rh.b], [kT.b])
                        chk('A4')
                        for t4 in range(4):
                            pv = PS()
                            mm(pv[:, :], ckvn[:, t4 * 128:(t4 + 1) * 128], wv[:, :], True, True, [ckvn.b, wv.b], [pv.b])
                            cp('act', Vb[:, g * 4 + t4, :, 0:64], pv[:, :].rearrange("p (h d) -> p h d", h=8),
                               [pv.b], [Vb.b])
                    if s == 0:
                        dump("qT0", qT[:], qT)
                        dump("kT0", kT[:], kT)
                        dump("Vb0", Vb[:], Vb)
                    sc.barrier()
                if dbg_stop == 'A':
                    break
                attnT = T("attnT", [64, 8, S], BF16, sa)
                with ExitStack() as st:
                    pTs = [T("pT%d" % i, [128, G], BF16, st) for i in range(3)]
                    accS = T("accS", [65, S], F32, st)
                    rrow = T("rrow", [65, S], F32, st)
                    scale = 96 ** -0.5
                    acc = PSB[0:4]
                    steps = []
                    for h in range(8):
                        for j in range(16):
                            for qg in range(j // 4, 4):
                                steps.append((h, j, qg))

                    def qk(step):
                        h, j, qg = step
                        q0 = max(qg * G, j * 128)
                        q1 = (qg + 1) * G
                        ps_ = PS(4, 8)
                        mm(ps_[:, 0:q1 - q0], kT[:, h, j * 128:(j + 1) * 128], qT[:, h, q0:q1], True, True,
                           [kT.b, qT.b], [ps_.b])
                        return ps_

                    def epilogue(h):
                        for qg in range(4):
                            cp('act', accS[:, qg * G:(qg + 1) * G], acc[qg][0:65, :], [acc[qg].b], [accS.b])
                        act(rrow[64:65, :], accS[64:65, :], AF.Ln, [accS.b], [rrow.b])
                        act(rrow[64:65, :], rrow[64:65, :], AF.Exp, [rrow.b], [rrow.b], scale=-1.0)
                        for qg in range(4):
                            pbc = acc[qg]
                            mm(pbc[0:64, :], ones_f[64:65, 0:64], rrow[64:65, qg * G:(qg + 1) * G], True, True,
                               [ones_f.b, rrow.b], [pbc.b])
                            tt('dve', attnT[:, h, qg * G:(qg + 1) * G], accS[0:64, qg * G:(qg + 1) * G], pbc[0:64, :],
                               ALU.mult, [accS.b, pbc.b], [attnT.b])

                    cur = qk(steps[0])
                    for n, (h, j, qg) in enumerate(steps):
                        nxt = qk(steps[n + 1]) if n + 1 < len(steps) else None
                        q0 = max(qg * G, j * 128)
                        q1 = (qg + 1) * G
                        N = q1 - q0
                        pT = pTs[n % 3]
                        act(pT[:, 0:N], cur[:, 0:N], AF.Exp, [cur.b], [pT.b], scale=scale)
                        if qg == j // 4:
                            sc.op('pool', lambda: nc.gpsimd.memset(pT[64:128, 0:64], 0.0), w=[pT.b])
                        c0 = q0 - qg * G
                        lastj = min(15, 4 * qg + 3)
                        mm(acc[qg][0:65, c0:c0 + N], Vb[:, j, h, :], pT[:, 0:N], j == 0, j == lastj,
                           [Vb.b, pT.b], [acc[qg].b])
                        cur = nxt
                        if n + 1 == len(steps) or steps[n + 1][0] != h:
                            epilogue(h)
                    if s == 0:
                        dump("attnT0", attnT[:], attnT)
                    sc.barrier()
                if dbg_stop == 'B':
                    break
                with ExitStack() as st:
                    wga = T("wga", [128, 8, 1024], BF16, st)
                    woa = T("woa", [64, 8, 1024], BF16, st)
                    for cc in range(2):
                        load_w_bf16(wga, wga[:, :, cc * 512:(cc + 1) * 512], wga_d[:, :, cc * 512:(cc + 1) * 512], (128, 8, 512))
                        load_w_bf16(woa, woa[:, :, cc * 512:(cc + 1) * 512], woa_d[:, :, cc * 512:(cc + 1) * 512], (64, 8, 512))
                    hTb = T("hTbC", [128, 8, G], BF16, st)
                    gaT = T("gaT", [128, G], F32, st)
                    yaT = T("yaT", [128, 8, G], BF16, st)
                    for g in range(NG):
                        tok = slice(g * G, (g + 1) * G)
                        ld(hTb[:], hT_d[s, :, :, tok], hTb, r=[hT_b[s][g]])
                        for oc in range(8):
                            p1 = PS()
                            for k in range(8):
                                mm(p1[:, :], wga[:, k, oc * 128:(oc + 1) * 128], hTb[:, k, :], k == 0, k == 7,
                                   [wga.b, hTb.b], [p1.b])
                            act(gaT[:], p1[:, :], AF.Sigmoid, [p1.b], [gaT.b])
                            p2 = PS()
                            for h in range(8):
                                mm(p2[:, :], woa[:, h, oc * 128:(oc + 1) * 128], attnT[:, h, tok], h == 0, h == 7,
                                   [woa.b, attnT.b], [p2.b])
                            tt('dve', yaT[:, oc, :], p2[:, :], gaT[:], ALU.mult, [p2.b, gaT.b], [yaT.b])
                        st_(ya_d[s, :, :, tok], yaT[:], yaT, [ya_b[s][g]])
                    sc.barrier()
            if dbg_stop in ('A', 'B', 'C'):
                break
            with ExitStack() as st:
                wlx = T("wlx", [128, 8, 1024], BF16, st)
                wlg = T("wlg", [128, 8, 1024], BF16, st)
                wgb = T("wgb", [128, 8, 1024], BF16, st)
                wol = T("wol", [128, 8, 1024], BF16, st)
                wout = T("wout", [128, 8, 1024], BF16, st)
                wrg = T("wrg", [128, 8, 128], BF16, st)
                wig = T("wig", [128, 8, 128], BF16, st)
                for cc in range(2):
                    cs_ = slice(cc * 512, (cc + 1) * 512)
                    load_w_bf16(wlx, wlx[:, :, cs_], wlx_d[:, :, cs_], (128, 8, 512))
                    load_w_bf16(wlg, wlg[:, :, cs_], wlg_d[:, :, cs_], (128, 8, 512))
                    load_w_bf16(wgb, wgb[:, :, cs_], wgb_d[:, :, cs_], (128, 8, 512))
                    load_w_bf16(wol, wol[:, :, cs_], wol_d[:, :, cs_], (128, 8, 512))
                    load_w_bf16(wout, wout[:, :, cs_], wout_d[:, :, cs_], (128, 8, 512))
                load_w_bf16(wrg, wrg[:], wrg_d[:, :, :], (128, 8, 128))
                load_w_bf16(wig, wig[:], wig_d[:, :, :], (128, 8, 128))
                hTb = T("hTbD", [128, 8, G], BF16, st)
                xb = T("xbD", [128, 8, G], F32, st)
                yaT = T("yaTD", [128, 8, G], BF16, st)
                xe = T("xe", [128, 8, G + 4], F32, st)
                hprev = T("hprev", [128, 8], F32, st)
                xcs = [T("xc%d" % i, [128, G], F32, st) for i in range(2)]
                xcbs = [T("xcb%d" % i, [128, G], BF16, st) for i in range(2)]
                rr = T("rr", [128, G], F32, st)
                ii = T("ii", [128, G], F32, st)
                aa = T("aa", [128, G], F32, st)
                a2 = T("a2", [128, G], F32, st)
                bb = T("bb", [128, G], F32, st)
                hs = T("hs", [128, G], F32, st)
                gls = [T("gl%d" % i, [128, G], F32, st) for i in range(2)]
                recT = T("recT", [128, 8, G], BF16, st)
                gb = T("gb", [128, G], F32, st)
                t1 = T("t1", [128, G], F32, st)
                sc.op('pool', lambda: nc.gpsimd.memset(xe[:], 0.0), w=[xe.b])
                sc.op('pool', lambda: nc.gpsimd.memset(hprev[:], 0.0), w=[hprev.b])
                for g in range(NG):
                    tok = slice(g * G, (g + 1) * G)
                    ld(hTb[:], hT_d[s, :, :, tok], hTb, r=[hT_b[s][g]])
                    ld(xb[:], xT_d[s, :, :, tok], xb)
                    ld(yaT[:], ya_d[s, :, :, tok], yaT, r=[ya_b[s][g]])
                    def lru_p1(n):
                        xc_, xcb_, gl_ = xcs[n % 2], xcbs[n % 2], gls[n % 2]
                        px = PS()
                        for k in range(8):
                            mm(px[:, :], wlx[:, k, n * 128:(n + 1) * 128], hTb[:, k, :], k == 0, k == 7,
                               [wlx.b, hTb.b], [px.b])
                        cp('act', xe[:, n, 4:G + 4], px[:, :], [px.b], [xe.b])
                        ts('dve', xc_[:], xe[:, n, 4:G + 4], pvec[:, C_CW + 3 * 8 + n:C_CW + 3 * 8 + n + 1],
                           pvec[:, C_CB + n:C_CB + n + 1], ALU.mult, ALU.add, [xe.b, pvec.b], [xc_.b])
                        for kk in range(3):
                            stt(xc_[:], xe[:, n, 1 + kk:1 + kk + G], pvec[:, C_CW + kk * 8 + n:C_CW + kk * 8 + n + 1], xc_[:],
                                ALU.mult, ALU.add, [xe.b, pvec.b, xc_.b], [xc_.b])
                        cp('pool', xe[:, n, 1:4], xe[:, n, G + 1:G + 4], [xe.b], [xe.b])
                        cp('act', xcb_[:], xc_[:], [xc_.b], [xcb_.b])
                        pg = PS()
                        for k in range(8):
                            mm(pg[:, :], wlg[:, k, n * 128:(n + 1) * 128], hTb[:, k, :], k == 0, k == 7,
                               [wlg.b, hTb.b], [pg.b])
                        act(gl_[:], pg[:, :], AF.Gelu_apprx_tanh, [pg.b], [gl_.b])

                    def lru_p2(n):
                        xc_, xcb_, gl_ = xcs[n % 2], xcbs[n % 2], gls[n % 2]
                        pr_ = PS()
                        mm(pr_[:, :], wrg[:, n, :], xcb_[:], True, True, [wrg.b, xcb_.b], [pr_.b])
                        pi_ = PS()
                        mm(pi_[:, :], wig[:, n, :], xcb_[:], True, True, [wig.b, xcb_.b], [pi_.b])
                        act(rr[:], pr_[:, :], AF.Sigmoid, [pr_.b, pvec.b], [rr.b], bias=pvec[:, C_BRG + n:C_BRG + n + 1])
                        act(ii[:], pi_[:, :], AF.Sigmoid, [pi_.b, pvec.b], [ii.b], bias=pvec[:, C_BIG + n:C_BIG + n + 1])
                        act(aa[:], rr[:], AF.Exp, [rr.b, kap.b], [aa.b], scale=kap[:, n:n + 1])
                        act(a2[:], rr[:], AF.Exp, [rr.b, kap2.b], [a2.b], scale=kap2[:, n:n + 1])
                        act(a2[:], a2[:], AF.Sqrt, [a2.b], [a2.b], scale=-1.0, bias=oneb[:, 0:1])
                        tt('dve', bb[:], a2[:], ii[:], ALU.mult, [a2.b, ii.b], [bb.b])
                        tt('dve', bb[:], bb[:], xc_[:], ALU.mult, [bb.b, xc_.b], [bb.b])
                        sc.op('dve', lambda: nc.vector.tensor_tensor_scan(out=hs[:], data0=aa[:], data1=bb[:],
                                                                          initial=hprev[:, n:n + 1], op0=ALU.mult, op1=ALU.add),
                              r=[aa.b, bb.b, hprev.b], w=[hs.b])
                        cp('pool', hprev[:, n:n + 1], hs[:, G - 1:G], [hs.b], [hprev.b])
                        tt('dve', recT[:, n, :], hs[:], gl_[:], ALU.mult, [hs.b, gl_.b], [recT.b])

                    lru_p1(0)
                    for n in range(8):
                        if n + 1 < 8:
                            lru_p1(n + 1)
                        lru_p2(n)
                    if s == 0 and g == 0:
                        dump("recT0", recT[:], recT)
                    for oc in range(8):
                        p1 = PS()
                        for k in range(8):
                            mm(p1[:, :], wgb[:, k, oc * 128:(oc + 1) * 128], hTb[:, k, :], k == 0, k == 7,
                               [wgb.b, hTb.b], [p1.b])
                        act(gb[:], p1[:, :], AF.Sigmoid, [p1.b], [gb.b])
                        p2 = PS()
                        for k in range(8):
                            mm(p2[:, :], wol[:, k, oc * 128:(oc + 1) * 128], recT[:, k, :], k == 0, k == 7,
                               [wol.b, recT.b], [p2.b])
                        tt('dve', t1[:], p2[:, :], gb[:], ALU.mult, [p2.b, gb.b], [t1.b])
                        tt('dve', yaT[:, oc, :], t1[:], yaT[:, oc, :], ALU.add, [t1.b, yaT.b], [yaT.b])
                    for oc in range(8):
                        p3 = PS()
                        for k in range(8):
                            mm(p3[:, :], wout[:, k, oc * 128:(oc + 1) * 128], yaT[:, k, :], k == 0, k == 7,
                               [wout.b, yaT.b], [p3.b])
                        stt(xb[:, oc, :], p3[:, :], modT[:, GT1 + oc, s:s + 1], xb[:, oc, :], ALU.mult, ALU.add,
                            [p3.b, modT.b, xb.b], [xb.b])
                    st_(out_d[s, :, :, tok], xb[:], xb, [out_b[s][g]])
                sc.barrier()
            if dbg_stop == 'D':
                continue
            with ExitStack() as st:
                wq = T("wq", [128, 8, 2048], BF16, st)
                skT = T("skT", [128, 16, 128], BF16, st)
                load_w_bf16(wq, wq[:], wq_d[:, :, :], (128, 8, 2048))
                load_w_bf16(skT, skT[:], skT_d[:, :, :], (128, 16, 128))
                x1g = T("x1g", [128, 8, G], F32, st)
                h2b = T("h2b", [128, 8, G], BF16, st)
                qTb = T("qTb", [128, 16, G], BF16, st)
                gi = 0
                for g in range(NG):
                    tok = slice(g * G, (g + 1) * G)
                    with ExitStack() as s3:
                        sqE = T("sqE", [128, 8, G], F32, s3)
                        rstdE = T("rstdE", [128, G], F32, s3)
                        ld(x1g[:], out_d[s, :, :, tok], x1g, r=[out_b[s][g]])
                        act(sqE[:], x1g[:], AF.Square, [x1g.b], [sqE.b])
                        pb = PS(2, 8)
                        for k in range(8):
                            mm(pb[:, :], onesD[:, :], sqE[:, k, :], k == 0, k == 7, [onesD.b, sqE.b], [pb.b])
                        act(rstdE[:], pb[:, :], AF.Ln, [pb.b], [rstdE.b], bias=epsb[:, 0:1])
                        act(rstdE[:], rstdE[:], AF.Exp, [rstdE.b], [rstdE.b], scale=-0.5)
                        for k in range(8):
                            stt(sqE[:, k, :], x1g[:, k, :], A2[:, k, s:s + 1], rstdE[:], ALU.mult, ALU.mult,
                                [x1g.b, A2.b, rstdE.b], [sqE.b])
                            act(h2b[:, k, :], sqE[:, k, :], AF.Identity, [sqE.b, modT.b], [h2b.b], bias=modT[:, SH2 + k, s:s + 1])
                        if s == 0 and g == 0:
                            dump("h2T0", h2b[:], h2b)
                        for m in range(16):
                            pq = PS(2, 8)
                            for k in range(8):
                                mm(pq[:, :], wq[:, k, m * 128:(m + 1) * 128], h2b[:, k, :], k == 0, k == 7,
                                   [wq.b, h2b.b], [pq.b])
                            cp('act', qTb[:, m, :], pq[:, :], [pq.b], [qTb.b])
                        sc.barrier()
                    with ExitStack() as s3:
                        scs = T("scs", [128, 16, 128], F32, s3)
                        top = T("top", [128, 16, 16], F32, s3)
                        tix = T("tix", [128, 16, 16], U32, s3)
                        tif = T("tif", [128, 16, 16], F32, s3)
                        cand = T("cand", [128, 8, 256], F32, s3)
                        eq = T("eq", [128, 128, 16], BF16, s3)
                        ai = T("ai", [128, 128], U32, s3)
                        bi = T("bi", [128, 128], U32, s3)
                        af = T("af", [128, 128], F32, s3)
                        bf = T("bf", [128, 128], F32, s3)
                        Isel = T("Isel", [128, 128], F32, s3)
                        Jsel = T("Jsel", [128, 128], F32, s3)
                        best = T("best", [128, 8, 16], F32, s3)
                        bpos = T("bpos", [128, 8, 16], U32, s3)
                        gsum = T("gsum", [128, 8], F32, s3)
                        idxf = T("idxf", [128, 128], F32, s3)
                        gw = T("gw", [128, 8, 16], F32, s3)
                        idxTs = [T("idxT%d" % i, [128, 128], U32, s3) for i in range(4)]
                        gTs = [T("gT%d" % i, [128, 128], F32, s3) for i in range(4)]
                        NGB = 10
                        DG = 6
                        assert NGB - DG >= 4
                        Gs = [T("Gg%d" % i, [128, 2048], BF16, s3) for i in range(NGB)]
                        wsel = [T("wsel%d" % i, [128, 128], BF16, s3) for i in range(3)]
                        actw = [T("actw%d" % i, [128, 1], F32, s3) for i in range(4)]
                        pacc = T("pacc", [128, 1024], F32, s3)
                        junk2s = [T("junk2_%d" % i, [128, 1024], BF16, s3) for i in range(3)]
                        actr = [T("actr%d" % i, [128, 1], F32, s3) for i in range(4)]
                        actg = [T("actg%d" % i, [128, 1], F32, s3) for i in range(4)]
                        scs2v = stg[0][:, :].rearrange("p (a b) -> p a b", a=16)
                        cand2v = stg[1][:, :].rearrange("p (a b) -> p a b", a=8)
                        top_b = [Buf("top_b%d" % i) for i in range(16)]
                        tix_b = [Buf("tix_b%d" % i) for i in range(16)]
                        scs2_b = [Buf("scs2_b%d" % i) for i in range(16)] + [stg[0].b]
                        cand_b = [Buf("cand_b%d" % i) for i in range(8)]
                        cand2_b = [Buf("cand2_b%d" % i) for i in range(8)] + [stg[1].b]
                        best_b = [Buf("best_b%d" % i) for i in range(8)]
                        bpos_b = [Buf("bpos_b%d" % i) for i in range(8)]
                        pS = PSB[6]
                        pT7 = PSB[7]

                        def topk_gen(t4):
                            tsl = slice(t4 * 128, (t4 + 1) * 128)
                            idxT = idxTs[t4]
                            gT = gTs[t4]
                            for q4 in range(4):
                                for mi in range(4):
                                    m = q4 * 4 + mi
                                    mm(pS[:, mi * 128:(mi + 1) * 128], qTb[:, m, tsl], skT[:, m, :], True, True,
                                       [qTb.b, skT.b], [pS.b])
                                yield
                                cp('act', scs[:, q4 * 4:(q4 + 1) * 4, :], pS[:, :].rearrange("p (a b) -> p a b", a=4),
                                   [pS.b], [scs.b])
                                yield
                            for m0 in range(0, 16, 2):
                                ms = (m0, m0 + 1)
                                for m in ms:
                                    sc.op('dve', lambda: nc.vector.max(out=top[:, m, 0:8], in_=scs[:, m, :]), r=[scs.b], w=[top_b[m]])
                                    yield
                                for m in ms:
                                    sc.op('dve', lambda: nc.vector.max_index(out=tix[:, m, 0:8], in_max=top[:, m, 0:8], in_values=scs[:, m, :]),
                                          r=[scs.b, top_b[m]], w=[tix_b[m]])
                                    yield
                                for m in ms:
                                    sc.op('dve', lambda: nc.vector.match_replace(out=scs2v[:, m, :], in_to_replace=top[:, m, 0:8],
                                                                                 in_values=scs[:, m, :], imm_value=-1e30),
                                          r=[scs.b, top_b[m]], w=[scs2_b[m]])
                                    yield
                                for m in ms:
                                    sc.op('dve', lambda: nc.vector.max(out=top[:, m, 8:16], in_=scs2v[:, m, :]), r=[scs2_b[m]], w=[top_b[m]])
                                    yield
                                for m in ms:
                                    sc.op('dve', lambda: nc.vector.max_index(out=tix[:, m, 8:16], in_max=top[:, m, 8:16], in_values=scs2v[:, m, :]),
                                          r=[scs2_b[m], top_b[m]], w=[tix_b[m]])
                                    yield
                            cp('dve', tif[:], tix[:], tix_b, [tif.b])
                            yield
                            for h0 in range(0, 8, 2):
                                hs_ = (h0, h0 + 1)
                                for h in hs_:
                                    cv = cand[:, h, :].rearrange("p (a b) -> p a b", a=16)
                                    tt('dve', cv, top[:, 2 * h, :].unsqueeze(2).to_broadcast([128, 16, 16]),
                                       top[:, 2 * h + 1, :].unsqueeze(1).to_broadcast([128, 16, 16]), ALU.add,
                                       [top_b[2 * h], top_b[2 * h + 1]], [cand_b[h]])
                                    yield
                                for h in hs_:
                                    sc.op('dve', lambda: nc.vector.max(out=best[:, h, 0:8], in_=cand[:, h, :]), r=[cand_b[h]], w=[best_b[h]])
                                    yield
                                for h in hs_:
                                    sc.op('dve', lambda: nc.vector.max_index(out=bpos[:, h, 0:8], in_max=best[:, h, 0:8], in_values=cand[:, h, :]),
                                          r=[cand_b[h], best_b[h]], w=[bpos_b[h]])
                                    yield
                                for h in hs_:
                                    sc.op('dve', lambda: nc.vector.match_replace(out=cand2v[:, h, :], in_to_replace=best[:, h, 0:8],
                                                                                 in_values=cand[:, h, :], imm_value=-1e30),
                                          r=[cand_b[h], best_b[h]], w=[cand2_b[h]])
                                    yield
                                for h in hs_:
                                    sc.op('dve', lambda: nc.vector.max(out=best[:, h, 8:16], in_=cand2v[:, h, :]), r=[cand2_b[h]], w=[best_b[h]])
                                    yield
                                for h in hs_:
                                    sc.op('dve', lambda: nc.vector.max_index(out=bpos[:, h, 8:16], in_max=best[:, h, 8:16], in_values=cand2v[:, h, :]),
                                          r=[cand2_b[h], best_b[h]], w=[bpos_b[h]])
                                    yield
                            bflat = bpos[:].rearrange("p a b -> p (a b)")
                            ts('dve', ai[:], bflat, 4, None, ALU.logical_shift_right, None, bpos_b, [ai.b])
                            yield
                            ts('dve', bi[:], bflat, 15, None, ALU.bitwise_and, None, bpos_b, [bi.b])
                            yield
                            cp('dve', af[:], ai[:], [ai.b], [af.b])
                            yield
                            cp('dve', bf[:], bi[:], [bi.b], [bf.b])
                            yield
                            for (xf, par, sel) in ((af, 0, Isel), (bf, 1, Jsel)):
                                tt('dve', eq[:], iota256[:, 0:16].unsqueeze(1).to_broadcast([128, 128, 16]),
                                   xf[:].unsqueeze(2).to_broadcast([128, 128, 16]), ALU.is_equal, [iota256.b, xf.b], [eq.b])
                                yield
                                e4 = eq[:].rearrange("p (h k) a -> p h k a", h=8)
                                tt('dve', e4, e4, tif[:, par::2, :].unsqueeze(2).to_broadcast([128, 8, 16, 16]), ALU.mult,
                                   [eq.b, tif.b], [eq.b])
                                yield
                                sc.op('dve', lambda: nc.vector.tensor_reduce(out=sel[:], in_=eq[:], axis=mybir.AxisListType.X, op=ALU.add),
                                      r=[eq.b], w=[sel.b])
                                yield
                            stt(idxf[:], Isel[:], 128.0, Jsel[:], ALU.mult, ALU.add, [Isel.b, Jsel.b], [idxf.b])
                            yield
                            tt('dve', gw[:], best[:], best[:, :, 0:1].to_broadcast([128, 8, 16]), ALU.subtract, best_b, [gw.b])
                            yield
                            act(gw[:], gw[:], AF.Exp, [gw.b], [gw.b])
                            yield
                            sc.op('dve', lambda: nc.vector.tensor_reduce(out=gsum[:], in_=gw[:], axis=mybir.AxisListType.X, op=ALU.add),
                                  r=[gw.b], w=[gsum.b])
                            yield
                            sc.op('dve', lambda: nc.vector.reciprocal(out=gsum[:], in_=gsum[:]), r=[gsum.b], w=[gsum.b])
                            yield
                            tt('dve', gw[:], gw[:], gsum[:].unsqueeze(2).to_broadcast([128, 8, 16]), ALU.mult, [gw.b, gsum.b], [gw.b])
                            yield
                            if s == 0 and g == 0 and t4 == 0:
                                dump("idxf0", idxf[:], idxf)
                                dump("gw0", gw[:], gw)
                            sc.op('pe', lambda: nc.tensor.transpose(pS[:, 0:128], idxf[:], ident_f[:]), r=[idxf.b, ident_f.b], w=[pS.b])
                            yield
                            cp('dve', idxT[:], pS[:, 0:128], [pS.b], [idxT.b])
                            yield
                            sc.op('pe', lambda: nc.tensor.transpose(pS[:, 128:256], gw[:].rearrange("p a b -> p (a b)"), ident_f[:]),
                                  r=[gw.b, ident_f.b], w=[pS.b])
                            yield
                            cp('act', gT[:], pS[:, 128:256], [pS.b], [gT.b])
                            yield

                        def pull(gen, k):
                            if gen is None:
                                return
                            for _ in range(k):
                                try:
                                    next(gen)
                                except StopIteration:
                                    return

                        gi = 0
                        pull(topk_gen(0), 100000)
                        for t4 in range(4):
                            tsl = slice(t4 * 128, (t4 + 1) * 128)
                            idxT = idxTs[t4]
                            gT = gTs[t4]
                            gen = topk_gen(t4 + 1) if t4 < 3 else None

                            def gather(t):
                                Gt = Gs[(gi + t) % NGB]
                                sc.dma('pool', lambda: nc.gpsimd.indirect_dma_start(
                                    out=Gt[:], out_offset=None, in_=uvb_d[:, :],
                                    in_offset=bass.IndirectOffsetOnAxis(ap=idxT[:, t:t + 1], axis=0),
                                    bounds_check=None), Gt.b, r=[idxT.b, uvb_b], w=[Gt.b])

                            for t in range(DG):
                                gather(t)
                            for i in range(128 + 3):
                                if i + DG < 128:
                                    gather(i + DG)
                                if i < 128:
                                    t = i
                                    pj = 1 + (t % 2)
                                    tg = t4 * 128 + t
                                    for k in range(8):
                                        pk = PSB[2 * pj + k // 4]
                                        mm(pk[:, (k % 4) * 128:(k % 4 + 1) * 128], h2b[:, k, tg:tg + 1].to_broadcast([128, 128]),
                                           ident_b[:, :], True, True, [h2b.b, ident_b.b], [pk.b])
                                if 0 <= i - 1 < 128:
                                    t = i - 1
                                    pj = 1 + (t % 2)
                                    Gt = Gs[(gi + t) % NGB]
                                    junk2 = junk2s[t % 3]
                                    stt(junk2[:, :], Gt[:, 0:1024], 1.0, P2[pj][:, :], ALU.mult, ALU.mult,
                                        [Gt.b, PSB[2 * pj].b, PSB[2 * pj + 1].b], [junk2.b, actr[t % 4].b],
                                        accum_out=actr[t % 4][:, 0:1])
                                if 0 <= i - 2 < 128:
                                    t = i - 2
                                    ws = wsel[t % 3]
                                    act(actg[t % 4][:], actr[t % 4][:], AF.Gelu_apprx_tanh, [actr[t % 4].b], [actg[t % 4].b])
                                    act(actw[t % 4][:], actg[t % 4][:], AF.Identity, [actg[t % 4].b, gT.b], [actw[t % 4].b],
                                        scale=gT[:, t:t + 1])
                                    act(ws[:], crow[:, 127 - t:255 - t], AF.Identity, [crow.b, actw[t % 4].b], [ws.b],
                                        scale=actw[t % 4][:, 0:1])
                                if 0 <= i - 3 < 128:
                                    t = i - 3
                                    ws = wsel[t % 3]
                                    Gt = Gs[(gi + t) % NGB]
                                    for hf in range(2):
                                        pa_ = PSB[hf]
                                        mm(pa_[:, :], ws[:], Gt[:, 1024 + hf * 512:1024 + (hf + 1) * 512],
                                           t == 0, t == 127, [ws.b, Gt.b], [pa_.b])
                                pull(gen, 3)
                            pull(gen, 100000)
                            gi += 128
                            for hf in range(2):
                                cp('act', pacc[:, hf * 512:(hf + 1) * 512], PSB[hf][:, :], [PSB[hf].b], [pacc.b])
                            if s == 0 and g == 0 and t4 == 0:
                                dump("pacc0", pacc[:], pacc)
                            for k in range(8):
                                sc.op('pe', lambda: nc.tensor.transpose(pT7[:, 0:128], pacc[:, k * 128:(k + 1) * 128], ident_f[:]),
                                      r=[pacc.b, ident_f.b], w=[pT7.b])
                                stt(x1g[:, k, tsl], pT7[:, 0:128], modT[:, GT2 + k, s:s + 1], x1g[:, k, tsl], ALU.mult, ALU.add,
                                    [pT7.b, modT.b, x1g.b], [x1g.b])
                        sc.barrier()
                    st_(out_d[s, :, :, tok], x1g[:], x1g, [out_b[s][g]])
                sc.barrier()
        except _Stop:
            pass
        sc.barrier()
    except AssertionError:
        if dbg_stop is None:
            raise
    return nc


def _chunkT(v, n):
    return np.ascontiguousarray(np.asarray(v, np.float32).reshape(n, 128).T)


def _kmaj(w):
    K, N = w.shape
    return np.ascontiguousarray(w.reshape(K // 128, 128, N).transpose(1, 0, 2))


def prep_shared(inp):
    f = lambda k: np.asarray(inp[k][0], np.float32)
    sh = {}
    sh["w_ada"] = _kmaj(f("w_ada"))
    sh["b_adaT"] = _chunkT(f("b_ada"), 48)
    pv = np.zeros((128, 96), np.float32)
    pv[:, 0:8] = _chunkT(f("norm1_g"), 8)
    pv[:, 8:16] = _chunkT(f("norm2_g"), 8)
    cw = f("conv_w")
    for k in range(4):
        pv[:, 16 + k * 8:16 + (k + 1) * 8] = _chunkT(cw[k], 8)
    pv[:, 48:56] = _chunkT(f("conv_b"), 8)
    pv[:, 56:64] = _chunkT(f("b_rg"), 8)
    pv[:, 64:72] = _chunkT(f("b_ig"), 8)
    pv[:, 72:80] = _chunkT(f("lru_lambda"), 8)
    pv[:, 80:82] = _chunkT(f("q_a_norm_g"), 2)
    pv[:, 82:83] = _chunkT(f("kv_a_norm_g"), 1)
    perm = np.concatenate([np.arange(64), 80 + np.arange(16), 64 + np.arange(16)])
    gq = f("q_norm_g")
    gk = f("k_norm_g")
    pv[:96, 83] = gq
    pv[:96, 84] = gq[perm]
    pv[:96, 85] = gk
    pv[:96, 86] = gk[perm]
    inv_freq = (1.0 / (10000.0 ** (np.arange(0, 32, 2, dtype=np.float32) / np.float32(32)))).astype(np.float32)
    pv[64:80, 87] = inv_freq
    pv[80:96, 87] = inv_freq
    pv[64:80, 88] = -1.0
    pv[80:96, 88] = 1.0
    sh["pvec"] = pv
    w_in = f("w_in")
    wl = np.zeros((1024, 512), np.float32)
    wl[:, 0:416] = w_in[:, 0:416]
    wl[:, 416:448] = w_in[:, 384:416][:, perm[64:] - 64]
    sh["w_lat"] = _kmaj(wl)
    w_uq = f("w_uq")
    wr = np.zeros((256, 1536), np.float32)
    wr[:, :768] = w_uq
    for h in range(8):
        wr[:, 768 + h * 96:768 + (h + 1) * 96] = w_uq[:, h * 96:(h + 1) * 96][:, perm]
    sh["w_uq2"] = _kmaj(wr)
    w_ukv = f("w_ukv")
    wkn = np.zeros((128, 768), np.float32)
    wv = np.zeros((128, 512), np.float32)
    for h in range(8):
        wkn[:, h * 96:h * 96 + 64] = w_ukv[:, h * 128:h * 128 + 64]
        wv[:, h * 64:(h + 1) * 64] = w_ukv[:, h * 128 + 64:h * 128 + 128]
    sh["w_kn"] = wkn
    sh["w_v"] = wv
    em = np.zeros((128, 96), np.float32)
    em[np.arange(32), 64 + np.arange(32)] = 1.0
    sh["emat"] = em
    sh["w_lx"] = _kmaj(w_in[:, 416:1440])
    sh["w_lg"] = _kmaj(w_in[:, 1440:2464])
    sh["w_ga"] = _kmaj(w_in[:, 2464:3488])
    sh["w_gb"] = _kmaj(w_in[:, 3488:4512])
    sh["w_rg"] = np.ascontiguousarray(f("w_rg").transpose(1, 0, 2))
    sh["w_ig"] = np.ascontiguousarray(f("w_ig").transpose(1, 0, 2))
    sh["w_oa"] = np.ascontiguousarray(f("w_o_attn").reshape(8, 64, 1024).transpose(1, 0, 2))
    sh["w_ol"] = _kmaj(f("w_o_lru"))
    sh["w_out"] = _kmaj(f("w_out"))
    sh["w_q"] = _kmaj(f("w_query"))
    sk = f("sub_keys").reshape(16, 128, 128)
    sh["skT"] = np.ascontiguousarray(sk.transpose(2, 0, 1))
    sh["uv"] = np.ascontiguousarray(np.concatenate([f("expert_u"), f("expert_v")], axis=1))
    return sh


def prep_core(inp, c):
    x = np.asarray(inp["x"][c * NSEQ:(c + 1) * NSEQ], np.float32)
    xT = np.ascontiguousarray(x.reshape(NSEQ, S, 8, 128).transpose(0, 3, 2, 1))
    cc = np.asarray(inp["c"][c * NSEQ:(c + 1) * NSEQ], np.float32)
    cT = np.ascontiguousarray(cc.reshape(NSEQ, 8, 128).transpose(2, 1, 0))
    pos = np.ascontiguousarray(np.asarray(inp["positions"][c * NSEQ:(c + 1) * NSEQ], np.int32))
    return {"xT": xT, "cT": cT, "pos": pos}


_NC_CACHE = {}


def kernel(**inputs):
    if "nc" not in _NC_CACHE:
        _NC_CACHE["nc"] = build_program()
    nc = _NC_CACHE["nc"]
    sh = prep_shared(inputs)
    in_maps = []
    for c in range(NCORES):
        m = dict(sh)
        m.update(prep_core(inputs, c))
        in_maps.append(m)
    res = run_bass_kernel_spmd(nc, in_maps, core_ids=list(range(NCORES)))
    outs = []
    for c in range(NCORES):
        o = np.asarray(res.results[c]["outT"])
        outs.append(o.transpose(0, 3, 2, 1).reshape(NSEQ, S, D))
    return np.ascontiguousarray(np.concatenate(outs, axis=0).astype(np.float32))
```

```python
import math
from contextlib import ExitStack

import numpy as np
import concourse.bass as bass
import concourse.mybir as mybir
from concourse.bass_utils import run_bass_kernel_spmd

F32 = mybir.dt.float32
F32R = mybir.dt.float32r
BF16 = mybir.dt.bfloat16
I32 = mybir.dt.int32
U32 = mybir.dt.uint32
AF = mybir.ActivationFunctionType
ALU = mybir.AluOpType

NCORES = 8
NSEQ = 2
S = 2048
D = 1024
G = 512
NG = S // G
EPS = 1e-6
NEXP = 16384
PI = math.pi

DEBUG = {}


class Buf:
    def __init__(self, name):
        self.name = name
        self.w = {}
        self.r = {}
        self.dsem = None
        self.dcnt = 0
        self.psum = False


class Tile:
    def __init__(self, t, name):
        self.t = t
        self.b = Buf(name)

    def __getitem__(self, k):
        return self.t[k]


class Sched:
    def __init__(self, nc, es):
        self.nc = nc
        self.es = es
        self.eng = dict(pe=nc.tensor, act=nc.scalar, dve=nc.vector, pool=nc.gpsimd, sp=nc.sync)
        self.sem = {k: es.enter_context(nc.semaphore("S_" + k)) for k in self.eng}
        self.cnt = {k: 0 for k in self.eng}
        self.seen = {k: {} for k in self.eng}
        self.dbufs = []
        self.free = []
        self.free_sw = []
        self.nd = 0

    def _wait(self, e, tok):
        key, sem, val, src = tok
        if self.seen[e].get(key, 0) >= val:
            return
        self.eng[e].wait_ge(sem, val)
        self.seen[e][key] = val

    def _deps(self, e, r, w, is_dma, dkey=None):
        for b in r:
            for tok in b.w.values():
                if (not is_dma) and tok[3] == e and e == 'pe':
                    continue
                self._wait(e, tok)
        for b in w:
            for tok in b.w.values():
                if (not is_dma) and tok[3] == e and e == 'pe':
                    continue
                if is_dma and tok[0] == dkey:
                    continue
                self._wait(e, tok)
            for tok in b.r.values():
                if (not is_dma) and tok[3] == e and e == 'pe':
                    continue
                self._wait(e, tok)

    def _post(self, tok, r, w):
        for b in w:
            if tok[3] == 'dma':
                b.w = {tok[0]: tok}
            else:
                b.w = {tok[0]: tok}
            b.r = {}
        for b in r:
            if b not in w:
                b.r[tok[0]] = tok

    def op(self, e, fn, r=(), w=()):
        pr_ = [b for b in r if b.psum and b not in w]
        if pr_:
            w = list(w) + pr_
            r = [b for b in r if not b.psum]
        self._deps(e, r, w, False)
        inst = fn()
        self.cnt[e] += 1
        inst.then_inc(self.sem[e], 1)
        tok = ("E_" + e, self.sem[e], self.cnt[e], e)
        self.seen[e][tok[0]] = max(self.seen[e].get(tok[0], 0), 0)
        self._post(tok, r, w)
        return inst

    def dma(self, e, fn, sb, r=(), w=()):
        if sb.dsem is None:
            fl = self.free_sw if e == 'pool' else self.free
            if fl:
                sb.dsem = fl.pop()
            else:
                sem = self.es.enter_context(self.nc.semaphore("D%d" % self.nd))
                sb.dsem = ["D%d" % self.nd, sem, 0, e == 'pool']
                self.nd += 1
                self.dbufs.append(sb.dsem)
        ds = sb.dsem
        self._deps(e, r, w, True, ds[0])
        inst = fn()
        ds[2] += 16
        inst.then_inc(ds[1], 16)
        tok = (ds[0], ds[1], ds[2], 'dma')
        self._post(tok, r, w)
        return inst

    def release(self, buf):
        if buf.dsem is not None:
            (self.free_sw if buf.dsem[3] else self.free).append(buf.dsem)
            buf.dsem = None

    def barrier(self):
        for e in self.eng:
            for e2 in self.eng:
                if e2 != e and self.cnt[e2] > 0:
                    self._wait(e, ("E_" + e2, self.sem[e2], self.cnt[e2], e2))
            for ds in self.dbufs:
                if ds[2] > 0:
                    self._wait(e, (ds[0], ds[1], ds[2], 'dma'))


class _Stop(Exception):
    pass


def build_program(dbg_stop=None):
    nc = bass.Bass("TRN2", target_bir_lowering=False)

    def din(name, shape, dt=F32):
        return nc.dram_tensor(name, list(shape), dt, kind="ExternalInput").ap()

    xT_d = din("xT", [NSEQ, 128, 8, S])
    cT_d = din("cT", [128, 8, NSEQ])
    pos_d = din("pos", [NSEQ, S], I32)
    wada_d = din("w_ada", [128, 8, 6144])
    bada_d = din("b_adaT", [128, 48])
    pvec_d = din("pvec", [128, 96])
    wlat_d = din("w_lat", [128, 8, 512])
    wuq_d = din("w_uq2", [128, 2, 1536])
    wkn_d = din("w_kn", [128, 8 * 96])
    wv_d = din("w_v", [128, 512])
    emat_d = din("emat", [128, 96])
    wlx_d = din("w_lx", [128, 8, 1024])
    wlg_d = din("w_lg", [128, 8, 1024])
    wga_d = din("w_ga", [128, 8, 1024])
    wgb_d = din("w_gb", [128, 8, 1024])
    wrg_d = din("w_rg", [128, 8, 128])
    wig_d = din("w_ig", [128, 8, 128])
    woa_d = din("w_oa", [64, 8, 1024])
    wol_d = din("w_ol", [128, 8, 1024])
    wout_d = din("w_out", [128, 8, 1024])
    wq_d = din("w_q", [128, 8, 2048])
    skT_d = din("skT", [128, 16, 128])
    uv_d = din("uv", [NEXP, 2048])
    out_d = nc.dram_tensor("outT", [NSEQ, 128, 8, S], F32, kind="ExternalOutput").ap()
    hT_d = nc.dram_tensor("hT_s", [NSEQ, 128, 8, S], BF16, kind="Internal").ap()
    ya_d = nc.dram_tensor("ya_s", [NSEQ, 128, 8, S], BF16, kind="Internal").ap()
    uvb_d = nc.dram_tensor("uvb_s", [NEXP, 2048], BF16, kind="Internal").ap()
    dbg_d = {k: nc.dram_tensor(k, list(v[0]), v[1], kind="ExternalOutput").ap() for k, v in DEBUG.items()}

    es = ExitStack()
    try:
      with es:
        sc = Sched(nc, es)

        uid = {'i': 0}

        def T(name, shape, dt, st=es):
            uid['i'] += 1
            nm = "t%d_%s" % (uid['i'], name)
            tl = Tile(st.enter_context(nc.sbuf_tensor(nm, list(shape), dt)), nm)
            if st is not es:
                st.callback(sc.release, tl.b)
            return tl

        class BankView:
            def __init__(self, t, off, name):
                self.t = t
                self.off = off
                self.b = Buf(name)

            def __getitem__(self, key):
                ps_, cs = key
                a = (cs.start or 0) + self.off
                b_ = (cs.stop if cs.stop is not None else 512) + self.off
                return self.t[ps_, slice(a, b_, cs.step)]

        P2 = [es.enter_context(nc.psum_tensor("pp%d" % i, [128, 1024], F32)) for i in range(4)]
        PSB = [BankView(P2[i // 2], (i % 2) * 512, "ps%d" % i) for i in range(8)]
        for p_ in PSB:
            p_.b.psum = True
        rot = {'i': 0}

        def PS(lo=0, hi=8):
            i = rot['i']
            n = hi - lo
            b = PSB[lo + (i % n)]
            rot['i'] = i + 1
            return b

        hT_b = [[Buf("hTd%d_%d" % (s, g)) for g in range(NG)] for s in range(NSEQ)]
        ya_b = [[Buf("yad%d_%d" % (s, g)) for g in range(NG)] for s in range(NSEQ)]
        out_b = [[Buf("outd%d_%d" % (s, g)) for g in range(NG)] for s in range(NSEQ)]
        dbg_b = {k: Buf("dbg_" + k) for k in DEBUG}

        def mm(out, lhsT, rhs, start, stop, r, w):
            return sc.op('pe', lambda: nc.tensor.matmul(out, lhsT, rhs, start=start, stop=stop), r=r, w=w)

        def act(out, in_, func, r, w, bias=None, scale=None):
            kw = {}
            if bias is not None:
                kw['bias'] = bias
            if scale is not None:
                kw['scale'] = scale
            return sc.op('act', lambda: nc.scalar.activation(out=out, in_=in_, func=func, **kw), r=r, w=w)

        def tt(e, out, in0, in1, op, r, w):
            eng = nc.vector if e == 'dve' else nc.gpsimd
            return sc.op(e, lambda: eng.tensor_tensor(out=out, in0=in0, in1=in1, op=op), r=r, w=w)

        def ts(e, out, in0, s1, s2, op0, op1, r, w):
            eng = nc.vector if e == 'dve' else nc.gpsimd
            if op1 is None:
                return sc.op(e, lambda: eng.tensor_scalar(out=out, in0=in0, scalar1=s1, scalar2=None, op0=op0), r=r, w=w)
            return sc.op(e, lambda: eng.tensor_scalar(out=out, in0=in0, scalar1=s1, scalar2=s2, op0=op0, op1=op1), r=r, w=w)

        def stt(out, in0, scalar, in1, op0, op1, r, w, accum_out=None):
            if accum_out is not None:
                return sc.op('dve', lambda: nc.vector.scalar_tensor_tensor(out=out, in0=in0, scalar=scalar, in1=in1, op0=op0, op1=op1, accum_out=accum_out), r=r, w=w)
            return sc.op('dve', lambda: nc.vector.scalar_tensor_tensor(out=out, in0=in0, scalar=scalar, in1=in1, op0=op0, op1=op1), r=r, w=w)

        def cp(e, out, in_, r, w):
            if e == 'act':
                return sc.op('act', lambda: nc.scalar.copy(out=out, in_=in_), r=r, w=w)
            eng = nc.vector if e == 'dve' else nc.gpsimd
            return sc.op(e, lambda: eng.tensor_copy(out=out, in_=in_), r=r, w=w)

        def ld(out, in_, tile, r=(), eng='sp'):
            e = {'sp': nc.sync, 'pool': nc.gpsimd, 'act': nc.scalar}[eng]
            return sc.dma(eng, lambda: e.dma_start(out=out, in_=in_), tile.b, r=r, w=[tile.b])

        def st_(out, in_, tile, wb, eng='sp'):
            e = {'sp': nc.sync, 'pool': nc.gpsimd, 'act': nc.scalar}[eng]
            return sc.dma(eng, lambda: e.dma_start(out=out, in_=in_), tile.b, r=[tile.b], w=wb)

        def dump(name, in_ap, tile, out_ap=None):
            if name in DEBUG:
                o = dbg_d[name] if out_ap is None else out_ap
                st_(o, in_ap, tile, [dbg_b[name]])

        pvec = T("pvec", [128, 96], F32)
        ld(pvec[:], pvec_d[:, :], pvec)
        C_N1, C_N2, C_CW, C_CB, C_BRG, C_BIG, C_LAM, C_QA, C_KVA = 0, 8, 16, 48, 56, 64, 72, 80, 82
        C_GQ, C_GQR, C_GK, C_GKR, C_INVF, C_SGN = 83, 84, 85, 86, 87, 88
        ones_f = T("ones_f", [128, 128], F32)
        sc.op('pool', lambda: nc.gpsimd.memset(ones_f[:], 1.0), w=[ones_f.b])
        onesD = T("onesD", [128, 128], F32)
        sc.op('pool', lambda: nc.gpsimd.memset(onesD[:], 1.0 / D), w=[onesD.b])
        ones256 = T("ones256", [128, 128], F32)
        sc.op('pool', lambda: nc.gpsimd.memset(ones256[:], 1.0 / 256), w=[ones256.b])
        ones128 = T("ones128", [128, 128], F32)
        sc.op('pool', lambda: nc.gpsimd.memset(ones128[:], 1.0 / 128), w=[ones128.b])
        ones96 = T("ones96", [128, 128], F32)
        sc.op('pool', lambda: nc.gpsimd.memset(ones96[:], 1.0 / 96), w=[ones96.b])
        epsb = T("epsb", [128, 1], F32)
        sc.op('pool', lambda: nc.gpsimd.memset(epsb[:], EPS), w=[epsb.b])
        oneb = T("oneb", [128, 1], F32)
        sc.op('pool', lambda: nc.gpsimd.memset(oneb[:], 1.0), w=[oneb.b])
        iot = T("iot", [128, 128], F32)
        sc.op('pool', lambda: nc.gpsimd.iota(iot[:], pattern=[[1, 128]], base=0, channel_multiplier=-1,
                                             allow_small_or_imprecise_dtypes=True), w=[iot.b])
        ident_f = T("ident_f", [128, 128], F32)
        ts('dve', ident_f[:], iot[:], 0.0, None, ALU.is_equal, None, [iot.b], [ident_f.b])
        ident_b = T("ident_b", [128, 128], BF16)
        cp('dve', ident_b[:], ident_f[:], [ident_f.b], [ident_b.b])
        crow = T("crow", [128, 255], F32)
        sc.op('pool', lambda: nc.gpsimd.iota(crow[:], pattern=[[1, 255]], base=-127, channel_multiplier=0,
                                             allow_small_or_imprecise_dtypes=True), w=[crow.b])
        ts('dve', crow[:], crow[:], 0.0, None, ALU.is_equal, None, [crow.b], [crow.b])
        iota256 = T("iota256", [128, 256], F32)
        sc.op('pool', lambda: nc.gpsimd.iota(iota256[:], pattern=[[1, 256]], base=0, channel_multiplier=0,
                                             allow_small_or_imprecise_dtypes=True), w=[iota256.b])

        sc.barrier()
        stg = [T("stg%d" % i, [128, 2048], F32) for i in range(2)]
        stg_i = {'i': 0}

        def load_w_bf16(dst_tile, dst_ap, src_ap, shape):
            p, a, b = shape
            cb = max(1, 2048 // a)
            for c0 in range(0, b, cb):
                c1 = min(b, c0 + cb)
                sg = stg[stg_i['i'] % 2]
                stg_i['i'] += 1
                sv = sg[0:p, 0:a * (c1 - c0)].rearrange("p (a b) -> p a b", a=a)
                ld(sv, src_ap[:, :, c0:c1], sg)
                cp('dve', dst_ap[:, :, c0:c1], sv, [sg.b], [dst_tile.b])

        modT = T("modT", [128, 48, NSEQ], F32)
        A1 = T("A1", [128, 8, NSEQ], F32)
        A2 = T("A2", [128, 8, NSEQ], F32)
        kap = T("kap", [128, 8], F32)
        kap2 = T("kap2", [128, 8], F32)
        gqs = T("gqs", [128, 1], F32)
        gks = T("gks", [128, 1], F32)
        with ExitStack() as st:
            cTt = T("cTt", [128, 8, NSEQ], F32, st)
            csT = T("csT", [128, 8, NSEQ], F32, st)
            badaT = T("badaT", [128, 48], F32, st)
            ld(cTt[:], cT_d[:, :, :], cTt)
            ld(badaT[:], bada_d[:, :], badaT)
            act(csT[:], cTt[:], AF.Silu, [cTt.b], [csT.b])
            pm = PSB[0]
            for cc in range(24):
                sg0 = stg[cc % 2]
                sg = sg0[:, :].rearrange("p (a b) -> p a b", a=8)
                ld(sg, wada_d[:, :, cc * 256:(cc + 1) * 256], sg0)
                for j in range(2):
                    oc = cc * 2 + j
                    for k in range(8):
                        mm(pm[:, oc * 2:oc * 2 + 2], sg[:, k, j * 128:(j + 1) * 128], csT[:, k, :],
                           k == 0, k == 7, [sg0.b, csT.b], [pm.b])
            for b in range(NSEQ):
                tt('dve', modT[:, :, b], pm[:, b:96:2], badaT[:, :], ALU.add, [pm.b, badaT.b], [modT.b])
            for b in range(NSEQ):
                stt(A1[:, :, b], modT[:, 8:16, b], 1.0, pvec[:, C_N1:C_N1 + 8], ALU.add, ALU.mult,
                    [modT.b, pvec.b], [A1.b])
                stt(A2[:, :, b], modT[:, 32:40, b], 1.0, pvec[:, C_N2:C_N2 + 8], ALU.add, ALU.mult,
                    [modT.b, pvec.b], [A2.b])
            tk = T("tk", [128, 8], F32, st)
            act(tk[:], pvec[:, C_LAM:C_LAM + 8], AF.Exp, [pvec.b], [tk.b], scale=-1.0)
            act(tk[:], tk[:], AF.Ln, [tk.b], [tk.b], bias=oneb[:, 0:1])
            ts('dve', kap[:], tk[:], -8.0, None, ALU.mult, None, [tk.b], [kap.b])
            ts('dve', kap2[:], tk[:], -16.0, None, ALU.mult, None, [tk.b], [kap2.b])
            tt('dve', gqs[:], pvec[:, C_GQR:C_GQR + 1], pvec[:, C_SGN:C_SGN + 1], ALU.mult, [pvec.b], [gqs.b])
            tt('dve', gks[:], pvec[:, C_GKR:C_GKR + 1], pvec[:, C_SGN:C_SGN + 1], ALU.mult, [pvec.b], [gks.b])
            dump("modT", modT[:], modT)
            sc.barrier()

        uvb_b = Buf("uvb")
        with ExitStack() as st:
            cin = [T("cin%d" % i, [128, 4, 2048], F32, st) for i in range(2)]
            co_d = [T("cod%d" % i, [128, 3, 2048], BF16, st) for i in range(2)]
            co_a = [T("coa%d" % i, [128, 1, 2048], BF16, st) for i in range(2)]
            uvv = uv_d.rearrange("(c p j) d -> c p j d", p=128, j=4)
            uvbv = uvb_d.rearrange("(c p j) d -> c p j d", p=128, j=4)
            for c in range(NEXP // 512):
                ci = cin[c % 2]
                ld(ci[:], uvv[c], ci)
                cp('dve', co_d[c % 2][:], ci[:, 0:3, :], [ci.b], [co_d[c % 2].b])
                cp('act', co_a[c % 2][:], ci[:, 3:4, :], [ci.b], [co_a[c % 2].b])
                st_(uvbv[c][:, 0:3, :], co_d[c % 2][:], co_d[c % 2], [uvb_b])
                st_(uvbv[c][:, 3:4, :], co_a[c % 2][:], co_a[c % 2], [uvb_b])
            sc.barrier()
        if dbg_stop == 'S0':
            sc.barrier()
            return nc
        SH1, GT1, SH2, GT2 = 0, 16, 24, 40

        def rms_modulate(xb, hTb, A, shoff, s, st, tag):
            sq = T("sq" + tag, [128, 8, G], F32, st)
            rstd = T("rstd" + tag, [128, G], F32, st)
            act(sq[:], xb[:], AF.Square, [xb.b], [sq.b])
            pb = PS()
            for k in range(8):
                mm(pb[:, :], onesD[:, :], sq[:, k, :], k == 0, k == 7, [onesD.b, sq.b], [pb.b])
            act(rstd[:], pb[:, :], AF.Ln, [pb.b], [rstd.b], bias=epsb[:, 0:1])
            act(rstd[:], rstd[:], AF.Exp, [rstd.b], [rstd.b], scale=-0.5)
            for k in range(8):
                stt(sq[:, k, :], xb[:, k, :], A[:, k, s:s + 1], rstd[:], ALU.mult, ALU.mult,
                    [xb.b, A.b, rstd.b], [sq.b])
                ts('pool', hTb[:, k, :], sq[:, k, :], modT[:, shoff + k, s:s + 1], None, ALU.add, None,
                   [sq.b, modT.b], [hTb.b])

        def chk(name):
            if dbg_stop == name:
                raise _Stop()

        try:
         for s in range(NSEQ):
            with ExitStack() as sa:
                qT = T("qT", [96, 8, S], BF16, sa)
                kT = T("kT", [96, 8, S], BF16, sa)
                Vb = T("Vb", [128, 16, 8, 65], BF16, sa)
                with ExitStack() as st:
                    cosT = T("cosT", [96, G], F32, st)
                    sinT = T("sinT", [96, G], F32, st)
                    posi = T("posi", [96, G], I32, st)
                    ang = T("ang", [96, G], F32, st)
                    nf = T("nf", [96, G], F32, st)
                    ni = T("ni", [96, G], I32, st)
                    wlat = T("wlat", [128, 8, 512], BF16, st)
                    wuq = T("wuq", [128, 2, 1536], BF16, st)
                    wkn = T("wkn", [128, 768], BF16, st)
                    wv = T("wv", [128, 512], BF16, st)
                    emat = T("emat", [128, 96], BF16, st)
                    load_w_bf16(wlat, wlat[:], wlat_d[:, :, :], (128, 8, 512))
                    load_w_bf16(wuq, wuq[:], wuq_d[:, :, :], (128, 2, 1536))
                    load_w_bf16(wkn, wkn[:].rearrange("p (a b) -> p a b", a=1), wkn_d[:, :].rearrange("p (a b) -> p a b", a=1), (128, 1, 768))
                    load_w_bf16(wv, wv[:].rearrange("p (a b) -> p a b", a=1), wv_d[:, :].rearrange("p (a b) -> p a b", a=1), (128, 1, 512))
                    load_w_bf16(emat, emat[:].rearrange("p (a b) -> p a b", a=1), emat_d[:, :].rearrange("p (a b) -> p a b", a=1), (128, 1, 96))
                    sc.op('pool', lambda: nc.gpsimd.memset(Vb[:], 1.0), w=[Vb.b])

                    def rope_tables(tok):
                        ld(posi[:], pos_d[s, tok].partition_broadcast(96), posi)
                        cp('dve', ang[:], posi[:], [posi.b], [ang.b])
                        ts('dve', ang[:], ang[:], pvec[0:96, C_INVF:C_INVF + 1], None, ALU.mult, None, [ang.b, pvec.b], [ang.b])
                        ts('dve', nf[:], ang[:], 1.0 / (2 * PI), None, ALU.mult, None, [ang.b], [nf.b])
                        cp('dve', ni[:], nf[:], [nf.b], [ni.b])
                        cp('dve', nf[:], ni[:], [ni.b], [nf.b])
                        C1 = 6.28125
                        C2 = 2 * PI - C1
                        stt(ang[:], nf[:], -C1, ang[:], ALU.mult, ALU.add, [nf.b, ang.b], [ang.b])
                        stt(ang[:], nf[:], -C2, ang[:], ALU.mult, ALU.add, [nf.b, ang.b], [ang.b])

                        def wrap_sin(dst, shift):
                            pf = posi[:].bitcast(F32)
                            ts('dve', nf[:], ang[:], shift, None, ALU.add, None, [ang.b], [nf.b])
                            for _ in range(2):
                                ts('dve', pf, nf[:], PI, -2 * PI, ALU.is_gt, ALU.mult, [nf.b], [posi.b])
                                tt('dve', nf[:], nf[:], pf, ALU.add, [nf.b, posi.b], [nf.b])
                                ts('dve', pf, nf[:], -PI, 2 * PI, ALU.is_lt, ALU.mult, [nf.b], [posi.b])
                                tt('dve', nf[:], nf[:], pf, ALU.add, [nf.b, posi.b], [nf.b])
                            ts('dve', nf[:], nf[:], 3.141592, -3.141592, ALU.min, ALU.max, [nf.b], [nf.b])
                            act(dst[:], nf[:], AF.Sin, [nf.b], [dst.b])
                        wrap_sin(sinT, 0.0)
                        wrap_sin(cosT, PI / 2)

                    xb = T("xb", [128, 8, G], F32, st)
                    hTb = T("hTb", [128, 8, G], BF16, st)
                    sqq = T("sqq", [128, 2, G], F32, st)
                    rq = T("rq", [128, G], F32, st)
                    cqn = T("cqn", [128, 2, G], BF16, st)
                    ckvn = T("ckvn", [128, G], BF16, st)
                    krb = T("krb", [128, G], BF16, st)
                    krrb = T("krrb", [128, G], BF16, st)
                    sc.op('pool', lambda: nc.gpsimd.memset(krb[:], 0.0), w=[krb.b])
                    sc.op('pool', lambda: nc.gpsimd.memset(krrb[:], 0.0), w=[krrb.b])
                    m2k = T("m2k", [96, G], F32, st)
                    sqh = T("sqh", [96, G], F32, st)
                    rh = T("rh", [96, G], F32, st)
                    m1 = T("m1", [96, G], F32, st)
                    m2 = T("m2", [96, G], F32, st)
                    for g in range(NG):
                        tok = slice(g * G, (g + 1) * G)
                        rope_tables(tok)
                        if g == 0 and s == 0:
                            dump("cosT", cosT[:], cosT)
                            dump("sinT", sinT[:], sinT)
                        chk('A1')
                        ld(xb[:], xT_d[s, :, :, tok], xb)
                        if g == 0:
                            sqA = T("sqA", [128, 8, G], F32, st)
                            rstdA = T("rstdA", [128, G], F32, st)
                        act(sqA[:], xb[:], AF.Square, [xb.b], [sqA.b])
                        pb = PS()
                        for k in range(8):
                            mm(pb[:, :], onesD[:, :], sqA[:, k, :], k == 0, k == 7, [onesD.b, sqA.b], [pb.b])
                        act(rstdA[:], pb[:, :], AF.Ln, [pb.b], [rstdA.b], bias=epsb[:, 0:1])
                        act(rstdA[:], rstdA[:], AF.Exp, [rstdA.b], [rstdA.b], scale=-0.5)
                        for k in range(8):
                            stt(sqA[:, k, :], xb[:, k, :], A1[:, k, s:s + 1], rstdA[:], ALU.mult, ALU.mult,
                                [xb.b, A1.b, rstdA.b], [sqA.b])
                            act(hTb[:, k, :], sqA[:, k, :], AF.Identity, [sqA.b, modT.b], [hTb.b], bias=modT[:, SH1 + k, s:s + 1])
                        st_(hT_d[s, :, :, tok], hTb[:], hTb, [hT_b[s][g]])
                        if g == 0 and s == 0:
                            dump("hT0", hTb[:], hTb)
                        chk('A2')
                        pcq = [PS(), PS()]
                        for j in range(2):
                            for k in range(8):
                                mm(pcq[j][:, :], wlat[:, k, j * 128:(j + 1) * 128], hTb[:, k, :], k == 0, k == 7,
                                   [wlat.b, hTb.b], [pcq[j].b])
                        pkv = PS()
                        for k in range(8):
                            mm(pkv[:, :], wlat[:, k, 256:384], hTb[:, k, :], k == 0, k == 7, [wlat.b, hTb.b], [pkv.b])
                        pkr = PS()
                        for k in range(8):
                            mm(pkr[0:32, :], wlat[:, k, 384:416], hTb[:, k, :], k == 0, k == 7, [wlat.b, hTb.b], [pkr.b])
                        pkrr = PS()
                        for k in range(8):
                            mm(pkrr[0:32, :], wlat[:, k, 416:448], hTb[:, k, :], k == 0, k == 7, [wlat.b, hTb.b], [pkrr.b])
                        for j in range(2):
                            act(sqq[:, j, :], pcq[j][:, :], AF.Square, [pcq[j].b], [sqq.b])
                        pn = PS()
                        for j in range(2):
                            mm(pn[:, :], ones256[:, :], sqq[:, j, :], j == 0, j == 1, [ones256.b, sqq.b], [pn.b])
                        act(rq[:], pn[:, :], AF.Ln, [pn.b], [rq.b], bias=epsb[:, 0:1])
                        act(rq[:], rq[:], AF.Exp, [rq.b], [rq.b], scale=-0.5)
                        for j in range(2):
                            stt(cqn[:, j, :], pcq[j][:, :], pvec[:, C_QA + j:C_QA + j + 1], rq[:], ALU.mult, ALU.mult,
                                [pcq[j].b, pvec.b, rq.b], [cqn.b])
                        act(sqq[:, 0, :], pkv[:, :], AF.Square, [pkv.b], [sqq.b])
                        pn = PS()
                        mm(pn[:, :], ones128[:, :], sqq[:, 0, :], True, True, [ones128.b, sqq.b], [pn.b])
                        act(rq[:], pn[:, :], AF.Ln, [pn.b], [rq.b], bias=epsb[:, 0:1])
                        act(rq[:], rq[:], AF.Exp, [rq.b], [rq.b], scale=-0.5)
                        stt(ckvn[:], pkv[:, :], pvec[:, C_KVA:C_KVA + 1], rq[:], ALU.mult, ALU.mult,
                            [pkv.b, pvec.b, rq.b], [ckvn.b])
                        cp('act', krb[0:32, :], pkr[0:32, :], [pkr.b], [krb.b])
                        cp('act', krrb[0:32, :], pkrr[0:32, :], [pkrr.b], [krrb.b])
                        chk('A3')
                        pr = PS()
                        mm(pr[0:96, :], emat[:, :], krrb[:, :], True, True, [emat.b, krrb.b], [pr.b])
                        stt(m2k[:], pr[0:96, :], gks[0:96, 0:1], sinT[:, :], ALU.mult, ALU.mult,
                            [pr.b, gks.b, sinT.b], [m2k.b])
                        chk('A3a')
                        for hh in range(16):
                            isq = hh < 8
                            h = hh % 8
                            pa = PS()
                            if isq:
                                for j in range(2):
                                    mm(pa[0:96, :], wuq[:, j, h * 96:(h + 1) * 96], cqn[:, j, :], j == 0, j == 1,
                                       [wuq.b, cqn.b], [pa.b])
                                pb2 = PS()
                                for j in range(2):
                                    mm(pb2[0:96, :], wuq[:, j, 768 + h * 96:768 + (h + 1) * 96], cqn[:, j, :], j == 0, j == 1,
                                       [wuq.b, cqn.b], [pb2.b])
                            else:
                                mm(pa[0:96, :], wkn[:, h * 96:(h + 1) * 96], ckvn[:, :], True, False, [wkn.b, ckvn.b], [pa.b])
                                mm(pa[0:96, :], emat[:, :], krb[:, :], False, True, [emat.b, krb.b], [pa.b])
                            chk('A3b')
                            act(sqh[:], pa[0:96, :], AF.Square, [pa.b], [sqh.b])
                            pc = PS()
                            mm(pc[0:96, :], ones96[0:96, 0:96], sqh[:, :], True, True, [ones96.b, sqh.b], [pc.b])
                            act(rh[:], pc[0:96, :], AF.Ln, [pc.b], [rh.b], bias=epsb[0:96, 0:1])
                            act(rh[:], rh[:], AF.Exp, [rh.b], [rh.b], scale=-0.5)
                            chk('A3c')
                            gcol = C_GQ if isq else C_GK
                            stt(m1[:], pa[0:96, :], pvec[0:96, gcol:gcol + 1], cosT[:, :], ALU.mult, ALU.mult,
                                [pa.b, pvec.b, cosT.b], [m1.b])
                            chk('A3d')
                            if isq:
                                stt(m2[:], pb2[0:96, :], gqs[0:96, 0:1], sinT[:, :], ALU.mult, ALU.mult,
                                    [pb2.b, gqs.b, sinT.b], [m2.b])
                                tt('dve', m1[:], m1[:], m2[:], ALU.add, [m1.b, m2.b], [m1.b])
                                tt('dve', qT[:, h, tok], m1[:], rh[:], ALU.mult, [m1.b, rh.b], [qT.b])
                            else:
                                tt('dve', m1[:], m1[:], m2k[:], ALU.add, [m1.b, m2k.b], [m1.b])
                                tt('dve', kT[:, h, tok], m1[:], rh[:], ALU.mult, [m1.b, rh.b], [kT.b])
                        chk('A4')
                        for t4 in range(4):
                            pv = PS()
                            mm(pv[:, :], ckvn[:, t4 * 128:(t4 + 1) * 128], wv[:, :], True, True, [ckvn.b, wv.b], [pv.b])
                            cp('act', Vb[:, g * 4 + t4, :, 0:64], pv[:, :].rearrange("p (h d) -> p h d", h=8),
                               [pv.b], [Vb.b])
                    if s == 0:
                        dump("qT0", qT[:], qT)
                        dump("kT0", kT[:], kT)
                        dump("Vb0", Vb[:], Vb)
                    sc.barrier()
                if dbg_stop == 'A':
                    break
                attnT = T("attnT", [64, 8, S], BF16, sa)
                with ExitStack() as st:
                    pTs = [T("pT%d" % i, [128, G], BF16, st) for i in range(3)]
                    accS = T("accS", [65, S], F32, st)
                    rrow = T("rrow", [65, S], F32, st)
                    scale = 96 ** -0.5
                    acc = PSB[0:4]
                    steps = []
                    for h in range(8):
                        for j in range(16):
                            for qg in range(j // 4, 4):
                                steps.append((h, j, qg))

                    def qk(step):
                        h, j, qg = step
                        q0 = max(qg * G, j * 128)
                        q1 = (qg + 1) * G
                        ps_ = PS(4, 8)
                        mm(ps_[:, 0:q1 - q0], kT[:, h, j * 128:(j + 1) * 128], qT[:, h, q0:q1], True, True,
                           [kT.b, qT.b], [ps_.b])
                        return ps_

                    def epilogue(h):
                        for qg in range(4):
                            cp('act', accS[:, qg * G:(qg + 1) * G], acc[qg][0:65, :], [acc[qg].b], [accS.b])
                        act(rrow[64:65, :], accS[64:65, :], AF.Ln, [accS.b], [rrow.b])
                        act(rrow[64:65, :], rrow[64:65, :], AF.Exp, [rrow.b], [rrow.b], scale=-1.0)
                        for qg in range(4):
                            pbc = acc[qg]
                            mm(pbc[0:64, :], ones_f[64:65, 0:64], rrow[64:65, qg * G:(qg + 1) * G], True, True,
                               [ones_f.b, rrow.b], [pbc.b])
                            tt('dve', attnT[:, h, qg * G:(qg + 1) * G], accS[0:64, qg * G:(qg + 1) * G], pbc[0:64, :],
                               ALU.mult, [accS.b, pbc.b], [attnT.b])

                    cur = qk(steps[0])
                    for n, (h, j, qg) in enumerate(steps):
                        nxt = qk(steps[n + 1]) if n + 1 < len(steps) else None
                        q0 = max(qg * G, j * 128)
                        q1 = (qg + 1) * G
                        N = q1 - q0
                        pT = pTs[n % 3]
                        act(pT[:, 0:N], cur[:, 0:N], AF.Exp, [cur.b], [pT.b], scale=scale)
                        if qg == j // 4:
                            sc.op('pool', lambda: nc.gpsimd.memset(pT[64:128, 0:64], 0.0), w=[pT.b])
                        c0 = q0 - qg * G
                        lastj = min(15, 4 * qg + 3)
                        mm(acc[qg][0:65, c0:c0 + N], Vb[:, j, h, :], pT[:, 0:N], j == 0, j == lastj,
                           [Vb.b, pT.b], [acc[qg].b])
                        cur = nxt
                        if n + 1 == len(steps) or steps[n + 1][0] != h:
                            epilogue(h)
                    if s == 0:
                        dump("attnT0", attnT[:], attnT)
                    sc.barrier()
                if dbg_stop == 'B':
                    break
                with ExitStack() as st:
                    wga = T("wga", [128, 8, 1024], BF16, st)
                    woa = T("woa", [64, 8, 1024], BF16, st)
                    for cc in range(2):
                        load_w_bf16(wga, wga[:, :, cc * 512:(cc + 1) * 512], wga_d[:, :, cc * 512:(cc + 1) * 512], (128, 8, 512))
                        load_w_bf16(woa, woa[:, :, cc * 512:(cc + 1) * 512], woa_d[:, :, cc * 512:(cc + 1) * 512], (64, 8, 512))
                    hTb = T("hTbC", [128, 8, G], BF16, st)
                    gaT = T("gaT", [128, G], F32, st)
                    yaT = T("yaT", [128, 8, G], BF16, st)
                    for g in range(NG):
                        tok = slice(g * G, (g + 1) * G)
                        ld(hTb[:], hT_d[s, :, :, tok], hTb, r=[hT_b[s][g]])
                        for oc in range(8):
                            p1 = PS()
                            for k in range(8):
                                mm(p1[:, :], wga[:, k, oc * 128:(oc + 1) * 128], hTb[:, k, :], k == 0, k == 7,
                                   [wga.b, hTb.b], [p1.b])
                            act(gaT[:], p1[:, :], AF.Sigmoid, [p1.b], [gaT.b])
                            p2 = PS()
                            for h in range(8):
                                mm(p2[:, :], woa[:, h, oc * 128:(oc + 1) * 128], attnT[:, h, tok], h == 0, h == 7,
                                   [woa.b, attnT.b], [p2.b])
                            tt('dve', yaT[:, oc, :], p2[:, :], gaT[:], ALU.mult, [p2.b, gaT.b], [yaT.b])
                        st_(ya_d[s, :, :, tok], yaT[:], yaT, [ya_b[s][g]])
                    sc.barrier()
            if dbg_stop in ('A', 'B', 'C'):
                break
            with ExitStack() as st:
                wlx = T("wlx", [128, 8, 1024], BF16, st)
                wlg = T("wlg", [128, 8, 1024], BF16, st)
                wgb = T("wgb", [128, 8, 1024], BF16, st)
                wol = T("wol", [128, 8, 1024], BF16, st)
                wout = T("wout", [128, 8, 1024], BF16, st)
                wrg = T("wrg", [128, 8, 128], BF16, st)
                wig = T("wig", [128, 8, 128], BF16, st)
                for cc in range(2):
                    cs_ = slice(cc * 512, (cc + 1) * 512)
                    load_w_bf16(wlx, wlx[:, :, cs_], wlx_d[:, :, cs_], (128, 8, 512))
                    load_w_bf16(wlg, wlg[:, :, cs_], wlg_d[:, :, cs_], (128, 8, 512))
                    load_w_bf16(wgb, wgb[:, :, cs_], wgb_d[:, :, cs_], (128, 8, 512))
                    load_w_bf16(wol, wol[:, :, cs_], wol_d[:, :, cs_], (128, 8, 512))
                    load_w_bf16(wout, wout[:, :, cs_], wout_d[:, :, cs_], (128, 8, 512))
                load_w_bf16(wrg, wrg[:], wrg_d[:, :, :], (128, 8, 128))
                load_w_bf16(wig, wig[:], wig_d[:, :, :], (128, 8, 128))
                hTb = T("hTbD", [128, 8, G], BF16, st)
                xb = T("xbD", [128, 8, G], F32, st)
                yaT = T("yaTD", [128, 8, G], BF16, st)
                xe = T("xe", [128, 8, G + 4], F32, st)
                hprev = T("hprev", [128, 8], F32, st)
                xcs = [T("xc%d" % i, [128, G], F32, st) for i in range(2)]
                xcbs = [T("xcb%d" % i, [128, G], BF16, st) for i in range(2)]
                rr = T("rr", [128, G], F32, st)
                ii = T("ii", [128, G], F32, st)
                aa = T("aa", [128, G], F32, st)
                a2 = T("a2", [128, G], F32, st)
                bb = T("bb", [128, G], F32, st)
                hs = T("hs", [128, G], F32, st)
                gls = [T("gl%d" % i, [128, G], F32, st) for i in range(2)]
                recT = T("recT", [128, 8, G], BF16, st)
                gb = T("gb", [128, G], F32, st)
                t1 = T("t1", [128, G], F32, st)
                sc.op('pool', lambda: nc.gpsimd.memset(xe[:], 0.0), w=[xe.b])
                sc.op('pool', lambda: nc.gpsimd.memset(hprev[:], 0.0), w=[hprev.b])
                for g in range(NG):
                    tok = slice(g * G, (g + 1) * G)
                    ld(hTb[:], hT_d[s, :, :, tok], hTb, r=[hT_b[s][g]])
                    ld(xb[:], xT_d[s, :, :, tok], xb)
                    ld(yaT[:], ya_d[s, :, :, tok], yaT, r=[ya_b[s][g]])
                    def lru_p1(n):
                        xc_, xcb_, gl_ = xcs[n % 2], xcbs[n % 2], gls[n % 2]
                        px = PS()
                        for k in range(8):
                            mm(px[:, :], wlx[:, k, n * 128:(n + 1) * 128], hTb[:, k, :], k == 0, k == 7,
                               [wlx.b, hTb.b], [px.b])
                        cp('act', xe[:, n, 4:G + 4], px[:, :], [px.b], [xe.b])
                        ts('dve', xc_[:], xe[:, n, 4:G + 4], pvec[:, C_CW + 3 * 8 + n:C_CW + 3 * 8 + n + 1],
                           pvec[:, C_CB + n:C_CB + n + 1], ALU.mult, ALU.add, [xe.b, pvec.b], [xc_.b])
                        for kk in range(3):
                            stt(xc_[:], xe[:, n, 1 + kk:1 + kk + G], pvec[:, C_CW + kk * 8 + n:C_CW + kk * 8 + n + 1], xc_[:],
                                ALU.mult, ALU.add, [xe.b, pvec.b, xc_.b], [xc_.b])
                        cp('pool', xe[:, n, 1:4], xe[:, n, G + 1:G + 4], [xe.b], [xe.b])
                        cp('act', xcb_[:], xc_[:], [xc_.b], [xcb_.b])
                        pg = PS()
                        for k in range(8):
                            mm(pg[:, :], wlg[:, k, n * 128:(n + 1) * 128], hTb[:, k, :], k == 0, k == 7,
                               [wlg.b, hTb.b], [pg.b])
                        act(gl_[:], pg[:, :], AF.Gelu_apprx_tanh, [pg.b], [gl_.b])

                    def lru_p2(n):
                        xc_, xcb_, gl_ = xcs[n % 2], xcbs[n % 2], gls[n % 2]
                        pr_ = PS()
                        mm(pr_[:, :], wrg[:, n, :], xcb_[:], True, True, [wrg.b, xcb_.b], [pr_.b])
                        pi_ = PS()
                        mm(pi_[:, :], wig[:, n, :], xcb_[:], True, True, [wig.b, xcb_.b], [pi_.b])
                        act(rr[:], pr_[:, :], AF.Sigmoid, [pr_.b, pvec.b], [rr.b], bias=pvec[:, C_BRG + n:C_BRG + n + 1])
                        act(ii[:], pi_[:, :], AF.Sigmoid, [pi_.b, pvec.b], [ii.b], bias=pvec[:, C_BIG + n:C_BIG + n + 1])
                        act(aa[:], rr[:], AF.Exp, [rr.b, kap.b], [aa.b], scale=kap[:, n:n + 1])
                        act(a2[:], rr[:], AF.Exp, [rr.b, kap2.b], [a2.b], scale=kap2[:, n:n + 1])
                        act(a2[:], a2[:], AF.Sqrt, [a2.b], [a2.b], scale=-1.0, bias=oneb[:, 0:1])
                        tt('dve', bb[:], a2[:], ii[:], ALU.mult, [a2.b, ii.b], [bb.b])
                        tt('dve', bb[:], bb[:], xc_[:], ALU.mult, [bb.b, xc_.b], [bb.b])
                        sc.op('dve', lambda: nc.vector.tensor_tensor_scan(out=hs[:], data0=aa[:], data1=bb[:],
                                                                          initial=hprev[:, n:n + 1], op0=ALU.mult, op1=ALU.add),
                              r=[aa.b, bb.b, hprev.b], w=[hs.b])
                        cp('pool', hprev[:, n:n + 1], hs[:, G - 1:G], [hs.b], [hprev.b])
                        tt('dve', recT[:, n, :], hs[:], gl_[:], ALU.mult, [hs.b, gl_.b], [recT.b])

                    lru_p1(0)
                    for n in range(8):
                        if n + 1 < 8:
                            lru_p1(n + 1)
                        lru_p2(n)
                    if s == 0 and g == 0:
                        dump("recT0", recT[:], recT)
                    for oc in range(8):
                        p1 = PS()
                        for k in range(8):
                            mm(p1[:, :], wgb[:, k, oc * 128:(oc + 1) * 128], hTb[:, k, :], k == 0, k == 7,
                               [wgb.b, hTb.b], [p1.b])
                        act(gb[:], p1[:, :], AF.Sigmoid, [p1.b], [gb.b])
                        p2 = PS()
                        for k in range(8):
                            mm(p2[:, :], wol[:, k, oc * 128:(oc + 1) * 128], recT[:, k, :], k == 0, k == 7,
                               [wol.b, recT.b], [p2.b])
                        tt('dve', t1[:], p2[:, :], gb[:], ALU.mult, [p2.b, gb.b], [t1.b])
                        tt('dve', yaT[:, oc, :], t1[:], yaT[:, oc, :], ALU.add, [t1.b, yaT.b], [yaT.b])
                    for oc in range(8):
                        p3 = PS()
                        for k in range(8):
                            mm(p3[:, :], wout[:, k, oc * 128:(oc + 1) * 128], yaT[:, k, :], k == 0, k == 7,
                               [wout.b, yaT.b], [p3.b])
                        stt(xb[:, oc, :], p3[:, :], modT[:, GT1 + oc, s:s + 1], xb[:, oc, :], ALU.mult, ALU.add,
                            [p3.b, modT.b, xb.b], [xb.b])
                    st_(out_d[s, :, :, tok], xb[:], xb, [out_b[s][g]])
                sc.barrier()
            if dbg_stop == 'D':
                continue
            with ExitStack() as st:
                wq = T("wq", [128, 8, 2048], BF16, st)
                skT = T("skT", [128, 16, 128], BF16, st)
                load_w_bf16(wq, wq[:], wq_d[:, :, :], (128, 8, 2048))
                load_w_bf16(skT, skT[:], skT_d[:, :, :], (128, 16, 128))
                x1g = T("x1g", [128, 8, G], F32, st)
                h2b = T("h2b", [128, 8, G], BF16, st)
                qTb = T("qTb", [128, 16, G], BF16, st)
                gi = 0
                for g in range(NG):
                    tok = slice(g * G, (g + 1) * G)
                    with ExitStack() as s3:
                        sqE = T("sqE", [128, 8, G], F32, s3)
                        rstdE = T("rstdE", [128, G], F32, s3)
                        ld(x1g[:], out_d[s, :, :, tok], x1g, r=[out_b[s][g]])
                        act(sqE[:], x1g[:], AF.Square, [x1g.b], [sqE.b])
                        pb = PS(2, 8)
                        for k in range(8):
                            mm(pb[:, :], onesD[:, :], sqE[:, k, :], k == 0, k == 7, [onesD.b, sqE.b], [pb.b])
                        act(rstdE[:], pb[:, :], AF.Ln, [pb.b], [rstdE.b], bias=epsb[:, 0:1])
                        act(rstdE[:], rstdE[:], AF.Exp, [rstdE.b], [rstdE.b], scale=-0.5)
                        for k in range(8):
                            stt(sqE[:, k, :], x1g[:, k, :], A2[:, k, s:s + 1], rstdE[:], ALU.mult, ALU.mult,
                                [x1g.b, A2.b, rstdE.b], [sqE.b])
                            act(h2b[:, k, :], sqE[:, k, :], AF.Identity, [sqE.b, modT.b], [h2b.b], bias=modT[:, SH2 + k, s:s + 1])
                        if s == 0 and g == 0:
                            dump("h2T0", h2b[:], h2b)
                        for m in range(16):
                            pq = PS(2, 8)
                            for k in range(8):
                                mm(pq[:, :], wq[:, k, m * 128:(m + 1) * 128], h2b[:, k, :], k == 0, k == 7,
                                   [wq.b, h2b.b], [pq.b])
                            cp('act', qTb[:, m, :], pq[:, :], [pq.b], [qTb.b])
                        sc.barrier()
                    with ExitStack() as s3:
                        scs = T("scs", [128, 16, 128], F32, s3)
                        top = T("top", [128, 16, 16], F32, s3)
                        tix = T("tix", [128, 16, 16], U32, s3)
                        tif = T("tif", [128, 16, 16], F32, s3)
                        cand = T("cand", [128, 8, 256], F32, s3)
                        eq = T("eq", [128, 128, 16], BF16, s3)
                        ai = T("ai", [128, 128], U32, s3)
                        bi = T("bi", [128, 128], U32, s3)
                        af = T("af", [128, 128], F32, s3)
                        bf = T("bf", [128, 128], F32, s3)
                        Isel = T("Isel", [128, 128], F32, s3)
                        Jsel = T("Jsel", [128, 128], F32, s3)
                        best = T("best", [128, 8, 16], F32, s3)
                        bpos = T("bpos", [128, 8, 16], U32, s3)
                        gsum = T("gsum", [128, 8], F32, s3)
                        idxf = T("idxf", [128, 128], F32, s3)
                        gw = T("gw", [128, 8, 16], F32, s3)
                        idxTs = [T("idxT%d" % i, [128, 128], U32, s3) for i in range(4)]
                        gTs = [T("gT%d" % i, [128, 128], F32, s3) for i in range(4)]
                        NGB = 12
                        DG = 8
                        assert NGB - DG >= 4
                        Gs = [T("Gg%d" % i, [128, 2048], BF16, s3) for i in range(NGB)]
                        wsel = [T("wsel%d" % i, [128, 128], BF16, s3) for i in range(3)]
                        actw = [T("actw%d" % i, [128, 1], F32, s3) for i in range(4)]
                        pacc = T("pacc", [128, 1024], F32, s3)
                        junk2s = [T("junk2_%d" % i, [128, 1024], BF16, s3) for i in range(3)]
                        actr = [T("actr%d" % i, [128, 1], F32, s3) for i in range(4)]
                        actg = [T("actg%d" % i, [128, 1], F32, s3) for i in range(4)]
                        scs2v = stg[0][:, :].rearrange("p (a b) -> p a b", a=16)
                        cand2v = stg[1][:, :].rearrange("p (a b) -> p a b", a=8)
                        scs2b, cand2b = stg[0].b, stg[1].b
                        pS = PSB[6]
                        pT7 = PSB[7]

                        def topk_gen(t4):
                            tsl = slice(t4 * 128, (t4 + 1) * 128)
                            idxT = idxTs[t4]
                            gT = gTs[t4]
                            for q4 in range(4):
                                for mi in range(4):
                                    m = q4 * 4 + mi
                                    mm(pS[:, mi * 128:(mi + 1) * 128], qTb[:, m, tsl], skT[:, m, :], True, True,
                                       [qTb.b, skT.b], [pS.b])
                                yield
                                cp('act', scs[:, q4 * 4:(q4 + 1) * 4, :], pS[:, :].rearrange("p (a b) -> p a b", a=4),
                                   [pS.b], [scs.b])
                                yield
                            for m in range(16):
                                sc.op('dve', lambda: nc.vector.max(out=top[:, m, 0:8], in_=scs[:, m, :]), r=[scs.b], w=[top.b])
                                yield
                                sc.op('dve', lambda: nc.vector.max_index(out=tix[:, m, 0:8], in_max=top[:, m, 0:8], in_values=scs[:, m, :]),
                                      r=[scs.b, top.b], w=[tix.b])
                                yield
                                sc.op('dve', lambda: nc.vector.match_replace(out=scs2v[:, m, :], in_to_replace=top[:, m, 0:8],
                                                                             in_values=scs[:, m, :], imm_value=-1e30),
                                      r=[scs.b, top.b], w=[scs2b])
                                yield
                                sc.op('dve', lambda: nc.vector.max(out=top[:, m, 8:16], in_=scs2v[:, m, :]), r=[scs2b], w=[top.b])
                                yield
                                sc.op('dve', lambda: nc.vector.max_index(out=tix[:, m, 8:16], in_max=top[:, m, 8:16], in_values=scs2v[:, m, :]),
                                      r=[scs2b, top.b], w=[tix.b])
                                yield
                            cp('dve', tif[:], tix[:], [tix.b], [tif.b])
                            yield
                            for h in range(8):
                                cv = cand[:, h, :].rearrange("p (a b) -> p a b", a=16)
                                tt('dve', cv, top[:, 2 * h, :].unsqueeze(2).to_broadcast([128, 16, 16]),
                                   top[:, 2 * h + 1, :].unsqueeze(1).to_broadcast([128, 16, 16]), ALU.add, [top.b], [cand.b])
                                yield
                                sc.op('dve', lambda: nc.vector.max(out=best[:, h, 0:8], in_=cand[:, h, :]), r=[cand.b], w=[best.b])
                                yield
                                sc.op('dve', lambda: nc.vector.max_index(out=bpos[:, h, 0:8], in_max=best[:, h, 0:8], in_values=cand[:, h, :]),
                                      r=[cand.b, best.b], w=[bpos.b])
                                yield
                                sc.op('dve', lambda: nc.vector.match_replace(out=cand2v[:, h, :], in_to_replace=best[:, h, 0:8],
                                                                             in_values=cand[:, h, :], imm_value=-1e30),
                                      r=[cand.b, best.b], w=[cand2b])
                                yield
                                sc.op('dve', lambda: nc.vector.max(out=best[:, h, 8:16], in_=cand2v[:, h, :]), r=[cand2b], w=[best.b])
                                yield
                                sc.op('dve', lambda: nc.vector.max_index(out=bpos[:, h, 8:16], in_max=best[:, h, 8:16], in_values=cand2v[:, h, :]),
                                      r=[cand2b, best.b], w=[bpos.b])
                                yield
                            bflat = bpos[:].rearrange("p a b -> p (a b)")
                            ts('dve', ai[:], bflat, 4, None, ALU.logical_shift_right, None, [bpos.b], [ai.b])
                            yield
                            ts('dve', bi[:], bflat, 15, None, ALU.bitwise_and, None, [bpos.b], [bi.b])
                            yield
                            cp('dve', af[:], ai[:], [ai.b], [af.b])
                            yield
                            cp('dve', bf[:], bi[:], [bi.b], [bf.b])
                            yield
                            for (xf, par, sel) in ((af, 0, Isel), (bf, 1, Jsel)):
                                tt('dve', eq[:], iota256[:, 0:16].unsqueeze(1).to_broadcast([128, 128, 16]),
                                   xf[:].unsqueeze(2).to_broadcast([128, 128, 16]), ALU.is_equal, [iota256.b, xf.b], [eq.b])
                                yield
                                e4 = eq[:].rearrange("p (h k) a -> p h k a", h=8)
                                tt('dve', e4, e4, tif[:, par::2, :].unsqueeze(2).to_broadcast([128, 8, 16, 16]), ALU.mult,
                                   [eq.b, tif.b], [eq.b])
                                yield
                                sc.op('dve', lambda: nc.vector.tensor_reduce(out=sel[:], in_=eq[:], axis=mybir.AxisListType.X, op=ALU.add),
                                      r=[eq.b], w=[sel.b])
                                yield
                            stt(idxf[:], Isel[:], 128.0, Jsel[:], ALU.mult, ALU.add, [Isel.b, Jsel.b], [idxf.b])
                            yield
                            tt('dve', gw[:], best[:], best[:, :, 0:1].to_broadcast([128, 8, 16]), ALU.subtract, [best.b], [gw.b])
                            yield
                            act(gw[:], gw[:], AF.Exp, [gw.b], [gw.b])
                            yield
                            sc.op('dve', lambda: nc.vector.tensor_reduce(out=gsum[:], in_=gw[:], axis=mybir.AxisListType.X, op=ALU.add),
                                  r=[gw.b], w=[gsum.b])
                            yield
                            sc.op('dve', lambda: nc.vector.reciprocal(out=gsum[:], in_=gsum[:]), r=[gsum.b], w=[gsum.b])
                            yield
                            tt('dve', gw[:], gw[:], gsum[:].unsqueeze(2).to_broadcast([128, 8, 16]), ALU.mult, [gw.b, gsum.b], [gw.b])
                            yield
                            if s == 0 and g == 0 and t4 == 0:
                                dump("idxf0", idxf[:], idxf)
                                dump("gw0", gw[:], gw)
                            sc.op('pe', lambda: nc.tensor.transpose(pS[:, 0:128], idxf[:], ident_f[:]), r=[idxf.b, ident_f.b], w=[pS.b])
                            yield
                            cp('dve', idxT[:], pS[:, 0:128], [pS.b], [idxT.b])
                            yield
                            sc.op('pe', lambda: nc.tensor.transpose(pS[:, 128:256], gw[:].rearrange("p a b -> p (a b)"), ident_f[:]),
                                  r=[gw.b, ident_f.b], w=[pS.b])
                            yield
                            cp('act', gT[:], pS[:, 128:256], [pS.b], [gT.b])
                            yield

                        def pull(gen, k):
                            if gen is None:
                                return
                            for _ in range(k):
                                try:
                                    next(gen)
                                except StopIteration:
                                    return

                        gi = 0
                        pull(topk_gen(0), 100000)
                        for t4 in range(4):
                            tsl = slice(t4 * 128, (t4 + 1) * 128)
                            idxT = idxTs[t4]
                            gT = gTs[t4]
                            gen = topk_gen(t4 + 1) if t4 < 3 else None

                            def gather(t):
                                Gt = Gs[(gi + t) % NGB]
                                sc.dma('pool', lambda: nc.gpsimd.indirect_dma_start(
                                    out=Gt[:], out_offset=None, in_=uvb_d[:, :],
                                    in_offset=bass.IndirectOffsetOnAxis(ap=idxT[:, t:t + 1], axis=0),
                                    bounds_check=None), Gt.b, r=[idxT.b, uvb_b], w=[Gt.b])

                            for t in range(DG):
                                gather(t)
                            for i in range(128 + 3):
                                if i + DG < 128:
                                    gather(i + DG)
                                if i < 128:
                                    t = i
                                    pj = 1 + (t % 2)
                                    tg = t4 * 128 + t
                                    for k in range(8):
                                        pk = PSB[2 * pj + k // 4]
                                        mm(pk[:, (k % 4) * 128:(k % 4 + 1) * 128], h2b[:, k, tg:tg + 1].to_broadcast([128, 128]),
                                           ident_b[:, :], True, True, [h2b.b, ident_b.b], [pk.b])
                                if 0 <= i - 1 < 128:
                                    t = i - 1
                                    pj = 1 + (t % 2)
                                    Gt = Gs[(gi + t) % NGB]
                                    junk2 = junk2s[t % 3]
                                    stt(junk2[:, :], Gt[:, 0:1024], 1.0, P2[pj][:, :], ALU.mult, ALU.mult,
                                        [Gt.b, PSB[2 * pj].b, PSB[2 * pj + 1].b], [junk2.b, actr[t % 4].b],
                                        accum_out=actr[t % 4][:, 0:1])
                                if 0 <= i - 2 < 128:
                                    t = i - 2
                                    ws = wsel[t % 3]
                                    act(actg[t % 4][:], actr[t % 4][:], AF.Gelu_apprx_tanh, [actr[t % 4].b], [actg[t % 4].b])
                                    act(actw[t % 4][:], actg[t % 4][:], AF.Identity, [actg[t % 4].b, gT.b], [actw[t % 4].b],
                                        scale=gT[:, t:t + 1])
                                    act(ws[:], crow[:, 127 - t:255 - t], AF.Identity, [crow.b, actw[t % 4].b], [ws.b],
                                        scale=actw[t % 4][:, 0:1])
                                if 0 <= i - 3 < 128:
                                    t = i - 3
                                    ws = wsel[t % 3]
                                    Gt = Gs[(gi + t) % NGB]
                                    for hf in range(2):
                                        pa_ = PSB[hf]
                                        mm(pa_[:, :], ws[:], Gt[:, 1024 + hf * 512:1024 + (hf + 1) * 512],
                                           t == 0, t == 127, [ws.b, Gt.b], [pa_.b])
                                pull(gen, 3)
                            pull(gen, 100000)
                            gi += 128
                            for hf in range(2):
                                cp('act', pacc[:, hf * 512:(hf + 1) * 512], PSB[hf][:, :], [PSB[hf].b], [pacc.b])
                            if s == 0 and g == 0 and t4 == 0:
                                dump("pacc0", pacc[:], pacc)
                            for k in range(8):
                                sc.op('pe', lambda: nc.tensor.transpose(pT7[:, 0:128], pacc[:, k * 128:(k + 1) * 128], ident_f[:]),
                                      r=[pacc.b, ident_f.b], w=[pT7.b])
                                stt(x1g[:, k, tsl], pT7[:, 0:128], modT[:, GT2 + k, s:s + 1], x1g[:, k, tsl], ALU.mult, ALU.add,
                                    [pT7.b, modT.b, x1g.b], [x1g.b])
                        sc.barrier()
                    st_(out_d[s, :, :, tok], x1g[:], x1g, [out_b[s][g]])
                sc.barrier()
        except _Stop:
            pass
        sc.barrier()
    except AssertionError:
        if dbg_stop is None:
            raise
    return nc


def _chunkT(v, n):
    return np.ascontiguousarray(np.asarray(v, np.float32).reshape(n, 128).T)


def _kmaj(w):
    K, N = w.shape
    return np.ascontiguousarray(w.reshape(K // 128, 128, N).transpose(1, 0, 2))


def prep_shared(inp):
    f = lambda k: np.asarray(inp[k][0], np.float32)
    sh = {}
    sh["w_ada"] = _kmaj(f("w_ada"))
    sh["b_adaT"] = _chunkT(f("b_ada"), 48)
    pv = np.zeros((128, 96), np.float32)
    pv[:, 0:8] = _chunkT(f("norm1_g"), 8)
    pv[:, 8:16] = _chunkT(f("norm2_g"), 8)
    cw = f("conv_w")
    for k in range(4):
        pv[:, 16 + k * 8:16 + (k + 1) * 8] = _chunkT(cw[k], 8)
    pv[:, 48:56] = _chunkT(f("conv_b"), 8)
    pv[:, 56:64] = _chunkT(f("b_rg"), 8)
    pv[:, 64:72] = _chunkT(f("b_ig"), 8)
    pv[:, 72:80] = _chunkT(f("lru_lambda"), 8)
    pv[:, 80:82] = _chunkT(f("q_a_norm_g"), 2)
    pv[:, 82:83] = _chunkT(f("kv_a_norm_g"), 1)
    perm = np.concatenate([np.arange(64), 80 + np.arange(16), 64 + np.arange(16)])
    gq = f("q_norm_g")
    gk = f("k_norm_g")
    pv[:96, 83] = gq
    pv[:96, 84] = gq[perm]
    pv[:96, 85] = gk
    pv[:96, 86] = gk[perm]
    inv_freq = (1.0 / (10000.0 ** (np.arange(0, 32, 2, dtype=np.float32) / np.float32(32)))).astype(np.float32)
    pv[64:80, 87] = inv_freq
    pv[80:96, 87] = inv_freq
    pv[64:80, 88] = -1.0
    pv[80:96, 88] = 1.0
    sh["pvec"] = pv
    w_in = f("w_in")
    wl = np.zeros((1024, 512), np.float32)
    wl[:, 0:416] = w_in[:, 0:416]
    wl[:, 416:448] = w_in[:, 384:416][:, perm[64:] - 64]
    sh["w_lat"] = _kmaj(wl)
    w_uq = f("w_uq")
    wr = np.zeros((256, 1536), np.float32)
    wr[:, :768] = w_uq
    for h in range(8):
        wr[:, 768 + h * 96:768 + (h + 1) * 96] = w_uq[:, h * 96:(h + 1) * 96][:, perm]
    sh["w_uq2"] = _kmaj(wr)
    w_ukv = f("w_ukv")
    wkn = np.zeros((128, 768), np.float32)
    wv = np.zeros((128, 512), np.float32)
    for h in range(8):
        wkn[:, h * 96:h * 96 + 64] = w_ukv[:, h * 128:h * 128 + 64]
        wv[:, h * 64:(h + 1) * 64] = w_ukv[:, h * 128 + 64:h * 128 + 128]
    sh["w_kn"] = wkn
    sh["w_v"] = wv
    em = np.zeros((128, 96), np.float32)
    em[np.arange(32), 64 + np.arange(32)] = 1.0
    sh["emat"] = em
    sh["w_lx"] = _kmaj(w_in[:, 416:1440])
    sh["w_lg"] = _kmaj(w_in[:, 1440:2464])
    sh["w_ga"] = _kmaj(w_in[:, 2464:3488])
    sh["w_gb"] = _kmaj(w_in[:, 3488:4512])
    sh["w_rg"] = np.ascontiguousarray(f("w_rg").transpose(1, 0, 2))
    sh["w_ig"] = np.ascontiguousarray(f("w_ig").transpose(1, 0, 2))
    sh["w_oa"] = np.ascontiguousarray(f("w_o_attn").reshape(8, 64, 1024).transpose(1, 0, 2))
    sh["w_ol"] = _kmaj(f("w_o_lru"))
    sh["w_out"] = _kmaj(f("w_out"))
    sh["w_q"] = _kmaj(f("w_query"))
    sk = f("sub_keys").reshape(16, 128, 128)
    sh["skT"] = np.ascontiguousarray(sk.transpose(2, 0, 1))
    sh["uv"] = np.ascontiguousarray(np.concatenate([f("expert_u"), f("expert_v")], axis=1))
    return sh


def prep_core(inp, c):
    x = np.asarray(inp["x"][c * NSEQ:(c + 1) * NSEQ], np.float32)
    xT = np.ascontiguousarray(x.reshape(NSEQ, S, 8, 128).transpose(0, 3, 2, 1))
    cc = np.asarray(inp["c"][c * NSEQ:(c + 1) * NSEQ], np.float32)
    cT = np.ascontiguousarray(cc.reshape(NSEQ, 8, 128).transpose(2, 1, 0))
    pos = np.ascontiguousarray(np.asarray(inp["positions"][c * NSEQ:(c + 1) * NSEQ], np.int32))
    return {"xT": xT, "cT": cT, "pos": pos}


_NC_CACHE = {}


def kernel(**inputs):
    if "nc" not in _NC_CACHE:
        _NC_CACHE["nc"] = build_program()
    nc = _NC_CACHE["nc"]
    sh = prep_shared(inputs)
    in_maps = []
    for c in range(NCORES):
        m = dict(sh)
        m.update(prep_core(inputs, c))
        in_maps.append(m)
    res = run_bass_kernel_spmd(nc, in_maps, core_ids=list(range(NCORES)))
    outs = []
    for c in range(NCORES):
        o = np.asarray(res.results[c]["outT"])
        outs.append(o.transpose(0, 3, 2, 1).reshape(NSEQ, S, D))
    return np.ascontiguousarray(np.concatenate(outs, axis=0).astype(np.float32))
```
